# Optimizing a Trainium2 kernel written in Bass

```python
import math
import jax, jax.numpy as jnp
from jax import lax
import numpy as np

D_MODEL = 2048
BATCH = 2
SEQ = 4096
DEPTH = 1

CHUNK = 64
PLE_DIM = 256
D_MIX = D_MODEL
RW_WIDTH = D_MIX // 2
RW_HEAD = 64
RW_HEADS = RW_WIDTH // RW_HEAD
RW_DECAY_LORA = 64
RW_ICLR_LORA = 64
DS_WIDTH = D_MIX - RW_WIDTH
DS_HEAD = 64
DS_HEADS = DS_WIDTH // DS_HEAD
DS_Q_RANK = 384
DS_KV_RANK = 256
IDX_HEADS = 16
IDX_DIM = 64
TOPK_MAX = 256
Q_BLOCK = 128
NUM_BUCKETS = 32
MAX_DISTANCE = 128
NORM_EPS = 1e-6
GN_EPS = 64e-5
SHIFTED_COLS = 3 * RW_WIDTH + RW_DECAY_LORA + RW_ICLR_LORA
D_IN = SHIFTED_COLS + RW_WIDTH + DS_Q_RANK + DS_KV_RANK + IDX_DIM + IDX_HEADS + DS_WIDTH

kernel_name = "hybrid_rwkv7_dsa_parallel_heads"


def _split_points(sizes):
    pts, acc = [], 0
    for s in sizes[:-1]:
        acc += s
        pts.append(acc)
    return pts


def rms_norm(x, g, eps=NORM_EPS):
    xf = x.astype(jnp.float32)
    y = xf * lax.rsqrt(jnp.mean(xf * xf, axis=-1, keepdims=True) + eps)
    return (y * g.astype(jnp.float32)).astype(x.dtype)


def t5_bucket(rel):
    nb = NUM_BUCKETS // 2
    max_exact = nb // 2
    ret = jnp.where(rel > 0, nb, 0)
    n = jnp.abs(rel)
    nf = jnp.maximum(n, 1).astype(jnp.float32)
    large = max_exact + (jnp.log(nf / max_exact) / math.log(MAX_DISTANCE / max_exact)
                         * (nb - max_exact)).astype(jnp.int32)
    large = jnp.minimum(large, nb - 1)
    return ret + jnp.where(n < max_exact, n, large)


def rwkv7_mixer(z, mu, w0, w_up, a0, a_up, k_k, k_a, r_k, ln_g, ln_b):
    B, T, _ = z.shape
    H, N = RW_HEADS, RW_HEAD
    f32 = jnp.float32
    z_prev = jnp.pad(z, ((0, 0), (1, 0), (0, 0)))[:, :-1]
    z = z + mu * (z_prev - z)
    r, k, v, wd, ad = jnp.split(z, [RW_WIDTH, 2 * RW_WIDTH, 3 * RW_WIDTH,
                                    3 * RW_WIDTH + RW_DECAY_LORA], axis=-1)
    w_log = -jax.nn.softplus(-(w0 + jnp.tanh(wd) @ w_up)) - 0.5
    decay = jnp.exp(-jnp.exp(w_log.astype(f32)))
    a = jax.nn.sigmoid(a0 + ad @ a_up).astype(f32)
    heads = lambda t: t.astype(f32).reshape(B, T, H, N)
    kk = heads(k * k_k)
    kk = kk / jnp.maximum(jnp.sqrt(jnp.sum(kk * kk, axis=-1, keepdims=True)), 1e-12)
    k = k.astype(f32) * (1.0 + (a - 1.0) * k_a.astype(f32))
    rh, kh, vh, ah, wh = heads(r), heads(k), heads(v), heads(a), heads(decay)
    a_vec = -kk
    b_vec = kk * ah

    def step(S, inp):
        r_t, w_t, k_t, v_t, a_t, b_t = inp
        Sa = jnp.einsum('bhvk,bhk->bhv', S, a_t)
        S = (S * w_t[:, :, None, :] + Sa[..., None] * b_t[:, :, None, :]
             + v_t[..., None] * k_t[:, :, None, :])
        y = jnp.einsum('bhvk,bhk->bhv', S, r_t)
        return S, y

    tm = lambda t: jnp.moveaxis(t, 1, 0)
    S0 = jnp.zeros((B, H, N, N), f32)
    _, y = lax.scan(step, S0, (tm(rh), tm(wh), tm(kh), tm(vh), tm(a_vec), tm(b_vec)))
    y = jnp.moveaxis(y, 0, 1)
    mean = jnp.mean(y, axis=-1, keepdims=True)
    var = jnp.mean(jnp.square(y - mean), axis=-1, keepdims=True)
    y = (y - mean) * lax.rsqrt(var + GN_EPS)
    y = y * ln_g.astype(f32).reshape(H, N) + ln_b.astype(f32).reshape(H, N)
    bonus = jnp.sum(rh * kh * r_k.astype(f32), axis=-1, keepdims=True) * vh
    return (y + bonus).reshape(B, T, RW_WIDTH).astype(z.dtype)


def dsa_mixer(q_lat, kv_lat, k_idx, w_idx, q_norm_g, kv_norm_g, ik_norm_g,
              w_uq, w_uk, w_uv, iw_q, rel_bias):
    B, T, _ = q_lat.shape
    H, DH, R = DS_HEADS, DS_HEAD, DS_KV_RANK
    f32 = jnp.float32
    q_lat = rms_norm(q_lat, q_norm_g)
    c_kv = rms_norm(kv_lat, kv_norm_g)
    q = (q_lat @ w_uq).reshape(B, T, H, DH)
    q_abs = jnp.einsum('bthd,rhd->bthr', q, w_uk) * (DH ** -0.5)
    q_idx = (q_lat @ iw_q).reshape(B, T, IDX_HEADS, IDX_DIM)
    k_idx = rms_norm(k_idx, ik_norm_g)
    w_idx = w_idx * (IDX_HEADS ** -0.5 * IDX_DIM ** -0.5)
    top_k = min(TOPK_MAX, T // 4)
    nb = T // Q_BLOCK
    blocks = lambda t: jnp.moveaxis(t.reshape(B, nb, Q_BLOCK, *t.shape[2:]), 1, 0)
    key_pos = jnp.arange(T, dtype=jnp.int32)
    bias_tab = rel_bias.astype(f32)

    def attend_block(args):
        qa, qi, wi, q0 = args
        q_pos = q0 + jnp.arange(Q_BLOCK, dtype=jnp.int32)
        limit = (q_pos // CHUNK + 1) * CHUNK
        admissible = key_pos[None, :] < limit[:, None]
        logit_i = jnp.einsum('bqhd,bsd->bqhs', qi, k_idx)
        score = jnp.einsum('bqh,bqhs->bqs', wi, jax.nn.relu(logit_i)).astype(f32)
        score = jnp.where(admissible[None], score, -jnp.inf)
        _, sel = lax.top_k(score, top_k)
        valid = sel < limit[None, :, None]
        kv_sel = jax.vmap(lambda c, idx: c[idx])(c_kv, sel)
        bias = bias_tab[t5_bucket(sel - q_pos[None, :, None])]
        logits = (jnp.einsum('bqhr,bqkr->bqhk', qa, kv_sel).astype(f32)
                  + jnp.moveaxis(bias, -1, 2))
        logits = jnp.where(valid[:, :, None, :], logits, -jnp.inf)
        probs = jax.nn.softmax(logits, axis=-1).astype(c_kv.dtype)
        o_lat = jnp.einsum('bqhk,bqkr->bqhr', probs, kv_sel)
        return jnp.einsum('bqhr,rhd->bqhd', o_lat, w_uv)

    starts = jnp.arange(nb, dtype=jnp.int32) * Q_BLOCK
    out = lax.map(attend_block, (blocks(q_abs), blocks(q_idx), blocks(w_idx), starts))
    return jnp.moveaxis(out, 0, 1).reshape(B, T, DS_WIDTH)


def setup_inputs(seed: int = 0) -> dict:
    key = jax.random.key(seed)
    ks = jax.random.split(key, 26)
    f32 = jnp.float32
    nrm = lambda k, shape, scale: scale * jax.random.normal(k, shape, f32)
    L = DEPTH
    return {
        "x": nrm(ks[0], (BATCH, SEQ, D_MODEL), 1.0),
        "p": nrm(ks[1], (DEPTH, BATCH, SEQ, PLE_DIM), 1.0),
        "w_in": nrm(ks[2], (L, D_MODEL, D_IN), D_MODEL ** -0.5),
        "norm_g": 1.0 + nrm(ks[3], (L, D_MODEL), 0.02),
        "rw_mu": jax.random.uniform(ks[4], (L, SHIFTED_COLS), f32),
        "rw_w0": nrm(ks[5], (L, RW_WIDTH), 0.5),
        "rw_w_up": nrm(ks[6], (L, RW_DECAY_LORA, RW_WIDTH), 0.5 * RW_DECAY_LORA ** -0.5),
        "rw_a0": nrm(ks[7], (L, RW_WIDTH), 0.5),
        "rw_a_up": nrm(ks[8], (L, RW_ICLR_LORA, RW_WIDTH), 0.5 * RW_ICLR_LORA ** -0.5),
        "rw_k_k": 1.0 + nrm(ks[9], (L, RW_WIDTH), 0.1),
        "rw_k_a": 1.0 + nrm(ks[10], (L, RW_WIDTH), 0.1),
        "rw_r_k": nrm(ks[11], (L, RW_HEADS, RW_HEAD), 0.1),
        "rw_ln_g": 1.0 + nrm(ks[12], (L, RW_WIDTH), 0.02),
        "rw_ln_b": nrm(ks[13], (L, RW_WIDTH), 0.02),
        "ds_q_norm_g": 1.0 + nrm(ks[14], (L, DS_Q_RANK), 0.02),
        "ds_kv_norm_g": 1.0 + nrm(ks[15], (L, DS_KV_RANK), 0.02),
        "idx_k_norm_g": 1.0 + nrm(ks[16], (L, IDX_DIM), 0.02),
        "ds_w_uq": nrm(ks[17], (L, DS_Q_RANK, DS_HEADS * DS_HEAD), DS_Q_RANK ** -0.5),
        "ds_w_uk": nrm(ks[18], (L, DS_KV_RANK, DS_HEADS, DS_HEAD), DS_KV_RANK ** -0.5),
        "ds_w_uv": nrm(ks[19], (L, DS_KV_RANK, DS_HEADS, DS_HEAD), DS_KV_RANK ** -0.5),
        "idx_w_q": nrm(ks[20], (L, DS_Q_RANK, IDX_HEADS * IDX_DIM), DS_Q_RANK ** -0.5),
        "rel_bias": nrm(ks[21], (NUM_BUCKETS, DS_HEADS), 0.5),
        "w_out": nrm(ks[22], (L, D_MIX, D_MODEL), D_MIX ** -0.5),
        "ple_w": nrm(ks[23], (L, PLE_DIM, D_MODEL), PLE_DIM ** -0.5),
        "ple_gate_w": nrm(ks[24], (L, D_MODEL, D_MODEL), D_MODEL ** -0.5),
        "final_g": 1.0 + nrm(ks[25], (D_MODEL,), 0.02),
    }


def reference(x, p, w_in, norm_g, rw_mu, rw_w0, rw_w_up, rw_a0, rw_a_up, rw_k_k, rw_k_a,
              rw_r_k, rw_ln_g, rw_ln_b, ds_q_norm_g, ds_kv_norm_g, idx_k_norm_g, ds_w_uq,
              ds_w_uk, ds_w_uv, idx_w_q, rel_bias, w_out, ple_w, ple_gate_w, final_g):
    splits = _split_points([SHIFTED_COLS, RW_WIDTH, DS_Q_RANK, DS_KV_RANK,
                            IDX_DIM, IDX_HEADS, DS_WIDTH])
    h = x
    for i in range(DEPTH):
        xn = rms_norm(h, norm_g[i])
        z = xn @ w_in[i]
        z_rw, g_rw, q_lat, kv_lat, k_idx, w_idx, g_ds = jnp.split(z, splits, axis=-1)
        o_rw = rwkv7_mixer(z_rw, rw_mu[i], rw_w0[i], rw_w_up[i], rw_a0[i], rw_a_up[i],
                           rw_k_k[i], rw_k_a[i], rw_r_k[i], rw_ln_g[i], rw_ln_b[i])
        o_ds = dsa_mixer(q_lat, kv_lat, k_idx, w_idx, ds_q_norm_g[i], ds_kv_norm_g[i],
                         idx_k_norm_g[i], ds_w_uq[i], ds_w_uk[i], ds_w_uv[i], idx_w_q[i], rel_bias)
        mixed = jnp.concatenate([o_rw * jax.nn.silu(g_rw), o_ds * jax.nn.silu(g_ds)], axis=-1)
        h = h + mixed @ w_out[i]
        h = h + (p[i] @ ple_w[i]) * jax.nn.sigmoid(h @ ple_gate_w[i])
    return rms_norm(h, final_g)
```

```python
import math, os
SKIP = set(os.environ.get('KSKIP', '').split(','))
K3STOP = int(os.environ.get('K3STOP', '9'))
from contextlib import ExitStack
import numpy as np
import ml_dtypes
import concourse.bass as bass
import concourse.mybir as mybir
from concourse.bass_utils import run_bass_kernel_spmd

F32 = mybir.dt.float32
BF16 = mybir.dt.bfloat16
ALU = mybir.AluOpType
AF = mybir.ActivationFunctionType
AX = mybir.AxisListType

NCORES = 8
T = 4096
D = 2048
KC = 16
NB = 32
OWNB = 8
TO = 1024
EPS = 1e-6
GN_EPS = 64e-5
NFA = 1152
NZ = 4224
NTA = 320
NTO = 400
NFO = 1024
SEG = 8
SEGT = SEG * 64
NIT = 20
KSEL = 256


class Dep:
    __slots__ = ("w", "r")

    def __init__(self):
        self.w = None
        self.r = []


class Prog:
    ENGS = ("pe", "act", "dve", "pool", "sp")

    def __init__(self, nc, es, ndsem=24):
        self.nc = nc
        self.q = {e: [] for e in self.ENGS}
        self.sem = {e: es.enter_context(nc.semaphore("s_" + e)) for e in self.ENGS}
        self.cnt = {e: 0 for e in self.ENGS}
        self.real = {e: 0 for e in self.ENGS}
        self.known = {e: {} for e in self.ENGS}
        self.dsem = [es.enter_context(nc.semaphore("d%d" % i)) for i in range(ndsem)]
        self.dcnt = [0] * ndsem
        self.dnext = 0

    def _waits(self, eng, deps):
        need = {}
        kn = self.known[eng]
        for ev in deps:
            if ev is None:
                continue
            s, v = ev
            k = id(s)
            if kn.get(k, 0) >= v:
                continue
            if k not in need or need[k][1] < v:
                need[k] = (s, v)
        for k, (s, v) in need.items():
            kn[k] = v
        return list(need.values())

    def _deps(self, eng, reads, writes):
        deps = []
        for t in reads:
            deps.append(t.w)
        for t in writes:
            deps.append(t.w)
            deps.extend(t.r)
        if eng == "pe":
            ps = self.sem["pe"]
            deps = [d for d in deps if d is not None and d[0] is not ps]
        return deps

    def _post(self, ev, reads, writes):
        for t in reads:
            t.r.append(ev)
            if len(t.r) > 48:
                best = {}
                for (s, v) in t.r:
                    if id(s) not in best or best[id(s)][1] < v:
                        best[id(s)] = (s, v)
                t.r = list(best.values())
        for t in writes:
            t.w = ev
            t.r = []

    def op(self, eng, fn, reads=(), writes=()):
        waits = self._waits(eng, self._deps(eng, reads, writes))
        self.cnt[eng] += 1
        ev = (self.sem[eng], self.cnt[eng])
        self.q[eng].append((waits, fn, ev, 1))
        self._post(ev, reads, writes)
        return ev

    def dma(self, eng, fn, reads=(), writes=()):
        i = self.dnext
        self.dnext = (i + 1) % len(self.dsem)
        deps = self._deps(eng, reads, writes)
        if self.dcnt[i] > 0:
            deps.append((self.dsem[i], 16 * self.dcnt[i]))
        waits = self._waits(eng, deps)
        self.dcnt[i] += 1
        ev = (self.dsem[i], 16 * self.dcnt[i])
        self.q[eng].append((waits, fn, ev, 16))
        self._post(ev, reads, writes)
        return ev

    def barrier(self):
        evs = [(self.sem[e], self.cnt[e]) for e in self.ENGS if self.cnt[e] > 0]
        evs += [(self.dsem[i], 16 * self.dcnt[i]) for i in range(len(self.dsem)) if self.dcnt[i] > 0]
        for e in self.ENGS:
            waits = self._waits(e, evs)
            if waits:
                self.q[e].append((waits, None, None, 0))

    def flush(self):
        nc = self.nc
        sem2eng = {id(self.sem[e]): e for e in self.ENGS}
        needed = {e: set() for e in self.ENGS}
        for e in self.ENGS:
            for waits, fn, ev, inc in self.q[e]:
                for s_, v in waits:
                    if id(s_) in sem2eng:
                        needed[sem2eng[id(s_)]].add(v)
        newval = {e: {} for e in self.ENGS}
        for e in self.ENGS:
            c = self.real[e]
            for waits, fn, ev, inc in self.q[e]:
                if fn is None or inc != 1:
                    continue
                if ev[1] in needed[e]:
                    c += 1
                    newval[e][ev[1]] = c
            self.real[e] = c

        def mk(eng):
            items = self.q[eng]

            def body(e):
                for waits, fn, ev, inc in items:
                    for s_, v in waits:
                        if id(s_) in sem2eng:
                            v = newval[sem2eng[id(s_)]][v]
                        e.wait_ge(s_, v)
                    if fn is not None:
                        ins = fn(e)
                        if inc != 1:
                            ins.then_inc(ev[0], inc)
                        elif ev[1] in newval[eng]:
                            ins.then_inc(ev[0], 1)
            return body

        with nc.Block() as block:
            block.tensor(mk("pe"))
            block.scalar(mk("act"))
            block.vector(mk("dve"))
            block.gpsimd(mk("pool"))
            block.sync(mk("sp"))
        self.q = {e: [] for e in self.ENGS}


class Defer:
    def __init__(self, P):
        self.P = P
        self.q = []

    def op(self, *a, **k):
        self.q.append(("op", a, k))

    def dma(self, *a, **k):
        self.q.append(("dma", a, k))

    def run(self, n=None):
        n = len(self.q) if n is None else min(n, len(self.q))
        for _ in range(n):
            kind, a, k = self.q.pop(0)
            getattr(self.P, kind)(*a, **k)


def ts(out, in0, s1, s2=None, op0=ALU.mult, op1=None):
    if op1 is None:
        return lambda e: e.tensor_scalar(out=out, in0=in0, scalar1=s1, scalar2=None, op0=op0)
    return lambda e: e.tensor_scalar(out=out, in0=in0, scalar1=s1, scalar2=s2, op0=op0, op1=op1)


def tsa(out, in0, s1, s2, op0, op1, accum):
    return lambda e: e.tensor_scalar(out=out, in0=in0, scalar1=s1, scalar2=s2, op0=op0, op1=op1, accum_out=accum)


def tt(out, a, b, op):
    return lambda e: e.tensor_tensor(out=out, in0=a, in1=b, op=op)


def stt(out, in0, s, in1, op0, op1):
    return lambda e: e.scalar_tensor_tensor(out=out, in0=in0, scalar=s, in1=in1, op0=op0, op1=op1)


def act(out, in_, f, bias=None, scale=None, accum=None):
    kw = {}
    if bias is not None:
        kw["bias"] = bias
    if scale is not None:
        kw["scale"] = scale
    if accum is not None:
        kw["accum_out"] = accum
    return lambda e: e.activation(out=out, in_=in_, func=f, **kw)


def mm(out, lhsT, rhs, start=True, stop=True):
    return lambda e: e.matmul(out=out, lhsT=lhsT, rhs=rhs, start=start, stop=stop)


def tr(out, in_, ident):
    return lambda e: e.transpose(out=out, in_=in_, identity=ident)


def cp(out, in_):
    return lambda e: e.tensor_copy(out=out, in_=in_)


def dm(out, in_):
    return lambda e: e.dma_start(out=out, in_=in_)


def rcp(out, in_):
    return lambda e: e.reciprocal(out=out, in_=in_)


def mset(ap, v):
    return lambda e: e.memset(ap, v)


def own_blocks(j):
    return [4 * i + j for i in range(8)]


def build_nc(dbg=False, phases=(1, 2, 3, 4), lim=(8, 2), mix_ext=False, zfa_ext=False, p1_ext=False):
    nc = bass.Bass("TRN2", target_bir_lowering=False)
    ext = lambda name, shape, dt=F32: nc.dram_tensor(name, list(shape), dt, kind="ExternalInput").ap()
    scr_kind = "ExternalOutput" if dbg else "Internal"
    scr = lambda name, shape, dt=F32: nc.dram_tensor(name, list(shape), dt, kind=scr_kind).ap()

    x_own = ext("x_own", [TO, D])
    ident_f = ext("ident_f", [128, 128])
    if 1 in phases:
        x_all = ext("x_all", [T, D])
        wAg = ext("wAg", [4 * D, 1024])
        wAx = ext("wAx", [D, 448])
        wO = ext("wO", [D, NTO + NFO])
        g_col = ext("g_col", [128, KC])
        rowc = ext("rowc", [128, 256 + 64 + 384])

    zfa = (ext if zfa_ext else scr)("zfa", [NZ, T])
    xnc = nc.dram_tensor("xnc", [8 * 128, KC * 512], BF16, kind="Internal").ap()
    d_xnc = [Dep() for _ in range(8)]
    scr1 = ext if p1_ext else scr
    ckv_tok = scr1("ckv_tok", [T, 256], BF16)
    ckvT = scr1("ckvT", [256, T], BF16)
    kiT2 = scr1("kiT2", [128, T])
    qlT = scr1("qlT", [384, TO], BF16)
    widx = scr1("widx", [TO, 16])
    gdsT = scr1("gdsT", [NFO, TO])
    if 3 in phases:
        w_uq = ext("w_uq", [384, 1024])
        iw_q = ext("iw_q", [384, 1024])
        ukT = ext("ukT", [128, 16 * 256])
        uvp = ext("uvp", [128, 2 * 16 * 128])
        madd = ext("madd", [OWNB * 128, 640])
        biasvar = ext("biasvar", [OWNB * 5 * 128, 2048])
        b15rep = ext("b15rep", [128, 2048])
        halfc = ext("halfc", [128, NIT])
    orw = scr("orw", [1024, T])
    if 2 in phases:
        pp = ext("pp", [4 * 128, 24])
        lora_w = ext("lora_w", [4 * 128, 256])
        w0_row = ext("w0_row", [4, 256])
        sel4 = ext("sel4", [128, 4])
        tri = ext("tri", [128, 256])
        ones_blk = ext("ones_blk", [128, 128])
        maskAT4 = ext("maskAT4", [128, 512])
        maskNN = ext("maskNN", [64, 512])
        id4 = ext("id4", [64, 512])
        lnrow = ext("lnrow", [4 * 64, 512])
    mixT = (ext if mix_ext else scr)("mixT", [D, TO])
    if 4 in phases:
        p_own = ext("p_own", [TO, 256])
        w_out = ext("w_out", [D, D])
        w_gate = ext("w_gate", [D, D])
        w_ple = ext("w_ple", [256, D])
        fin_row = ext("fin_row", [128, D])
        out = nc.dram_tensor("out", [TO, D], F32, kind="ExternalOutput").ap()

    with ExitStack() as top:
        P = Prog(nc, top)
        banks = [top.enter_context(nc.psum_tensor("bank%d" % i, [128, 512], F32)) for i in range(7)]
        bdep = [Dep() for _ in range(7)]
        pbf = top.enter_context(nc.psum_tensor("pbf", [128, 1024], BF16))
        d_pbf = Dep()
        d_b6b = d_pbf
        pbf32 = pbf.bitcast(F32)

        if 1 in phases:
            STQ = "pool"
            with ExitStack() as es:
                sb = lambda name, shape, dt=F32: es.enter_context(nc.sbuf_tensor(name, list(shape), dt))
                idf = sb("idf", [128, 128]); d_idf = Dep()
                idb = sb("idb", [128, 128], BF16); d_idb = Dep()
                gc = sb("gc", [128, KC]); d_gc = Dep()
                rc = sb("rc", [128, 704]); d_rc = Dep()
                P.dma("sp", dm(idf[:], ident_f), writes=[d_idf])
                P.dma("sp", dm(gc[:], g_col), writes=[d_gc])
                P.dma("sp", dm(rc[:], rowc), writes=[d_rc])
                P.op("dve", cp(idb[:], idf[:]), reads=[d_idf], writes=[d_idb])
                Wb = sb("Wb", [128, KC, NFA + NTA], BF16); d_Wb = Dep()
                stg = [sb("stg%d" % i, [128, NFA + NTA]) for i in range(2)]
                d_stg = [Dep(), Dep()]
                xt = [sb("xt%d" % i, [128, D]) for i in range(2)]
                d_xt = [Dep(), Dep()]
                junk = sb("junk", [128, D], BF16); d_junk = Dep()
                st = sb("st", [128, 8]); d_st = Dep()
                xs = sb("xs", [128, D]); d_xs = Dep()
                xnT = sb("xnT", [128, KC, 512], BF16); d_xnT = Dep()
                xnT_b = sb("xnT_b", [128, KC, 512], BF16); d_xnT_b = Dep()
                xs_b = sb("xs_b", [128, D]); d_xs_b = Dep()
                xs_l = [(xs, d_xs), (xs_b, d_xs_b)]
                X = {"t": xnT, "d": d_xnT}

                def setx(k):
                    X["t"], X["d"] = (xnT, d_xnT) if k % 2 == 0 else (xnT_b, d_xnT_b)

                zst = [sb("zst%d" % i, [128, 512]) for i in range(2)]
                d_zst = [Dep(), Dep()]
                ckv_st = sb("ckv_st", [128, 4, 256], BF16); d_ckv_st = Dep()
                ckvT_st = sb("ckvT_st", [128, 2, 512], BF16); d_ckvT_st = Dep()
                ki2 = sb("ki2", [128, 128]); d_ki2 = Dep()
                kiT_st = sb("kiT_st", [128, 512]); d_kiT_st = Dep()
                qln = sb("qln", [128, 384]); d_qln = Dep()
                ckv_f = sb("ckv_f", [128, 256]); d_ckv_f = Dep()
                qlT_st = sb("qlT_st", [128, 3, 512], BF16); d_qlT_st = Dep()
                wi_st = sb("wi_st", [128, 4, 16]); d_wi_st = Dep()

                def load_weights(wsrc, ncols, col0=0):
                    for kc in range(KC):
                        s = kc % 2
                        P.dma("sp", dm(stg[s][:, 0:ncols], wsrc[kc * 128:(kc + 1) * 128, :]), writes=[d_stg[s]])
                        eng = "dve" if kc % 2 == 0 else "pool"
                        P.op(eng, ts(Wb[:, kc, col0:col0 + ncols], stg[s][:, 0:ncols], gc[:, kc:kc + 1]),
                             reads=[d_stg[s], d_gc], writes=[d_Wb])

                def norm_transpose_group(xsrc, grp):
                    for r4 in range(4):
                        rt = grp * 4 + r4
                        s = rt % 2
                        P.dma("sp", dm(xt[s][:], xsrc[rt * 128:(rt + 1) * 128, :]), writes=[d_xt[s]])
                        P.op("act", act(junk[:], xt[s][:], AF.Square, accum=st[:, 0:1]), reads=[d_xt[s]],
                             writes=[d_junk, d_st])
                        P.op("act", act(st[:, 1:2], st[:, 0:1], AF.Sqrt, bias=EPS, scale=1.0 / D), reads=[d_st],
                             writes=[d_st])
                        P.op("dve", rcp(st[:, 2:3], st[:, 1:2]), reads=[d_st], writes=[d_st])
                        xsc, d_xsc = xs_l[s]
                        P.op("dve", ts(xsc[:], xt[s][:], st[:, 2:3]), reads=[d_xt[s], d_st], writes=[d_xsc])
                        for g in range(4):
                            for q in range(4):
                                kc = g * 4 + q
                                P.op("pe", tr(banks[g][:, q * 128:(q + 1) * 128], xsc[:, kc * 128:(kc + 1) * 128],
                                              idf[:]), reads=[d_xsc, d_idf], writes=[bdep[g]])
                            eng = "act" if g % 2 == 0 else "dve"
                            o = X["t"][:, g * 4:(g + 1) * 4, r4 * 128:(r4 + 1) * 128]
                            i = banks[g][:].rearrange("p (k t) -> p k t", k=4)
                            if eng == "act":
                                P.op("act", act(o, i, AF.Copy), reads=[bdep[g]], writes=[X["d"]])
                            else:
                                P.op("dve", cp(o, i), reads=[bdep[g]], writes=[X["d"]])

                def fm_proj(col0, nchunk, dst, tok0, xb=None, dxb=None):
                    xb = X["t"] if xb is None else xb
                    dxb = X["d"] if dxb is None else dxb
                    for fcn in range(nchunk):
                        b = 4 + (fcn % 2)
                        for kc in range(KC):
                            P.op("pe", mm(banks[b][:], Wb[:, kc, col0 + fcn * 128:col0 + (fcn + 1) * 128],
                                          xb[:, kc, :], start=(kc == 0), stop=(kc == KC - 1)),
                                 reads=[d_Wb, dxb], writes=[bdep[b]])
                        s = fcn % 2
                        if s == 0:
                            P.op("act", act(zst[s][:], banks[b][:], AF.Copy), reads=[bdep[b]], writes=[d_zst[s]])
                        else:
                            P.op("dve", cp(zst[s][:], banks[b][:]), reads=[bdep[b]], writes=[d_zst[s]])
                        P.dma(STQ, dm(dst[fcn * 128:(fcn + 1) * 128, tok0:tok0 + 512], zst[s][:]),
                              reads=[d_zst[s]])

                for hgp in range(4):
                  load_weights(wAg[hgp * D:(hgp + 1) * D], 1024)
                  if hgp == 0:
                      load_weights(wAx, 448, 1024)
                  for grp in range(lim[0]):
                    if hgp == 0:
                        setx(grp)
                        norm_transpose_group(x_all, grp)
                        P.dma(STQ, dm(xnc[grp * 128:(grp + 1) * 128, :], X["t"][:].rearrange("p k t -> p (k t)")),
                              reads=[X["d"]], writes=[d_xnc[grp]])
                    else:
                        xb_, dxb_ = (xnT, d_xnT) if grp % 2 == 0 else (xnT_b, d_xnT_b)
                        P.dma("sp", dm(xb_[:].rearrange("p k t -> p (k t)"), xnc[grp * 128:(grp + 1) * 128, :]),
                              reads=[d_xnc[grp]], writes=[dxb_])
                        fm_proj(0, 8, zfa[hgp * 1024:(hgp + 1) * 1024], grp * 512, xb_, dxb_)
                        continue
                    fm_proj(0, 8, zfa[hgp * 1024:(hgp + 1) * 1024], grp * 512)
                    fm_proj(1024, 1, zfa[4096:4224], grp * 512)
                    for r4 in range(4):
                        for kc in range(KC):
                            P.op("pe", mm(banks[6][:, 0:NTA], X["t"][:, kc, r4 * 128:(r4 + 1) * 128],
                                          Wb[:, kc, NFA:NFA + NTA], start=(kc == 0), stop=(kc == KC - 1)),
                                 reads=[d_Wb, X["d"]], writes=[bdep[6]])
                        pt = banks[6]
                        P.op("act", act(junk[:, 0:256], pt[:, 0:256], AF.Square, accum=st[:, 3:4]),
                             reads=[bdep[6]], writes=[d_junk, d_st])
                        P.op("act", act(junk[:, 0:64], pt[:, 256:320], AF.Square, accum=st[:, 4:5]),
                             reads=[bdep[6]], writes=[d_junk, d_st])
                        P.op("act", act(st[:, 3:4], st[:, 3:4], AF.Sqrt, bias=EPS, scale=1.0 / 256), reads=[d_st],
                             writes=[d_st])
                        P.op("act", act(st[:, 4:5], st[:, 4:5], AF.Sqrt, bias=EPS, scale=1.0 / 64), reads=[d_st],
                             writes=[d_st])
                        P.op("dve", rcp(st[:, 5:7], st[:, 3:5]), reads=[d_st], writes=[d_st])
                        P.op("dve", stt(ckv_f[:], pt[:, 0:256], st[:, 5:6], rc[:, 0:256], ALU.mult, ALU.mult),
                             reads=[bdep[6], d_st, d_rc], writes=[d_ckv_f])
                        P.op("pool", cp(ckv_st[:, r4, :], ckv_f[:]), reads=[d_ckv_f], writes=[d_ckv_st])
                        P.op("dve", stt(ki2[:, 0:64], pt[:, 256:320], st[:, 6:7], rc[:, 256:320], ALU.mult, ALU.mult),
                             reads=[bdep[6], d_st, d_rc], writes=[d_ki2])
                        P.op("dve", stt(ki2[:, 64:128], pt[:, 256:320], st[:, 6:7], rc[:, 256:320], ALU.mult,
                                        ALU.mult), reads=[bdep[6], d_st, d_rc], writes=[d_ki2])
                        for h2 in range(0 if 'tr' in SKIP else 2):
                            P.op("pe", tr(pbf32[:, h2 * 128:(h2 + 1) * 128], ckv_f[:, h2 * 128:(h2 + 1) * 128],
                                          idf[:]), reads=[d_ckv_f, d_idf], writes=[d_pbf])
                        if 'tr' not in SKIP:
                            for k2 in range(2):
                                P.op("act", act(ckvT_st[:, k2, r4 * 128:(r4 + 1) * 128],
                                                pbf32[:, k2 * 128:(k2 + 1) * 128], AF.Copy),
                                     reads=[d_pbf], writes=[d_ckvT_st])
                        if 'ki' not in SKIP:
                            P.op("pe", tr(pbf32[:, 256:384], ki2[:], idf[:]), reads=[d_ki2, d_idf], writes=[d_b6b])
                            P.op("act", act(kiT_st[:, r4 * 128:(r4 + 1) * 128], pbf32[:, 256:384], AF.Copy),
                                 reads=[d_b6b], writes=[d_kiT_st])
                    t0 = grp * 512
                    P.dma(STQ, dm(ckv_tok[t0:t0 + 512, :].rearrange("(r p) c -> p r c", p=128), ckv_st[:]),
                          reads=[d_ckv_st])
                    for k2 in range(0 if 'trd' in SKIP else 2):
                        P.dma(STQ, dm(ckvT[k2 * 128:(k2 + 1) * 128, t0:t0 + 512], ckvT_st[:, k2, :]),
                              reads=[d_ckvT_st])
                    P.dma(STQ, dm(kiT2[:, t0:t0 + 512], kiT_st[:]), reads=[d_kiT_st])

                load_weights(wO, NTO + NFO)
                for grp in range(lim[1]):
                    setx(grp)
                    norm_transpose_group(x_own, grp)
                    fm_proj(NTO, NFO // 128, gdsT, grp * 512)
                    for r4 in range(4):
                        for kc in range(KC):
                            P.op("pe", mm(banks[6][:, 0:NTO], X["t"][:, kc, r4 * 128:(r4 + 1) * 128],
                                          Wb[:, kc, 0:NTO], start=(kc == 0), stop=(kc == KC - 1)),
                                 reads=[d_Wb, X["d"]], writes=[bdep[6]])
                        pt = banks[6]
                        P.op("act", act(junk[:, 0:384], pt[:, 0:384], AF.Square, accum=st[:, 3:4]),
                             reads=[bdep[6]], writes=[d_junk, d_st])
                        P.op("act", act(st[:, 3:4], st[:, 3:4], AF.Sqrt, bias=EPS, scale=1.0 / 384), reads=[d_st],
                             writes=[d_st])
                        P.op("dve", rcp(st[:, 5:6], st[:, 3:4]), reads=[d_st], writes=[d_st])
                        P.op("dve", stt(qln[:], pt[:, 0:384], st[:, 5:6], rc[:, 320:704], ALU.mult, ALU.mult),
                             reads=[bdep[6], d_st, d_rc], writes=[d_qln])
                        P.op("dve", ts(wi_st[:, r4, :], pt[:, 384:400], 1.0 / 32.0), reads=[bdep[6]],
                             writes=[d_wi_st])
                        for h3 in range(3):
                            P.op("pe", tr(pbf32[:, h3 * 128:(h3 + 1) * 128], qln[:, h3 * 128:(h3 + 1) * 128], idf[:]),
                                 reads=[d_qln, d_idf], writes=[d_pbf])
                        P.op("act", act(qlT_st[:, :, r4 * 128:(r4 + 1) * 128],
                                        pbf32[:, 0:384].rearrange("p (k t) -> p k t", k=3), AF.Copy),
                             reads=[d_pbf], writes=[d_qlT_st])
                    t0 = grp * 512
                    for k3 in range(3):
                        P.dma(STQ, dm(qlT[k3 * 128:(k3 + 1) * 128, t0:t0 + 512], qlT_st[:, k3, :]),
                              reads=[d_qlT_st])
                    P.dma(STQ, dm(widx[t0:t0 + 512, :].rearrange("(r p) c -> p r c", p=128), wi_st[:]),
                          reads=[d_wi_st])
                P.barrier()
                P.flush()

        if 2 in phases:
            with ExitStack() as es:
                sb = lambda name, shape, dt=F32: es.enter_context(nc.sbuf_tensor(name, list(shape), dt))
                NH = 4
                C = 64
                cst = {}
                for nm, src, shp in (("idf", ident_f, [128, 128]), ("pp", pp[0:128], [128, 24]),
                                     ("lw", lora_w[0:128], [128, 256]),
                                     ("w0", w0_row[0:1], [1, 256]), ("tri", tri, [128, 256]), ("ob", ones_blk, [128, 128]),
                                     ("mAT", maskAT4, [128, 512]), ("mNN", maskNN, [64, 512]), ("id4", id4, [64, 512]),
                                     ("sel4", sel4, [128, 4])):
                    t_ = sb("c_" + nm, shp)
                    dd = Dep()
                    P.dma("sp", dm(t_[:], src), writes=[dd])
                    cst[nm] = (t_, dd)
                idf, d_idf = cst["idf"]; ppt, d_pp = cst["pp"]; lwt, d_lw = cst["lw"]; w0t, d_w0 = cst["w0"]
                trit, d_tri = cst["tri"]; obt, d_ob = cst["ob"]; mAT, d_mAT = cst["mAT"]; mNN, d_mNN = cst["mNN"]
                i4, d_i4 = cst["id4"]; s4t, d_s4 = cst["sel4"]
                om = sb("om", [128, 24]); d_om = Dep()

                onesr = sb("onesr", [1, 128]); d_onesr = Dep()
                P.op("dve", mset(onesr[:], 1.0), writes=[d_onesr])
                zseg = sb("zseg", [128, 9, SEGT + 1]); d_zseg = Dep()
                zs = sb("zs", [128, 7, SEGT]); d_zs = [Dep() for _ in range(7)]
                t1 = sb("t1", [128, SEGT]); d_t1 = Dep()
                t2 = sb("t2", [128, SEGT]); d_t2 = Dep()
                t3 = sb("t3", [128, SEGT]); d_t3 = Dep()
                Wt2 = [[sb("W%d_%d" % (p_, k_), [128, SEGT]) for p_ in range(2)] for k_ in range(2)]
                d_W2 = [[Dep(), Dep()], [Dep(), Dep()]]
                Wi = [sb("Wi%d" % p_, [128, SEGT]) for p_ in range(2)]; d_Wi = [Dep(), Dep()]
                Wp = [sb("Wp%d" % p_, [128, SEGT]) for p_ in range(2)]; d_Wp = [Dep(), Dep()]
                a_sb = [sb("a%d" % p_, [128, SEGT]) for p_ in range(2)]; d_a = [Dep(), Dep()]
                AR2 = [[sb("AR%d_%d" % (p_, k_), [128, SEG, 2, C], BF16) for p_ in range(2)] for k_ in range(2)]
                d_AR2 = [[Dep(), Dep()], [Dep(), Dep()]]
                BK = [sb("BK%d" % p_, [128, SEG, 2, C], BF16) for p_ in range(2)]; d_BK = [Dep(), Dep()]
                BKh = [sb("BKh%d" % p_, [128, SEG, 2, C]) for p_ in range(2)]; d_BKh = [Dep(), Dep()]
                ARo2 = [[sb("ARo%d_%d" % (p_, k_), [64, SEG, 2, C], BF16) for p_ in range(2)] for k_ in range(2)]
                d_ARo2 = [[Dep(), Dep()], [Dep(), Dep()]]
                BKo = [sb("BKo%d" % p_, [64, SEG, 2, C], BF16) for p_ in range(2)]; d_BKo = [Dep(), Dep()]
                BKho = [sb("BKho%d" % p_, [64, SEG, 2, C]) for p_ in range(2)]; d_BKho = [Dep(), Dep()]
                WCo2 = [[sb("WCo%d_%d" % (p_, k_), [64, SEG]) for p_ in range(2)] for k_ in range(2)]
                d_WCo2 = [[Dep(), Dep()], [Dep(), Dep()]]
                bonT2 = [[sb("bon%d_%d" % (p_, k_), [128, SEGT]) for p_ in range(2)] for k_ in range(2)]
                d_bon2 = [[Dep(), Dep()], [Dep(), Dep()]]
                sgT2 = [[sb("sgT%d_%d" % (p_, k_), [128, SEGT]) for p_ in range(2)] for k_ in range(2)]
                d_sgT2 = [[Dep(), Dep()], [Dep(), Dep()]]
                WC4_2 = [sb("WC4_%d" % k_, [64, SEG, 4]) for k_ in range(2)]; d_WC4_2 = [Dep(), Dep()]
                tmpH = sb("tmpH", [64, 4, 64]); d_tmpH = Dep()
                bufs = [(Wt2[k_], d_W2[k_], WCo2[k_], d_WCo2[k_], AR2[k_], d_AR2[k_], ARo2[k_], d_ARo2[k_], bonT2[k_],
                         d_bon2[k_], sgT2[k_], d_sgT2[k_]) for k_ in range(2)]
                sig4 = sb("sig4", [128, SEGT // 128, 256]); d_sig = Dep()
                lnt4 = sb("lnt4", [64, 4, 512]); d_ln = Dep()
                P.dma("sp", dm(lnt4[:], lnrow.rearrange("(g p) c -> p g c", p=64)), writes=[d_ln])
                UV = sb("UV", [128, SEG, NH, C], BF16); d_UV = [Dep() for _ in range(SEG)]
                BKt = sb("BKt", [128, SEG, NH, C], BF16); d_BKt = [Dep() for _ in range(SEG)]
                ATs = sb("ATs", [128, SEG, NH, 128], BF16); d_ATs = [Dep() for _ in range(SEG)]
                TT = sb("TT", [64, SEG, NH, 128], BF16); d_TT = [Dep() for _ in range(SEG)]
                NNl = [sb("NN%d" % k_, [64, NH, 2 * C], BF16) for k_ in range(2)]; d_NNl = [Dep(), Dep()]
                PQl = [sb("PQ%d" % k_, [64, NH, 2 * C], BF16) for k_ in range(2)]; d_PQl = [Dep(), Dep()]
                Xs = sb("Xs", [64, NH, C], BF16); d_Xs = Dep()
                Hb2 = [sb("Hb%d" % k_, [64, NH, C], BF16) for k_ in range(2)]; d_Hb2 = [Dep(), Dep()]
                Hs = sb("Hs", [64, NH, C]); d_Hs = Dep()
                Ysb = sb("Ysb", [64, SEG, NH * C]); d_Y = [Dep() for _ in range(SEG)]
                ysq = sb("ysq", [64, NH * C]); d_ysq = Dep()
                stt_ = sb("stt_", [64, 8]); d_stt = Dep()
                oT = sb("oT", [128, SEGT]); d_oT = Dep()
                osel = sb("osel", [128, 128]); d_osel = Dep()
                P.op("dve", mset(TT[:], 0.0), writes=d_TT)

                def hAP(tiles, shifted, h):
                    p_, q_ = h // 2, h % 2
                    return (tiles[p_] if q_ == 0 else shifted[p_]), p_, q_

                def emit_load(PP, hg, seg):
                    tok0 = seg * SEGT
                    if seg == 0 and hg > 0:
                        PP.dma("sp", dm(ppt[:], pp[hg * 128:(hg + 1) * 128]), writes=[d_pp])
                        PP.dma("sp", dm(lwt[:], lora_w[hg * 128:(hg + 1) * 128]), writes=[d_lw])
                        PP.dma("sp", dm(w0t[:], w0_row[hg:hg + 1]), writes=[d_w0])
                    zv = zfa[hg * 1024:(hg + 1) * 1024].rearrange("(c p) t -> p c t", p=128)
                    zl = zfa[4096:4224]
                    if seg == 0:
                        PP.op("dve", mset(zseg[:, :, 0:1], 0.0), writes=[d_zseg])
                        PP.dma("sp", dm(zseg[:, 0:8, 1:SEGT + 1], zv[:, :, 0:SEGT]), writes=[d_zseg])
                        PP.dma("sp", dm(zseg[:, 8, 1:SEGT + 1], zl[:, 0:SEGT]), writes=[d_zseg])
                    else:
                        PP.dma("sp", dm(zseg[:, 0:8, :], zv[:, :, tok0 - 1:tok0 + SEGT]), writes=[d_zseg])
                        PP.dma("sp", dm(zseg[:, 8, :], zl[:, tok0 - 1:tok0 + SEGT]), writes=[d_zseg])

                def emit_prep(PP, hg, seg, par):
                    Wt, d_W, WCo, d_WCo, AR, d_AR, ARo, d_ARo, bonT, d_bon, sgT, d_sgT = bufs[par]
                    tok0 = seg * SEGT
                    if seg == 0:
                        PP.op("dve", ts(om[:], ppt[:], -1.0, 1.0, ALU.mult, ALU.add), reads=[d_pp], writes=[d_om])
                    for zi, ch in enumerate((0, 1, 2, 3, 4, 5, 8)):
                        tb, dtb = (t1, d_t1) if zi % 2 == 0 else (t2, d_t2)
                        PP.op("pool", ts(tb[:], zseg[:, ch, 0:SEGT], ppt[:, ch:ch + 1]),
                             reads=[d_zseg, d_pp], writes=[dtb])
                        PP.op("dve", stt(zs[:, zi, :], zseg[:, ch, 1:SEGT + 1], om[:, ch:ch + 1], tb[:], ALU.mult,
                                        ALU.add), reads=[d_zseg, d_om, dtb], writes=[d_zs[zi]])
                    PP.op("act", act(zs[0:64, 6, :], zs[0:64, 6, :], AF.Tanh), reads=[d_zs[6]], writes=[d_zs[6]])
                    for tl in range(SEGT // 128):
                        PP.op("pe", mm(banks[4][:, 0:256], zs[0:64, 6, tl * 128:(tl + 1) * 128], lwt[0:64, :],
                                       start=True, stop=False), reads=[d_zs[6], d_lw], writes=[bdep[4]])
                        PP.op("pe", mm(banks[4][:, 0:256], onesr[0:1, :], w0t[0:1, :], start=False, stop=True),
                              reads=[d_onesr, d_w0], writes=[bdep[4]])
                        PP.op("act", act(sig4[:, tl, :], banks[4][:, 0:256], AF.Sigmoid), reads=[bdep[4]],
                              writes=[d_sig])
                    for p_ in range(2):
                        for tl in range(SEGT // 128):
                            for ie in range(2):
                                PP.op("pe", mm(banks[5 + ie][:, tl * 128:(tl + 1) * 128],
                                               sig4[:, tl, p_ * 128:(p_ + 1) * 128], trit[:, ie * 128:(ie + 1) * 128]),
                                      reads=[d_sig, d_tri], writes=[bdep[5 + ie]])
                        PP.op("act", act(Wt[p_][:], banks[5][:], AF.Exp), reads=[bdep[5]], writes=[d_W[p_]])
                        PP.op("act", act(Wi[p_][:], banks[5][:], AF.Exp, scale=-1.0), reads=[bdep[5]],
                              writes=[d_Wi[p_]])
                        PP.op("act", act(Wp[p_][:], banks[6][:], AF.Exp), reads=[bdep[6]], writes=[d_Wp[p_]])
                    for p_ in range(2):
                        rI, kI, vI = p_, 2 + p_, 4 + p_
                        c3 = lambda ap: ap.rearrange("p (c t) -> p c t", t=C)
                        PP.op("pe", mm(pbf32[:], lwt[64:128, p_ * 128:(p_ + 1) * 128], zs[64:128, 6, :]),
                             reads=[d_lw, d_zs[6]], writes=[d_pbf])
                        PP.op("act", act(a_sb[p_][:], pbf32[:], AF.Sigmoid, bias=ppt[:, 11 + p_:12 + p_]),
                             reads=[d_pbf, d_pp], writes=[d_a[p_]])
                        PP.op("act", act(sgT[p_][:], zseg[:, 6 + p_, 1:SEGT + 1], AF.Silu), reads=[d_zseg],
                             writes=[d_sgT[p_]])
                        PP.op("dve", ts(t1[:], zs[:, kI, :], ppt[:, 13 + p_:14 + p_]), reads=[d_zs[kI], d_pp],
                             writes=[d_t1])
                        PP.op("dve", tt(t2[:], t1[:], t1[:], ALU.mult), reads=[d_t1], writes=[d_t2])
                        PP.op("pe", mm(pbf32[:], obt[:], t2[:]), reads=[d_ob, d_t2], writes=[d_pbf])
                        PP.op("act", act(t2[:], pbf32[:], AF.Sqrt), reads=[d_pbf], writes=[d_t2])
                        PP.op("dve", ts(t2[:], t2[:], 1e-12, None, ALU.max), reads=[d_t2], writes=[d_t2])
                        PP.op("dve", rcp(t2[:], t2[:]), reads=[d_t2], writes=[d_t2])
                        PP.op("dve", tt(t1[:], t1[:], t2[:], ALU.mult), reads=[d_t1, d_t2], writes=[d_t1])
                        PP.op("dve", ts(t3[:], a_sb[p_][:], ppt[:, 15 + p_:16 + p_], om[:, 15 + p_:16 + p_], ALU.mult,
                                       ALU.add), reads=[d_a[p_], d_pp, d_om], writes=[d_t3])
                        PP.op("dve", tt(t3[:], t3[:], zs[:, kI, :], ALU.mult), reads=[d_t3, d_zs[kI]], writes=[d_t3])
                        PP.op("dve", stt(AR[p_][:, :, 0, :], c3(t1[:]), -1.0, c3(Wp[p_][:]), ALU.mult, ALU.mult),
                             reads=[d_t1, d_Wp[p_]], writes=[d_AR[p_]])
                        PP.op("pool", tt(AR[p_][:, :, 1, :], c3(zs[:, rI, :]), c3(Wt[p_][:]), ALU.mult),
                             reads=[d_zs[rI], d_W[p_]], writes=[d_AR[p_]])
                        PP.op("pool", tt(BK[p_][:, :, 0, :], c3(t3[:]), c3(Wi[p_][:]), ALU.mult),
                             reads=[d_t3, d_Wi[p_]], writes=[d_BK[p_]])
                        PP.op("dve", tt(t2[:], t1[:], a_sb[p_][:], ALU.mult), reads=[d_t1, d_a[p_]], writes=[d_t2])
                        PP.op("dve", tt(BK[p_][:, :, 1, :], c3(t2[:]), c3(Wi[p_][:]), ALU.mult),
                             reads=[d_t2, d_Wi[p_]], writes=[d_BK[p_]])
                        for c in range(SEG):
                            eng = "dve" if c % 2 == 0 else "pool"
                            PP.op(eng, ts(BKh[p_][:, c, :, :], BK[p_][:, c, :, :],
                                         Wt[p_][:, c * C + C - 1:c * C + C]), reads=[d_BK[p_], d_W[p_]],
                                 writes=[d_BKh[p_]])
                        PP.op("dve", stt(t2[:], zs[:, rI, :], ppt[:, 17 + p_:18 + p_], t3[:], ALU.mult, ALU.mult),
                             reads=[d_zs[rI], d_pp, d_t3], writes=[d_t2])
                        PP.op("pe", mm(pbf32[:], obt[:], t2[:]), reads=[d_ob, d_t2], writes=[d_pbf])
                        PP.op("dve", tt(bonT[p_][:], pbf32[:], zs[:, vI, :], ALU.mult), reads=[d_pbf, d_zs[vI]],
                             writes=[d_bon[p_]])
                        PP.dma("sp", dm(ARo[p_][:], AR[p_][64:128]), reads=[d_AR[p_]], writes=[d_ARo[p_]])
                        PP.dma("sp", dm(BKo[p_][:], BK[p_][64:128]), reads=[d_BK[p_]], writes=[d_BKo[p_]])
                        PP.dma("sp", dm(BKho[p_][:], BKh[p_][64:128]), reads=[d_BKh[p_]], writes=[d_BKho[p_]])
                        PP.dma("sp", (lambda e, p_=p_: e.dma_start(
                            out=WCo[p_][:], in_=Wt[p_][64:128, :].rearrange("p (c t) -> p c t", t=C)[:, :, C - 1],
                            allow_slow_non_contiguous=True)),
                              reads=[d_W[p_]], writes=[d_WCo[p_]])
                    for p_ in range(2):
                        PP.op("pool", cp(WC4_2[par][:, :, 2 * p_],
                                         Wt[p_][0:64, :].rearrange("p (c t) -> p c t", t=C)[:, :, C - 1]),
                              reads=[d_W[p_]], writes=[d_WC4_2[par]])
                        PP.op("pool", cp(WC4_2[par][:, :, 2 * p_ + 1], WCo[p_][:]), reads=[d_WCo[p_]],
                              writes=[d_WC4_2[par]])


                def emit_out(PP, hg, seg, par):
                    tok0 = seg * SEGT
                    Wt, d_W, WCo, d_WCo, AR, d_AR, ARo, d_ARo, bonT, d_bon, sgT, d_sgT = bufs[par]
                    for c in range(SEG):
                        yv = Ysb[:, c, :].rearrange("p (h v) -> p h v", h=NH)
                        PP.op("dve", lambda e, yv=yv: e.tensor_reduce(out=stt_[:, 0:4], in_=yv, axis=AX.X, op=ALU.add),
                             reads=[d_Y[c]], writes=[d_stt])
                        PP.op("pool", tt(ysq[:], Ysb[:, c, :], Ysb[:, c, :], ALU.mult), reads=[d_Y[c]], writes=[d_ysq])
                        PP.op("dve", lambda e: e.tensor_reduce(out=stt_[:, 4:8],
                                                              in_=ysq[:].rearrange("p (h v) -> p h v", h=NH),
                                                              axis=AX.X, op=ALU.add), reads=[d_ysq], writes=[d_stt])
                        PP.op("dve", ts(stt_[:, 0:8], stt_[:, 0:8], 1.0 / C), reads=[d_stt], writes=[d_stt])
                        PP.op("dve", tt(ysq[:, 0:4], stt_[:, 0:4], stt_[:, 0:4], ALU.mult), reads=[d_stt],
                             writes=[d_ysq])
                        PP.op("dve", tt(stt_[:, 4:8], stt_[:, 4:8], ysq[:, 0:4], ALU.subtract), reads=[d_stt, d_ysq],
                             writes=[d_stt])
                        PP.op("act", act(stt_[:, 4:8], stt_[:, 4:8], AF.Sqrt, bias=GN_EPS), reads=[d_stt],
                             writes=[d_stt])
                        PP.op("dve", rcp(stt_[:, 4:8], stt_[:, 4:8]), reads=[d_stt], writes=[d_stt])
                        for h in range(NH):
                            PP.op("dve", ts(Ysb[:, c, h * C:(h + 1) * C], Ysb[:, c, h * C:(h + 1) * C],
                                           stt_[:, h:h + 1], stt_[:, 4 + h:5 + h], ALU.subtract, ALU.mult),
                                 reads=[d_Y[c], d_stt], writes=[d_Y[c]])
                        PP.op("pool", tt(Ysb[:, c, :], Ysb[:, c, :], lnt4[:, hg, 0:256], ALU.mult), reads=[d_Y[c], d_ln],
                             writes=[d_Y[c]])
                        PP.op("pool", tt(Ysb[:, c, :], Ysb[:, c, :], lnt4[:, hg, 256:512], ALU.add), reads=[d_Y[c], d_ln],
                             writes=[d_Y[c]])
                    for p_ in range(2):
                        b_ = 4 + p_
                        for c in range(SEG):
                            PP.op("pe", tr(banks[b_][:, c * C:(c + 1) * C], Ysb[:, c, p_ * 128:(p_ + 1) * 128],
                                          idf[0:64, 0:64]), reads=[d_Y[c], d_idf], writes=[bdep[b_]])
                        PP.op("dve", tt(oT[:], banks[b_][:], bonT[p_][:], ALU.add), reads=[bdep[b_], d_bon[p_]],
                             writes=[d_oT])
                        PP.op("dve", tt(oT[:], oT[:], sgT[p_][:], ALU.mult), reads=[d_oT, d_sgT[p_]], writes=[d_oT])
                        if dbg:
                            PP.dma("sp", dm(orw[hg * 256 + p_ * 128:hg * 256 + (p_ + 1) * 128, tok0:tok0 + SEGT], oT[:]),
                                  reads=[d_oT])
                        PP.op("dve", ts(osel[:], oT[:, 0:128], s4t[:, 0:1]), reads=[d_oT, d_s4], writes=[d_osel])
                        for jj in range(1, 4):
                            PP.op("dve", stt(osel[:], oT[:, jj * 128:(jj + 1) * 128], s4t[:, jj:jj + 1], osel[:], ALU.mult,
                                            ALU.add), reads=[d_oT, d_s4, d_osel], writes=[d_osel])
                        PP.dma("sp", dm(mixT[hg * 256 + p_ * 128:hg * 256 + (p_ + 1) * 128, seg * 128:(seg + 1) * 128],
                                       osel[:]), reads=[d_osel])

                eq = Defer(P)
                per_e = 0
                units = [(a_, b_) for a_ in range(4) for b_ in range(T // SEGT)]
                emit_load(P, units[0][0], units[0][1])
                emit_prep(P, units[0][0], units[0][1], 0)
                for n_, (hg, seg) in enumerate(units):
                    tok0 = seg * SEGT
                    par = n_ % 2
                    Wt, d_W, WCo, d_WCo, AR, d_AR, ARo, d_ARo, bonT, d_bon, sgT, d_sgT = bufs[par]
                    if n_ + 1 < len(units):
                        emit_load(P, units[n_ + 1][0], units[n_ + 1][1])
                    nn2 = lambda ap: ap.rearrange("p (a h v) -> p a h v", a=2, h=NH)
                    sqb = [(banks[4], bdep[4]), (banks[6], bdep[6])]
                    pqb = [(banks[5], bdep[5]), (pbf32, d_pbf)]

                    def pre0(c, NN, d_NN, PQ, d_PQ):
                        for p_ in range(2):
                            P.op("pe", tr(banks[0][0:64, p_ * 128:(p_ + 1) * 128], zs[:, 4 + p_, c * C:(c + 1) * C],
                                          idf[:]), reads=[d_zs[4 + p_], d_idf], writes=[bdep[0]])
                        P.op("act", act(UV[0:64, c, :, :], banks[0][0:64, 0:256].rearrange("p (h v) -> p h v", h=NH),
                                        AF.Copy), reads=[bdep[0]], writes=[d_UV[c]])
                        for h in range(NH):
                            tl_, p_, q_ = hAP(BKh, BKho, h)
                            P.op("pe", tr(banks[1][:, h * C:(h + 1) * C],
                                          tl_[0:64, c, :, :].rearrange("p a t -> p (a t)"), idf[0:64, 0:64]),
                                 reads=[d_BKh[p_] if q_ == 0 else d_BKho[p_], d_idf], writes=[bdep[1]])
                        P.op("act", act(BKt[:, c, :, :], banks[1][:, 0:256].rearrange("p (h v) -> p h v", h=NH),
                                        AF.Copy), reads=[bdep[1]], writes=[d_BKt[c]])
                        for h in range(NH):
                            bk_, p_, q_ = hAP(BK, BKo, h)
                            ar_, _, _ = hAP(AR, ARo, h)
                            rd = [d_BK[p_] if q_ == 0 else d_BKo[p_], d_AR[p_] if q_ == 0 else d_ARo[p_]]
                            P.op("pe", mm(banks[2][:, h * 128:(h + 1) * 128],
                                          bk_[0:64, c, :, :].rearrange("p a t -> p (a t)"),
                                          ar_[0:64, c, :, :].rearrange("p a t -> p (a t)")), reads=rd,
                                 writes=[bdep[2]])
                            P.op("pe", mm(banks[3][0:64, h * C:(h + 1) * C], ar_[0:64, c, 0, :], bk_[0:64, c, 1, :]),
                                 reads=rd, writes=[bdep[3]])
                            P.op("pe", mm(banks[3][0:64, 256 + h * C:256 + (h + 1) * C], bk_[0:64, c, 1, :],
                                          ar_[0:64, c, 0, :]), reads=rd, writes=[bdep[3]])
                        P.op("dve", tt(ATs[:, c, :, :], banks[2][:].rearrange("p (h v) -> p h v", h=NH),
                                       mAT[:].rearrange("p (h v) -> p h v", h=NH), ALU.mult),
                             reads=[bdep[2], d_mAT], writes=[d_ATs[c]])
                        h3 = lambda ap: ap.rearrange("p (h v) -> p h v", h=NH)
                        P.op("dve", tt(NN[:, :, 0:C], h3(banks[3][0:64, 0:256]), h3(mNN[:, 0:256]), ALU.mult),
                             reads=[bdep[3], d_mNN], writes=[d_NN])
                        P.op("dve", tt(PQ[:, :, 0:C], h3(banks[3][0:64, 256:512]), h3(mNN[:, 256:512]), ALU.mult),
                             reads=[bdep[3], d_mNN], writes=[d_PQ])
                        P.op("pool", tt(PQ[:, :, C:2 * C], PQ[:, :, 0:C], h3(i4[:, 0:256]), ALU.add),
                             reads=[d_PQ, d_i4], writes=[d_PQ])
                        P.op("pool", tt(NN[:, :, C:2 * C], NN[:, :, 0:C], h3(i4[:, 0:256]), ALU.add),
                             reads=[d_NN, d_i4], writes=[d_NN])

                    for c2 in range(0, SEG, 2):
                        pair = [(c2 + k_, NNl[k_], d_NNl[k_], PQl[k_], d_PQl[k_], sqb[k_], pqb[k_]) for k_ in range(2)]
                        for (c, NN, d_NN, PQ, d_PQ, _, _) in pair:
                            pre0(c, NN, d_NN, PQ, d_PQ)
                        for lev in range(0, 6):
                            for (c, NN, d_NN, PQ, d_PQ, (bA, d_bA), (bB, d_bB)) in pair:
                                for h in range(NH):
                                    if lev == 0:
                                        P.op("pe", mm(bA[0:64, h * 128:h * 128 + C], NN[:, h, 0:C], PQ[:, h, 0:C]),
                                             reads=[d_NN, d_PQ], writes=[d_bA])
                                        P.op("pe", mm(bB[0:64, h * 128:h * 128 + C], PQ[:, h, 0:C], NN[:, h, 0:C]),
                                             reads=[d_NN, d_PQ], writes=[d_bB])
                                    elif lev <= 3:
                                        P.op("pe", mm(bA[0:64, h * 128:(h + 1) * 128], NN[:, h, 0:C], PQ[:, h, :]),
                                             reads=[d_NN, d_PQ], writes=[d_bA])
                                        P.op("pe", mm(bB[0:64, h * 128:(h + 1) * 128], PQ[:, h, 0:C], NN[:, h, :]),
                                             reads=[d_NN, d_PQ], writes=[d_bB])
                                    elif lev == 4:
                                        P.op("pe", mm(bA[0:64, h * 128 + C:(h + 1) * 128], NN[:, h, 0:C], PQ[:, h, C:2 * C]),
                                             reads=[d_NN, d_PQ], writes=[d_bA])
                                        P.op("pe", mm(bB[0:64, h * 128:h * 128 + C], PQ[:, h, 0:C], NN[:, h, 0:C]),
                                             reads=[d_NN, d_PQ], writes=[d_bB])
                                    else:
                                        P.op("pe", mm(bA[0:64, h * 128 + C:(h + 1) * 128], NN[:, h, 0:C], PQ[:, h, C:2 * C]),
                                             reads=[d_NN, d_PQ], writes=[d_bA])
                            for (c, NN, d_NN, PQ, d_PQ, (bA, d_bA), (bB, d_bB)) in pair:
                                vA = bA[0:64, :].rearrange("p (h v) -> p h v", h=NH)
                                vB = bB[0:64, :].rearrange("p (h v) -> p h v", h=NH)
                                if lev <= 3:
                                    P.op("act", act(PQ[:, :, 0:C], vA[:, :, 0:C], AF.Copy), reads=[d_bA], writes=[d_PQ])
                                    P.op("act", act(NN[:, :, 0:C], vB[:, :, 0:C], AF.Copy), reads=[d_bB], writes=[d_NN])
                                if 1 <= lev <= 3:
                                    P.op("dve", tt(PQ[:, :, C:2 * C], PQ[:, :, C:2 * C], vA[:, :, C:2 * C], ALU.add),
                                         reads=[d_PQ, d_bA], writes=[d_PQ])
                                    P.op("dve", tt(NN[:, :, C:2 * C], NN[:, :, C:2 * C], vB[:, :, C:2 * C], ALU.add),
                                         reads=[d_NN, d_bB], writes=[d_NN])
                                if lev == 4:
                                    P.op("dve", tt(PQ[:, :, C:2 * C], PQ[:, :, C:2 * C], vA[:, :, C:2 * C], ALU.add),
                                         reads=[d_PQ, d_bA], writes=[d_PQ])
                                    P.op("act", act(NN[:, :, 0:C], vB[:, :, 0:C], AF.Copy), reads=[d_bB], writes=[d_NN])
                                if lev == 5:
                                    P.op("dve", tt(TT[:, c, :, C:2 * C], PQ[:, :, C:2 * C], vA[:, :, C:2 * C], ALU.add),
                                         reads=[d_PQ, d_bA], writes=[d_TT[c]])
                        eq.run(per_e)
                    eq.run()

                    dq = Defer(P)
                    if n_ + 1 < len(units):
                        emit_prep(dq, units[n_ + 1][0], units[n_ + 1][1], 1 - par)
                    per = (len(dq.q) + SEG - 1) // SEG
                    if seg == 0:
                        P.op("dve", mset(Hs[:], 0.0), writes=[d_Hs])
                        P.op("dve", mset(Hb2[0][:], 0.0), writes=[d_Hb2[0]])
                    for c in range(SEG):
                        Hb, d_Hb = Hb2[c % 2], d_Hb2[c % 2]
                        Hbn, d_Hbn = Hb2[(c + 1) % 2], d_Hb2[(c + 1) % 2]
                        for h in range(NH):
                            ar_, p_, q_ = hAP(AR, ARo, h)
                            rd = [d_AR[p_] if q_ == 0 else d_ARo[p_], d_Hb]
                            P.op("pe", mm(banks[0][0:64, h * C:(h + 1) * C], ar_[0:64, c, 0, :], Hb[:, h, :],
                                          start=True, stop=False), reads=rd, writes=[bdep[0]])
                            P.op("pe", mm(banks[0][0:64, h * C:(h + 1) * C], ATs[0:64, c, h, 0:C], UV[0:64, c, h, :],
                                          start=False, stop=True), reads=[d_ATs[c], d_UV[c]], writes=[bdep[0]])
                        P.op("act", act(Xs[:], banks[0][0:64, 0:256].rearrange("p (h v) -> p h v", h=NH), AF.Copy),
                             reads=[bdep[0]], writes=[d_Xs])
                        P.op("dve", tt(tmpH[:], Hs[:], WC4_2[par][:, c, :].unsqueeze(2).to_broadcast([64, NH, C]), ALU.mult),
                             reads=[d_Hs, d_WC4_2[par]], writes=[d_tmpH])
                        for h in range(NH):
                            P.op("pe", mm(banks[1][:, h * C:(h + 1) * C], TT[:, c, h, :], Xs[:, h, :]),
                                 reads=[d_TT[c], d_Xs], writes=[bdep[1]])
                        P.op("act", act(UV[64:128, c, :, :],
                                        banks[1][64:128, 0:256].rearrange("p (h v) -> p h v", h=NH), AF.Copy),
                             reads=[bdep[1]], writes=[d_UV[c]])
                        for h in range(NH):
                            P.op("pe", mm(banks[3][0:64, h * C:(h + 1) * C], BKt[:, c, h, :], UV[:, c, h, :]),
                                 reads=[d_BKt[c], d_UV[c]], writes=[bdep[3]])
                        for h in range(NH):
                            ar_, p_, q_ = hAP(AR, ARo, h)
                            rd = [d_AR[p_] if q_ == 0 else d_ARo[p_], d_Hb]
                            P.op("pe", mm(banks[2][0:64, h * C:(h + 1) * C], ar_[0:64, c, 1, :], Hb[:, h, :],
                                          start=True, stop=False), reads=rd, writes=[bdep[2]])
                            P.op("pe", mm(banks[2][0:64, h * C:(h + 1) * C], ATs[:, c, h, C:2 * C], UV[:, c, h, :],
                                          start=False, stop=True), reads=[d_ATs[c], d_UV[c]], writes=[bdep[2]])
                        P.op("act", act(Ysb[:, c, :], banks[2][0:64, 0:256], AF.Copy), reads=[bdep[2]],
                             writes=[d_Y[c]])
                        st3 = banks[3][0:64, 0:256].rearrange("p (h v) -> p h v", h=NH)
                        P.op("dve", tt(Hbn[:], tmpH[:], st3, ALU.add), reads=[d_tmpH, bdep[3]], writes=[d_Hbn])
                        P.op("dve", tt(Hs[:], tmpH[:], st3, ALU.add), reads=[d_tmpH, bdep[3]], writes=[d_Hs])
                        dq.run(per)

                    dq.run()
                    eq = Defer(P)
                    emit_out(eq, hg, seg, par)
                    per_e = (len(eq.q) + 3) // 4
                eq.run()
                P.barrier()
                P.flush()

        if 3 in phases:
            with ExitStack() as es:
                sb = lambda name, shape, dt=F32: es.enter_context(nc.sbuf_tensor(name, list(shape), dt))
                idf = sb("idf3", [128, 128]); d_idf = Dep()
                P.dma("sp", dm(idf[:], ident_f), writes=[d_idf])
                ckvT_sb = sb("ckvT_sb", [128, 2, T], BF16); d_ckvT = Dep()
                ckv_sb = sb("ckv_sb", [128, NB, 256], BF16); d_ckv = Dep()
                kiTb = sb("kiTb", [128, T], BF16); d_kiT = Dep()
                stage = sb("stage3", [128, T]); d_stage = Dep()
                wuq_b = sb("wuq_b", [128, 3, 1024], BF16); d_wuq = Dep()
                iwq_b = sb("iwq_b", [128, 3, 1024], BF16); d_iwq = Dep()
                ukT_b = sb("ukT_b", [128, 16, 256], BF16); d_uk = Dep()
                uvp_b = sb("uvp_b", [128, 2, 16, 128], BF16); d_uv = Dep()
                b15 = sb("b15", [128, 2048]); d_b15 = Dep()
                hc = sb("hc", [128, NIT]); d_hc = Dep()
                ones_bf = sb("ones_bf", [128, 128], BF16); d_ones = Dep()
                P.op("dve", mset(ones_bf[:], 1.0), writes=[d_ones])
                for k2 in range(2):
                    P.dma("sp", dm(ckvT_sb[:, k2, :], ckvT[k2 * 128:(k2 + 1) * 128, :]), writes=[d_ckvT])
                P.dma("sp", dm(ckv_sb[:], ckv_tok.rearrange("(kb p) r -> p kb r", p=128)), writes=[d_ckv])
                P.dma("sp", dm(b15[:], b15rep), writes=[d_b15])
                P.dma("sp", dm(hc[:], halfc), writes=[d_hc])
                P.dma("sp", dm(stage[:], kiT2), writes=[d_stage])
                P.op("dve", cp(kiTb[:], stage[:]), reads=[d_stage], writes=[d_kiT])
                for wsrc, wdst, dd in ((w_uq, wuq_b, d_wuq), (iw_q, iwq_b, d_iwq)):
                    P.dma("sp", dm(stage[:, 0:3072].rearrange("p (k n) -> p k n", k=3),
                                   wsrc.rearrange("(k p) n -> p k n", p=128)), writes=[d_stage])
                    P.op("dve", cp(wdst[:], stage[:, 0:3072].rearrange("p (k n) -> p k n", k=3)), reads=[d_stage],
                         writes=[dd])
                P.dma("sp", dm(stage[:, 0:4096], ukT), writes=[d_stage])
                P.op("dve", cp(ukT_b[:], stage[:, 0:4096].rearrange("p (a r) -> p a r", a=16)), reads=[d_stage],
                     writes=[d_uk])
                P.dma("sp", dm(stage[:, 0:4096], uvp), writes=[d_stage])
                P.op("dve", cp(uvp_b[:], stage[:, 0:4096].rearrange("p (c h m) -> p c h m", c=2, h=16)),
                     reads=[d_stage], writes=[d_uv])

                qlb = sb("qlb", [128, 3, 128], BF16); d_qlb = Dep()
                wi = sb("wi", [128, 48]); d_wi = Dep()
                qT_sb = sb("qT_sb", [128, 16, 128], BF16); d_qT = Dep()
                qiT_sb = sb("qiT_sb", [128, 16, 128], BF16); d_qiT = Dep()
                P.op("dve", mset(qT_sb[:], 0.0), writes=[d_qT])
                P.op("dve", mset(qiT_sb[:], 0.0), writes=[d_qiT])
                qaT = sb("qaT", [128, 2, 16, 128], BF16); d_qaT = Dep()
                acc = sb("acc", [128, T]); d_acc = Dep()
                rl = [sb("rl%d" % k, [128, 512]) for k in range(2)]; d_rl = [Dep(), Dep()]
                cj = sb("cj", [128, T], BF16); d_cj = Dep()
                msk = stage; d_msk = d_stage
                maskT = sb("maskT", [128, NB, 128], BF16); d_maskT = Dep()
                bs = sb("bs", [128, 8]); d_bs = Dep()
                hk = sb("hk", [128, NIT]); d_hk = Dep()
                mad = sb("mad", [128, 640]); d_mad = Dep()
                bv = [sb("bv%d" % k, [128, 512]) for k in range(2)]; d_bv = [Dep(), Dep()]
                lg = sb("lg", [128, 512]); d_lg = Dep()
                e_sb = [sb("e_sb%d" % k, [128, 512], BF16) for k in range(2)]; d_e = [Dep(), Dep()]
                pT = [sb("pT%d" % k, [128, 4, 128], BF16) for k in range(2)]; d_pT = [Dep(), Dep()]
                rs = sb("rs", [128, 512]); d_rs = Dep()
                on = sb("on", [128, 2, 4, 128], BF16); d_on = Dep()
                gd = sb("gd", [128, 128]); d_gd = Dep()
                mo = sb("mo", [128, 128]); d_mo = Dep()

                for i in range(OWNB):
                    L = 512 * (i + 1)
                    nkb = 4 * (i + 1)
                    wk0 = max(4 * i - 1, 0)
                    P.dma("sp", dm(qlb[:], qlT[:, i * 128:(i + 1) * 128].rearrange("(k p) t -> p k t", p=128)),
                          writes=[d_qlb])
                    P.dma("sp", dm(wi[:, 0:16], widx[i * 128:(i + 1) * 128, :]), writes=[d_wi])
                    P.op("act", act(wi[:, 32:48], wi[:, 0:16], AF.Sign), reads=[d_wi], writes=[d_wi])
                    P.op("dve", tt(wi[:, 16:32], wi[:, 0:16], wi[:, 32:48], ALU.mult), reads=[d_wi], writes=[d_wi])
                    for W_, dW, dst, ddst in ((wuq_b, d_wuq, qT_sb, d_qT), (iwq_b, d_iwq, qiT_sb, d_qiT)):
                        for hb in range(4):
                            b_ = hb % 2
                            for hl in range(4):
                                h = hb * 4 + hl
                                for kc in range(3):
                                    P.op("pe", mm(banks[b_][0:64, hl * 128:(hl + 1) * 128], W_[:, kc, h * 64:(h + 1) * 64],
                                                  qlb[:, kc, :], start=(kc == 0), stop=(kc == 2)),
                                         reads=[dW, d_qlb], writes=[bdep[b_]])
                            P.op("act", act(dst[0:64, hb * 4:(hb + 1) * 4, :],
                                            banks[b_][0:64, :].rearrange("p (k t) -> p k t", k=4), AF.Copy),
                                 reads=[bdep[b_]], writes=[ddst])
                    if K3STOP < 1:
                        continue
                    for rc in range(2):
                        for hb in range(4):
                            b_ = 5 + (hb % 2)
                            for hl in range(4):
                                h = hb * 4 + hl
                                P.op("pe", mm(banks[b_][:, hl * 128:(hl + 1) * 128],
                                              ukT_b[:, h, rc * 128:(rc + 1) * 128], qT_sb[:, h, :]),
                                     reads=[d_uk, d_qT], writes=[bdep[b_]])
                            P.op("dve", ts(qaT[:, rc, hb * 4:(hb + 1) * 4, :],
                                           banks[b_][:].rearrange("p (k t) -> p k t", k=4), 0.125),
                                 reads=[bdep[b_]], writes=[d_qaT])
                    if K3STOP < 2:
                        continue
                    for st_ in range(i + 1):
                        for h in range(16):
                            b_ = h % 2
                            P.op("pe", mm(banks[b_][:], qiT_sb[:, h, :], kiTb[:, st_ * 512:(st_ + 1) * 512]),
                                 reads=[d_qiT, d_kiT], writes=[bdep[b_]])
                            P.op("act", act(rl[b_][:], banks[b_][:], AF.Relu, scale=wi[:, 16 + h:17 + h]),
                                 reads=[bdep[b_], d_wi], writes=[d_rl[b_]])
                            a_ = acc[:, st_ * 512:(st_ + 1) * 512]
                            if h == 0:
                                P.op("dve", ts(a_, rl[b_][:], wi[:, 32:33]), reads=[d_rl[b_], d_wi], writes=[d_acc])
                            else:
                                P.op("dve", stt(a_, rl[b_][:], wi[:, 32 + h:33 + h], a_, ALU.mult, ALU.add),
                                     reads=[d_rl[b_], d_wi, d_acc], writes=[d_acc])
                    if K3STOP < 3:
                        continue
                    P.op("dve", lambda e, L=L: e.tensor_reduce(out=bs[:, 0:1], in_=acc[:, 0:L], axis=AX.X, op=ALU.max),
                         reads=[d_acc], writes=[d_bs])
                    P.op("dve", lambda e, L=L: e.tensor_reduce(out=bs[:, 1:2], in_=acc[:, 0:L], axis=AX.X, op=ALU.min),
                         reads=[d_acc], writes=[d_bs])
                    P.op("dve", ts(bs[:, 3:4], bs[:, 1:2], -1.0, None, ALU.add), reads=[d_bs], writes=[d_bs])
                    P.op("dve", tt(bs[:, 2:3], bs[:, 0:1], bs[:, 1:2], ALU.subtract), reads=[d_bs], writes=[d_bs])
                    P.op("dve", ts(bs[:, 2:3], bs[:, 2:3], 2.0, None, ALU.add), reads=[d_bs], writes=[d_bs])
                    P.op("dve", ts(hk[:], hc[:], bs[:, 2:3]), reads=[d_hc, d_bs], writes=[d_hk])
                    P.dma("sp", dm(mad[:], madd[i * 128:(i + 1) * 128, :]), writes=[d_mad])
                    wcol0 = 128 if i == 0 else 0
                    P.op("dve", tt(acc[:, wk0 * 128:L], acc[:, wk0 * 128:L], mad[:, wcol0:640], ALU.add),
                         reads=[d_acc, d_mad], writes=[d_acc])
                    P.op("dve", tt(bs[:, 4:5], bs[:, 3:4], hk[:, 0:1], ALU.add), reads=[d_bs, d_hk], writes=[d_bs])
                    for it in range(NIT):
                        P.op("dve", tsa(cj[:, 0:L], acc[:, 0:L], bs[:, 4:5], 0.0, ALU.is_ge, ALU.add, bs[:, 5:6]),
                             reads=[d_acc, d_bs], writes=[d_cj, d_bs])
                        P.op("dve", ts(bs[:, 6:7], bs[:, 5:6], KSEL - 0.5, hk[:, it:it + 1], ALU.is_ge, ALU.mult),
                             reads=[d_bs, d_hk], writes=[d_bs])
                        if it < NIT - 1:
                            P.op("dve", stt(bs[:, 4:5], bs[:, 6:7], hk[:, it + 1:it + 2], bs[:, 4:5], ALU.subtract, ALU.add),
                                 reads=[d_bs, d_hk], writes=[d_bs])
                        else:
                            P.op("dve", stt(bs[:, 3:4], bs[:, 6:7], hk[:, it:it + 1], bs[:, 4:5], ALU.subtract, ALU.add),
                                 reads=[d_bs, d_hk], writes=[d_bs])
                    P.op("dve", ts(msk[:, 0:L], acc[:, 0:L], bs[:, 3:4], None, ALU.is_ge), reads=[d_acc, d_bs],
                         writes=[d_msk])
                    if K3STOP < 4:
                        continue
                    for kg in range(nkb // 4):
                        for k4 in range(4):
                            kb = kg * 4 + k4
                            P.op("pe", tr(banks[4][:, k4 * 128:(k4 + 1) * 128], msk[:, kb * 128:(kb + 1) * 128], idf[:]),
                                 reads=[d_msk, d_idf], writes=[bdep[4]])
                        P.op("act", act(maskT[:, kg * 4:(kg + 1) * 4, :],
                                        banks[4][:].rearrange("p (k t) -> p k t", k=4), AF.Copy),
                             reads=[bdep[4]], writes=[d_maskT])
                    if K3STOP < 5:
                        continue
                    for hq in range(4):
                        for kb in range(nkb):
                            b_ = kb % 2
                            for rc in range(2):
                                P.op("pe", mm(banks[b_][:], ckvT_sb[:, rc, kb * 128:(kb + 1) * 128],
                                              qaT[:, rc, hq * 4:(hq + 1) * 4, :].rearrange("p h t -> p (h t)"),
                                              start=(rc == 0), stop=(rc == 1)), reads=[d_ckvT, d_qaT],
                                     writes=[bdep[b_]])
                            if kb >= wk0:
                                w_ = kb - (4 * i - 1)
                                r0 = (i * 5 + w_) * 128
                                P.dma("sp", dm(bv[b_][:], biasvar[r0:r0 + 128, hq * 512:(hq + 1) * 512]),
                                      writes=[d_bv[b_]])
                                P.op("pool", tt(bv[b_][:], bv[b_][:], b15[:, hq * 512:(hq + 1) * 512], ALU.subtract),
                                     reads=[d_bv[b_], d_b15], writes=[d_bv[b_]])
                                P.op("dve", tt(lg[:], banks[b_][:], bv[b_][:], ALU.add), reads=[bdep[b_], d_bv[b_]],
                                     writes=[d_lg])
                                P.op("act", act(e_sb[b_][:], lg[:], AF.Exp), reads=[d_lg], writes=[d_e[b_]])
                            else:
                                P.op("act", act(e_sb[b_][:], banks[b_][:], AF.Exp), reads=[bdep[b_]], writes=[d_e[b_]])
                            P.op("dve", tt(pT[b_][:], e_sb[b_][:].rearrange("p (h t) -> p h t", h=4),
                                           maskT[:, kb, :].unsqueeze(1).to_broadcast([128, 4, 128]), ALU.mult),
                                 reads=[d_e[b_], d_maskT], writes=[d_pT[b_]])
                            pflat = pT[b_][:].rearrange("p h t -> p (h t)")
                            for rc in range(2):
                                P.op("pe", mm(banks[2 + rc][:], ckv_sb[:, kb, rc * 128:(rc + 1) * 128], pflat,
                                              start=(kb == 0), stop=(kb == nkb - 1)), reads=[d_ckv, d_pT[b_]],
                                     writes=[bdep[2 + rc]])
                            P.op("pe", mm(banks[5][:], ones_bf[:], pflat, start=(kb == 0), stop=(kb == nkb - 1)),
                                 reads=[d_ones, d_pT[b_]], writes=[bdep[5]])
                        P.op("dve", rcp(rs[:], banks[5][:]), reads=[bdep[5]], writes=[d_rs])
                        for rc in range(2):
                            P.op("dve", tt(on[:, rc, :, :].rearrange("p h t -> p (h t)"), banks[2 + rc][:], rs[:],
                                           ALU.mult), reads=[bdep[2 + rc], d_rs], writes=[d_on])
                        for pr in range(2):
                            for hl2 in range(2):
                                hl = pr * 2 + hl2
                                h = hq * 4 + hl
                                for rc in range(2):
                                    P.op("pe", mm(banks[6][:, pr * 128:(pr + 1) * 128], uvp_b[:, rc, h, :],
                                                  on[:, rc, hl, :], start=(hl2 == 0 and rc == 0),
                                                  stop=(hl2 == 1 and rc == 1)), reads=[d_uv, d_on], writes=[bdep[6]])
                        for pr in range(2):
                            fch = hq * 2 + pr
                            P.dma("sp", dm(gd[:], gdsT[fch * 128:(fch + 1) * 128, i * 128:(i + 1) * 128]), writes=[d_gd])
                            P.op("act", act(gd[:], gd[:], AF.Silu), reads=[d_gd], writes=[d_gd])
                            P.op("dve", tt(mo[:], banks[6][:, pr * 128:(pr + 1) * 128], gd[:], ALU.mult),
                                 reads=[bdep[6], d_gd], writes=[d_mo])
                            P.dma("sp", dm(mixT[1024 + fch * 128:1024 + (fch + 1) * 128, i * 128:(i + 1) * 128], mo[:]),
                                  reads=[d_mo])
                P.barrier()
                P.flush()

        if (2 not in phases or 3 not in phases) and not mix_ext:
            with ExitStack() as es:
                zt = es.enter_context(nc.sbuf_tensor("zt", [128, TO], F32)); d_zt = Dep()
                P.op("dve", mset(zt[:], 0.0), writes=[d_zt])
                ks = ([] if 2 in phases else list(range(8))) + ([] if 3 in phases else list(range(8, KC)))
                for k in ks:
                    P.dma("sp", dm(mixT[k * 128:(k + 1) * 128, :], zt[:]), reads=[d_zt])
                P.barrier()
                P.flush()

        if 4 in phases:
            with ExitStack() as es:
                sb = lambda name, shape, dt=F32: es.enter_context(nc.sbuf_tensor(name, list(shape), dt))
                idf = sb("idf4", [128, 128]); d_idf = Dep()
                P.dma("sp", dm(idf[:], ident_f), writes=[d_idf])
                Wf = sb("Wf", [128, KC, D], BF16); d_Wf = Dep()
                Pw = sb("Pw", [128, 2, D], BF16); d_Pw = Dep()
                stg = [sb("stg4_%d" % i, [128, D]) for i in range(2)]
                d_stg = [Dep(), Dep()]
                h_all = sb("h_all", [128, OWNB, D]); d_h = [Dep() for _ in range(OWNB)]
                mx = sb("mx", [128, KC, 128]); d_mx = Dep()
                mxb = sb("mxb", [128, KC, 128], BF16); d_mxb = Dep()
                xt = sb("xt4", [128, D]); d_xt = Dep()
                fr = sb("fr", [128, D]); d_fr = Dep()
                sg = sb("sg", [128, 1024]); d_sg = Dep()
                pt_ = sb("pt4", [128, 256]); d_pt = Dep()
                pTb = sb("pTb", [128, 2, 128], BF16); d_pTb = Dep()
                st = sb("st4", [128, 4]); d_st = Dep()
                junk = sb("junk4", [128, D], BF16); d_junk = Dep()
                P.dma("sp", dm(fr[:], fin_row), writes=[d_fr])

                def load_w(wsrc, nk, dst, d_dst):
                    for kc in range(nk):
                        s_ = kc % 2
                        P.dma("sp", dm(stg[s_][:], wsrc[kc * 128:(kc + 1) * 128, :]), writes=[d_stg[s_]])
                        eng = "dve" if kc % 2 == 0 else "pool"
                        P.op(eng, cp(dst[:, kc, :], stg[s_][:]), reads=[d_stg[s_]], writes=[d_dst])

                load_w(w_out, KC, Wf, d_Wf)
                for i in range(OWNB):
                    P.dma("sp", dm(mx[:], mixT[:, i * 128:(i + 1) * 128].rearrange("(k p) t -> p k t", p=128)),
                          writes=[d_mx])
                    P.op("pool", cp(mxb[:], mx[:]), reads=[d_mx], writes=[d_mxb])
                    P.dma("sp", dm(xt[:], x_own[i * 128:(i + 1) * 128, :]), writes=[d_xt])
                    for n4 in range(4):
                        for kc in range(KC):
                            P.op("pe", mm(banks[n4][:], mxb[:, kc, :], Wf[:, kc, n4 * 512:(n4 + 1) * 512],
                                          start=(kc == 0), stop=(kc == KC - 1)), reads=[d_mxb, d_Wf],
                                 writes=[bdep[n4]])
                        P.op("dve", tt(h_all[:, i, n4 * 512:(n4 + 1) * 512], banks[n4][:],
                                       xt[:, n4 * 512:(n4 + 1) * 512], ALU.add), reads=[bdep[n4], d_xt],
                             writes=[d_h[i]])
                load_w(w_gate, KC, Wf, d_Wf)
                load_w(w_ple, 2, Pw, d_Pw)
                for i in range(OWNB):
                    for g in range(4):
                        for q in range(4):
                            kc = g * 4 + q
                            P.op("pe", tr(banks[4][:, q * 128:(q + 1) * 128], h_all[:, i, kc * 128:(kc + 1) * 128],
                                          idf[:]), reads=[d_h[i], d_idf], writes=[bdep[4]])
                        P.op("act", act(mxb[:, g * 4:(g + 1) * 4, :], banks[4][:].rearrange("p (k t) -> p k t", k=4),
                                        AF.Copy), reads=[bdep[4]], writes=[d_mxb])
                    P.dma("sp", dm(pt_[:], p_own[i * 128:(i + 1) * 128, :]), writes=[d_pt])
                    for k2 in range(2):
                        P.op("pe", tr(banks[4][:, k2 * 128:(k2 + 1) * 128], pt_[:, k2 * 128:(k2 + 1) * 128], idf[:]),
                             reads=[d_pt, d_idf], writes=[bdep[4]])
                    P.op("act", act(pTb[:], banks[4][:, 0:256].rearrange("p (k t) -> p k t", k=2), AF.Copy),
                         reads=[bdep[4]], writes=[d_pTb])
                    for half in range(2):
                        for n2 in range(2):
                            n0 = half * 1024 + n2 * 512
                            for kc in range(KC):
                                P.op("pe", mm(banks[n2][:], mxb[:, kc, :], Wf[:, kc, n0:n0 + 512],
                                              start=(kc == 0), stop=(kc == KC - 1)), reads=[d_mxb, d_Wf],
                                     writes=[bdep[n2]])
                            P.op("act", act(sg[:, n2 * 512:(n2 + 1) * 512], banks[n2][:], AF.Sigmoid),
                                 reads=[bdep[n2]], writes=[d_sg])
                            for k2 in range(2):
                                P.op("pe", mm(banks[2 + n2][:], pTb[:, k2, :], Pw[:, k2, n0:n0 + 512],
                                              start=(k2 == 0), stop=(k2 == 1)), reads=[d_pTb, d_Pw],
                                     writes=[bdep[2 + n2]])
                            P.op("dve", tt(sg[:, n2 * 512:(n2 + 1) * 512], banks[2 + n2][:],
                                           sg[:, n2 * 512:(n2 + 1) * 512], ALU.mult), reads=[bdep[2 + n2], d_sg],
                                 writes=[d_sg])
                            P.op("pool", tt(h_all[:, i, n0:n0 + 512], h_all[:, i, n0:n0 + 512],
                                            sg[:, n2 * 512:(n2 + 1) * 512], ALU.add), reads=[d_sg, d_h[i]],
                                 writes=[d_h[i]])
                    P.op("act", act(junk[:], h_all[:, i, :], AF.Square, accum=st[:, 0:1]), reads=[d_h[i]],
                         writes=[d_junk, d_st])
                    P.op("act", act(st[:, 1:2], st[:, 0:1], AF.Sqrt, bias=EPS, scale=1.0 / D), reads=[d_st],
                         writes=[d_st])
                    P.op("dve", rcp(st[:, 2:3], st[:, 1:2]), reads=[d_st], writes=[d_st])
                    P.op("dve", stt(h_all[:, i, :], h_all[:, i, :], st[:, 2:3], fr[:], ALU.mult, ALU.mult),
                         reads=[d_h[i], d_st, d_fr], writes=[d_h[i]])
                    P.dma("sp", dm(out[i * 128:(i + 1) * 128, :], h_all[:, i, :]), reads=[d_h[i]])
                P.barrier()
                P.flush()

        P.barrier()
        P.flush()
    return nc


def _bucket_table():
    rel = np.arange(-4095, 4096)
    n = np.abs(rel)
    nf = np.maximum(n, 1).astype(np.float32)
    large = 8 + (np.log(nf / np.float32(8.0)) / np.float32(math.log(16.0)) * np.float32(8.0)).astype(np.int32)
    large = np.minimum(large, 15)
    return np.where(rel > 0, 16, 0) + np.where(n < 8, n, large)


def host_prep(inp, c):
    b, j = c // 4, c % 4
    f32 = np.float32
    w_in = inp["w_in"][0]
    heads = [4 * j + i for i in range(4)]
    hc = lambda base: np.concatenate([np.arange(base + h * 64, base + (h + 1) * 64) for h in heads])
    gcols = lambda g: np.concatenate([np.arange(base + h * 64, base + (h + 1) * 64)
                                      for base in (0, 1024, 2048, 3200) for h in range(4 * g, 4 * g + 4)])
    ta_cols = np.arange(4608, 4928)
    to_cols = np.concatenate([np.arange(4224, 4608), np.arange(4928, 4944)])
    fo_cols = np.arange(4944, 5968)
    toks = np.concatenate([np.arange(bk * 128, (bk + 1) * 128) for bk in own_blocks(j)])
    m = {}
    m["x_all"] = np.ascontiguousarray(inp["x"][b])
    m["x_own"] = np.ascontiguousarray(inp["x"][b][toks])
    m["wAg"] = np.ascontiguousarray(np.concatenate([w_in[:, gcols(g)] for g in range(4)], 0))
    m["wAx"] = np.ascontiguousarray(w_in[:, np.concatenate([np.arange(3072, 3200), ta_cols])])
    m["wO"] = np.ascontiguousarray(w_in[:, np.concatenate([to_cols, fo_cols])])
    m["g_col"] = np.ascontiguousarray(inp["norm_g"][0].reshape(KC, 128).T)
    m["ident_f"] = np.eye(128, dtype=f32)
    row = np.concatenate([inp["ds_kv_norm_g"][0], inp["idx_k_norm_g"][0], inp["ds_q_norm_g"][0]])
    m["rowc"] = np.ascontiguousarray(np.broadcast_to(row[None, :], (128, 704))).astype(f32)
    mu = inp["rw_mu"][0]
    ppm = np.zeros((4, 128, 24), f32)
    lwm = np.zeros((4, 128, 256), f32)
    w0m = np.zeros((4, 256), f32)
    lnm = np.zeros((4, 64, 512), f32)
    for g in range(4):
        own = np.arange(g * 256, (g + 1) * 256)
        gc_ = gcols(g)
        for ch in range(6):
            ppm[g, :, ch] = mu[gc_[ch * 128:(ch + 1) * 128]]
        ppm[g, :, 8] = mu[3072:3200]
        for p_ in range(2):
            oc = own[p_ * 128:(p_ + 1) * 128]
            ppm[g, :, 11 + p_] = inp["rw_a0"][0][oc]
            ppm[g, :, 13 + p_] = inp["rw_k_k"][0][oc]
            ppm[g, :, 15 + p_] = inp["rw_k_a"][0][oc]
            ppm[g, :, 17 + p_] = inp["rw_r_k"][0].reshape(-1)[oc]
        lwm[g, 0:64] = inp["rw_w_up"][0][:, own]
        lwm[g, 64:128] = inp["rw_a_up"][0][:, own]
        w0m[g] = inp["rw_w0"][0][own]
        lnm[g] = np.concatenate([inp["rw_ln_g"][0][own], inp["rw_ln_b"][0][own]])[None, :]
    m["pp"] = ppm.reshape(4 * 128, 24)
    m["lora_w"] = lwm.reshape(4 * 128, 256)
    m["w0_row"] = w0m
    m["lnrow"] = lnm.reshape(4 * 64, 512)
    s4 = np.zeros((128, 4), f32); s4[:, j] = 1.0
    m["sel4"] = s4
    ii = np.arange(128)
    same = (ii[:, None] // 64) == (ii[None, :] // 64)
    cdec = -math.exp(-0.5)
    m["tri"] = np.concatenate([(same & (ii[:, None] <= ii[None, :])), (same & (ii[:, None] < ii[None, :]))],
                              1).astype(f32) * f32(cdec)
    m["ones_blk"] = same.astype(f32)
    i6 = np.arange(64)
    lt = (i6[:, None] < i6[None, :]).astype(f32)
    le = (i6[:, None] <= i6[None, :]).astype(f32)
    mat = np.block([[lt, le], [lt, le]])
    m["maskAT4"] = np.ascontiguousarray(np.tile(mat, (1, 4)))
    m["maskNN"] = np.ascontiguousarray(np.concatenate([np.tile(lt.T, (1, 4)), np.tile(lt, (1, 4))], 1))
    m["id4"] = np.ascontiguousarray(np.tile(np.eye(64, dtype=f32), (1, 8)))
    m["w_uq"] = np.ascontiguousarray(inp["ds_w_uq"][0])
    m["iw_q"] = np.ascontiguousarray(inp["idx_w_q"][0])
    wuk = inp["ds_w_uk"][0]
    ukm = np.zeros((128, 16, 256), f32)
    ukm[0:64] = wuk.transpose(2, 1, 0)
    m["ukT"] = ukm.reshape(128, 16 * 256)
    wuv = inp["ds_w_uv"][0]
    uvpm = np.zeros((128, 2, 16, 128), f32)
    for h in range(16):
        q_ = h % 2
        uvpm[:, :, h, q_ * 64:(q_ + 1) * 64] = wuv[:, h, :].reshape(2, 128, 64).transpose(1, 0, 2)
    m["uvp"] = uvpm.reshape(128, 4096)
    bt = _bucket_table()
    maddm = np.full((OWNB, 128, 640), -1e30, f32)
    biasm = np.zeros((OWNB, 5, 128, 16, 128), f32)
    rb = inp["rel_bias"]
    for i in range(OWNB):
        blk = 4 * i + j
        tpos = blk * 128 + np.arange(128)
        limit = (tpos // 64 + 1) * 64
        for w_ in range(5):
            kb = 4 * i - 1 + w_
            if kb < 0:
                continue
            spos = kb * 128 + np.arange(128)
            adm = spos[None, :] < limit[:, None]
            maddm[i, :, w_ * 128:(w_ + 1) * 128] = np.where(adm, 0.0, -1e30)
            rel = spos[:, None] - tpos[None, :]
            biasm[i, w_] = rb[bt[rel + 4095]].transpose(0, 2, 1)
    m["madd"] = maddm.reshape(OWNB * 128, 640)
    m["biasvar"] = biasm.reshape(OWNB * 5 * 128, 2048)
    m["b15rep"] = np.ascontiguousarray(np.broadcast_to(np.repeat(rb[15], 128)[None, :], (128, 2048))).astype(f32)
    m["halfc"] = np.ascontiguousarray(np.broadcast_to((0.5 ** np.arange(1, NIT + 1))[None, :], (128, NIT))).astype(f32)
    m["p_own"] = np.ascontiguousarray(inp["p"][0, b][toks])
    m["w_out"] = np.ascontiguousarray(inp["w_out"][0])
    m["w_gate"] = np.ascontiguousarray(inp["ple_gate_w"][0])
    m["w_ple"] = np.ascontiguousarray(inp["ple_w"][0])
    m["fin_row"] = np.ascontiguousarray(np.broadcast_to(inp["final_g"][None, :], (128, D))).astype(f32)
    return m


def kernel(**inputs):
    inp = {k: np.asarray(v) for k, v in inputs.items()}
    nc = build_nc()
    in_maps = [host_prep(inp, c) for c in range(NCORES)]
    res = run_bass_kernel_spmd(nc, in_maps, core_ids=list(range(NCORES)))
    out = np.zeros((2, T, D), np.float32)
    for c in range(NCORES):
        b, j = c // 4, c % 4
        o = res.results[c]["out"]
        for i, bk in enumerate(own_blocks(j)):
            out[b, bk * 128:(bk + 1) * 128] = o[i * 128:(i + 1) * 128]
    return out
```

```python
import math, os
SKIP = set(os.environ.get('KSKIP', '').split(','))
K3STOP = int(os.environ.get('K3STOP', '9'))
from contextlib import ExitStack
import numpy as np
import ml_dtypes
import concourse.bass as bass
import concourse.mybir as mybir
from concourse.bass_utils import run_bass_kernel_spmd

F32 = mybir.dt.float32
BF16 = mybir.dt.bfloat16
ALU = mybir.AluOpType
AF = mybir.ActivationFunctionType
AX = mybir.AxisListType

NCORES = 8
T = 4096
D = 2048
KC = 16
NB = 32
OWNB = 8
TO = 1024
EPS = 1e-6
GN_EPS = 64e-5
NFA = 1152
NZ = 4224
NTA = 320
NTO = 400
NFO = 1024
SEG = 8
SEGT = SEG * 64
NIT = 16
KSEL = 256


class Dep:
    __slots__ = ("w", "r")

    def __init__(self):
        self.w = None
        self.r = []


class Prog:
    ENGS = ("pe", "act", "dve", "pool", "sp")

    def __init__(self, nc, es, ndsem=24):
        self.nc = nc
        self.q = {e: [] for e in self.ENGS}
        self.sem = {e: es.enter_context(nc.semaphore("s_" + e)) for e in self.ENGS}
        self.cnt = {e: 0 for e in self.ENGS}
        self.real = {e: 0 for e in self.ENGS}
        self.known = {e: {} for e in self.ENGS}
        self.dsem = [es.enter_context(nc.semaphore("d%d" % i)) for i in range(ndsem)]
        self.dcnt = [0] * ndsem
        self.dnext = 0

    def _waits(self, eng, deps):
        need = {}
        kn = self.known[eng]
        for ev in deps:
            if ev is None:
                continue
            s, v = ev
            k = id(s)
            if kn.get(k, 0) >= v:
                continue
            if k not in need or need[k][1] < v:
                need[k] = (s, v)
        for k, (s, v) in need.items():
            kn[k] = v
        return list(need.values())

    def _deps(self, eng, reads, writes):
        deps = []
        for t in reads:
            deps.append(t.w)
        for t in writes:
            deps.append(t.w)
            deps.extend(t.r)
        if eng == "pe":
            ps = self.sem["pe"]
            deps = [d for d in deps if d is not None and d[0] is not ps]
        return deps

    def _post(self, ev, reads, writes):
        for t in reads:
            t.r.append(ev)
            if len(t.r) > 48:
                best = {}
                for (s, v) in t.r:
                    if id(s) not in best or best[id(s)][1] < v:
                        best[id(s)] = (s, v)
                t.r = list(best.values())
        for t in writes:
            t.w = ev
            t.r = []

    def op(self, eng, fn, reads=(), writes=()):
        waits = self._waits(eng, self._deps(eng, reads, writes))
        self.cnt[eng] += 1
        ev = (self.sem[eng], self.cnt[eng])
        self.q[eng].append((waits, fn, ev, 1))
        self._post(ev, reads, writes)
        return ev

    def dma(self, eng, fn, reads=(), writes=()):
        i = self.dnext
        self.dnext = (i + 1) % len(self.dsem)
        deps = self._deps(eng, reads, writes)
        if self.dcnt[i] > 0:
            deps.append((self.dsem[i], 16 * self.dcnt[i]))
        waits = self._waits(eng, deps)
        self.dcnt[i] += 1
        ev = (self.dsem[i], 16 * self.dcnt[i])
        self.q[eng].append((waits, fn, ev, 16))
        self._post(ev, reads, writes)
        return ev

    def barrier(self):
        evs = [(self.sem[e], self.cnt[e]) for e in self.ENGS if self.cnt[e] > 0]
        evs += [(self.dsem[i], 16 * self.dcnt[i]) for i in range(len(self.dsem)) if self.dcnt[i] > 0]
        for e in self.ENGS:
            waits = self._waits(e, evs)
            if waits:
                self.q[e].append((waits, None, None, 0))

    def flush(self):
        nc = self.nc
        sem2eng = {id(self.sem[e]): e for e in self.ENGS}
        needed = {e: set() for e in self.ENGS}
        for e in self.ENGS:
            for waits, fn, ev, inc in self.q[e]:
                for s_, v in waits:
                    if id(s_) in sem2eng:
                        needed[sem2eng[id(s_)]].add(v)
        newval = {e: {} for e in self.ENGS}
        for e in self.ENGS:
            c = self.real[e]
            for waits, fn, ev, inc in self.q[e]:
                if fn is None or inc != 1:
                    continue
                if ev[1] in needed[e]:
                    c += 1
                    newval[e][ev[1]] = c
            self.real[e] = c

        def mk(eng):
            items = self.q[eng]

            def body(e):
                for waits, fn, ev, inc in items:
                    for s_, v in waits:
                        if id(s_) in sem2eng:
                            v = newval[sem2eng[id(s_)]][v]
                        e.wait_ge(s_, v)
                    if fn is not None:
                        ins = fn(e)
                        if inc != 1:
                            ins.then_inc(ev[0], inc)
                        elif ev[1] in newval[eng]:
                            ins.then_inc(ev[0], 1)
            return body

        with nc.Block() as block:
            block.tensor(mk("pe"))
            block.scalar(mk("act"))
            block.vector(mk("dve"))
            block.gpsimd(mk("pool"))
            block.sync(mk("sp"))
        self.q = {e: [] for e in self.ENGS}


class Defer:
    def __init__(self, P):
        self.P = P
        self.q = []

    def op(self, *a, **k):
        self.q.append(("op", a, k))

    def dma(self, *a, **k):
        self.q.append(("dma", a, k))

    def run(self, n=None):
        n = len(self.q) if n is None else min(n, len(self.q))
        for _ in range(n):
            kind, a, k = self.q.pop(0)
            getattr(self.P, kind)(*a, **k)


def ts(out, in0, s1, s2=None, op0=ALU.mult, op1=None):
    if op1 is None:
        return lambda e: e.tensor_scalar(out=out, in0=in0, scalar1=s1, scalar2=None, op0=op0)
    return lambda e: e.tensor_scalar(out=out, in0=in0, scalar1=s1, scalar2=s2, op0=op0, op1=op1)


def tsa(out, in0, s1, s2, op0, op1, accum):
    return lambda e: e.tensor_scalar(out=out, in0=in0, scalar1=s1, scalar2=s2, op0=op0, op1=op1, accum_out=accum)


def tt(out, a, b, op):
    return lambda e: e.tensor_tensor(out=out, in0=a, in1=b, op=op)


def stt(out, in0, s, in1, op0, op1):
    return lambda e: e.scalar_tensor_tensor(out=out, in0=in0, scalar=s, in1=in1, op0=op0, op1=op1)


def act(out, in_, f, bias=None, scale=None, accum=None):
    kw = {}
    if bias is not None:
        kw["bias"] = bias
    if scale is not None:
        kw["scale"] = scale
    if accum is not None:
        kw["accum_out"] = accum
    return lambda e: e.activation(out=out, in_=in_, func=f, **kw)


def mm(out, lhsT, rhs, start=True, stop=True):
    return lambda e: e.matmul(out=out, lhsT=lhsT, rhs=rhs, start=start, stop=stop)


def tr(out, in_, ident):
    return lambda e: e.transpose(out=out, in_=in_, identity=ident)


def cp(out, in_):
    return lambda e: e.tensor_copy(out=out, in_=in_)


def dm(out, in_):
    return lambda e: e.dma_start(out=out, in_=in_)


def rcp(out, in_):
    return lambda e: e.reciprocal(out=out, in_=in_)


def mset(ap, v):
    return lambda e: e.memset(ap, v)


def own_blocks(j):
    return [4 * i + j for i in range(8)]


def build_nc(dbg=False, phases=(1, 2, 3, 4), lim=(8, 2), mix_ext=False, zfa_ext=False, p1_ext=False):
    nc = bass.Bass("TRN2", target_bir_lowering=False)
    ext = lambda name, shape, dt=F32: nc.dram_tensor(name, list(shape), dt, kind="ExternalInput").ap()
    scr_kind = "ExternalOutput" if dbg else "Internal"
    scr = lambda name, shape, dt=F32: nc.dram_tensor(name, list(shape), dt, kind=scr_kind).ap()

    x_own = ext("x_own", [TO, D])
    ident_f = ext("ident_f", [128, 128])
    if 1 in phases:
        x_all = ext("x_all", [T, D])
        wAg = ext("wAg", [4 * D, 1024])
        wAx = ext("wAx", [D, 448])
        wO = ext("wO", [D, NTO + NFO])
        g_col = ext("g_col", [128, KC])
        rowc = ext("rowc", [128, 256 + 64 + 384])

    zfa = (ext if zfa_ext else scr)("zfa", [NZ, T])
    xnc = nc.dram_tensor("xnc", [8 * 128, KC * 512], BF16, kind="Internal").ap()
    d_xnc = [Dep() for _ in range(8)]
    scr1 = ext if p1_ext else scr
    ckv_tok = scr1("ckv_tok", [T, 256], BF16)
    ckvT = scr1("ckvT", [256, T], BF16)
    kiT2 = scr1("kiT2", [128, T])
    qlT = scr1("qlT", [384, TO], BF16)
    widx = scr1("widx", [TO, 16])
    gdsT = scr1("gdsT", [NFO, TO])
    if 3 in phases:
        w_uq = ext("w_uq", [384, 1024])
        iw_q = ext("iw_q", [384, 1024])
        ukT = ext("ukT", [128, 16 * 256])
        uvp = ext("uvp", [128, 2 * 16 * 128])
        madd = ext("madd", [OWNB * 128, 640])
        biasvar = ext("biasvar", [OWNB * 5 * 128, 2048])
        b15rep = ext("b15rep", [128, 2048])
        halfc = ext("halfc", [128, NIT])
    orw = scr("orw", [1024, T])
    if 2 in phases:
        pp = ext("pp", [4 * 128, 24])
        lora_w = ext("lora_w", [4 * 128, 256])
        w0_row = ext("w0_row", [4, 256])
        sel4 = ext("sel4", [128, 4])
        tri = ext("tri", [128, 256])
        ones_blk = ext("ones_blk", [128, 128])
        maskAT4 = ext("maskAT4", [128, 512])
        maskNN = ext("maskNN", [64, 512])
        id4 = ext("id4", [64, 512])
        lnrow = ext("lnrow", [4 * 64, 512])
    mixT = (ext if mix_ext else scr)("mixT", [D, TO])
    if 4 in phases:
        p_own = ext("p_own", [TO, 256])
        w_out = ext("w_out", [D, D])
        w_gate = ext("w_gate", [D, D])
        w_ple = ext("w_ple", [256, D])
        fin_row = ext("fin_row", [128, D])
        out = nc.dram_tensor("out", [TO, D], F32, kind="ExternalOutput").ap()

    with ExitStack() as top:
        P = Prog(nc, top)
        banks = [top.enter_context(nc.psum_tensor("bank%d" % i, [128, 512], F32)) for i in range(7)]
        bdep = [Dep() for _ in range(7)]
        pbf = top.enter_context(nc.psum_tensor("pbf", [128, 1024], BF16))
        d_pbf = Dep()
        d_b6b = d_pbf
        pbf32 = pbf.bitcast(F32)

        if 1 in phases:
            STQ = "pool"
            with ExitStack() as es:
                sb = lambda name, shape, dt=F32: es.enter_context(nc.sbuf_tensor(name, list(shape), dt))
                idf = sb("idf", [128, 128]); d_idf = Dep()
                idb = sb("idb", [128, 128], BF16); d_idb = Dep()
                gc = sb("gc", [128, KC]); d_gc = Dep()
                rc = sb("rc", [128, 704]); d_rc = Dep()
                P.dma("sp", dm(idf[:], ident_f), writes=[d_idf])
                P.dma("sp", dm(gc[:], g_col), writes=[d_gc])
                P.dma("sp", dm(rc[:], rowc), writes=[d_rc])
                P.op("dve", cp(idb[:], idf[:]), reads=[d_idf], writes=[d_idb])
                Wb = sb("Wb", [128, KC, NFA + NTA], BF16); d_Wb = Dep()
                stg = [sb("stg%d" % i, [128, NFA + NTA]) for i in range(2)]
                d_stg = [Dep(), Dep()]
                xt = [sb("xt%d" % i, [128, D]) for i in range(2)]
                d_xt = [Dep(), Dep()]
                junk = sb("junk", [128, D], BF16); d_junk = Dep()
                st = sb("st", [128, 8]); d_st = Dep()
                xs = sb("xs", [128, D]); d_xs = Dep()
                xnT = sb("xnT", [128, KC, 512], BF16); d_xnT = Dep()
                xnT_b = sb("xnT_b", [128, KC, 512], BF16); d_xnT_b = Dep()
                xs_b = sb("xs_b", [128, D]); d_xs_b = Dep()
                xs_l = [(xs, d_xs), (xs_b, d_xs_b)]
                X = {"t": xnT, "d": d_xnT}

                def setx(k):
                    X["t"], X["d"] = (xnT, d_xnT) if k % 2 == 0 else (xnT_b, d_xnT_b)

                zst = [sb("zst%d" % i, [128, 512]) for i in range(2)]
                d_zst = [Dep(), Dep()]
                ckv_st = sb("ckv_st", [128, 4, 256], BF16); d_ckv_st = Dep()
                ckvT_st = sb("ckvT_st", [128, 2, 512], BF16); d_ckvT_st = Dep()
                ki2 = sb("ki2", [128, 128]); d_ki2 = Dep()
                kiT_st = sb("kiT_st", [128, 512]); d_kiT_st = Dep()
                qln = sb("qln", [128, 384]); d_qln = Dep()
                ckv_f = sb("ckv_f", [128, 256]); d_ckv_f = Dep()
                qlT_st = sb("qlT_st", [128, 3, 512], BF16); d_qlT_st = Dep()
                wi_st = sb("wi_st", [128, 4, 16]); d_wi_st = Dep()

                def load_weights(wsrc, ncols, col0=0):
                    for kc in range(KC):
                        s = kc % 2
                        P.dma("sp", dm(stg[s][:, 0:ncols], wsrc[kc * 128:(kc + 1) * 128, :]), writes=[d_stg[s]])
                        eng = "dve" if kc % 2 == 0 else "pool"
                        P.op(eng, ts(Wb[:, kc, col0:col0 + ncols], stg[s][:, 0:ncols], gc[:, kc:kc + 1]),
                             reads=[d_stg[s], d_gc], writes=[d_Wb])

                def norm_transpose_group(xsrc, grp):
                    for r4 in range(4):
                        rt = grp * 4 + r4
                        s = rt % 2
                        P.dma("sp", dm(xt[s][:], xsrc[rt * 128:(rt + 1) * 128, :]), writes=[d_xt[s]])
                        P.op("act", act(junk[:], xt[s][:], AF.Square, accum=st[:, 0:1]), reads=[d_xt[s]],
                             writes=[d_junk, d_st])
                        P.op("act", act(st[:, 1:2], st[:, 0:1], AF.Sqrt, bias=EPS, scale=1.0 / D), reads=[d_st],
                             writes=[d_st])
                        P.op("dve", rcp(st[:, 2:3], st[:, 1:2]), reads=[d_st], writes=[d_st])
                        xsc, d_xsc = xs_l[s]
                        P.op("dve", ts(xsc[:], xt[s][:], st[:, 2:3]), reads=[d_xt[s], d_st], writes=[d_xsc])
                        for g in range(4):
                            for q in range(4):
                                kc = g * 4 + q
                                P.op("pe", tr(banks[g][:, q * 128:(q + 1) * 128], xsc[:, kc * 128:(kc + 1) * 128],
                                              idf[:]), reads=[d_xsc, d_idf], writes=[bdep[g]])
                            eng = "act" if g % 2 == 0 else "dve"
                            o = X["t"][:, g * 4:(g + 1) * 4, r4 * 128:(r4 + 1) * 128]
                            i = banks[g][:].rearrange("p (k t) -> p k t", k=4)
                            if eng == "act":
                                P.op("act", act(o, i, AF.Copy), reads=[bdep[g]], writes=[X["d"]])
                            else:
                                P.op("dve", cp(o, i), reads=[bdep[g]], writes=[X["d"]])

                def fm_proj(col0, nchunk, dst, tok0, xb=None, dxb=None):
                    xb = X["t"] if xb is None else xb
                    dxb = X["d"] if dxb is None else dxb
                    for fcn in range(nchunk):
                        b = 4 + (fcn % 2)
                        for kc in range(KC):
                            P.op("pe", mm(banks[b][:], Wb[:, kc, col0 + fcn * 128:col0 + (fcn + 1) * 128],
                                          xb[:, kc, :], start=(kc == 0), stop=(kc == KC - 1)),
                                 reads=[d_Wb, dxb], writes=[bdep[b]])
                        s = fcn % 2
                        if s == 0:
                            P.op("act", act(zst[s][:], banks[b][:], AF.Copy), reads=[bdep[b]], writes=[d_zst[s]])
                        else:
                            P.op("dve", cp(zst[s][:], banks[b][:]), reads=[bdep[b]], writes=[d_zst[s]])
                        P.dma(STQ, dm(dst[fcn * 128:(fcn + 1) * 128, tok0:tok0 + 512], zst[s][:]),
                              reads=[d_zst[s]])

                for hgp in range(4):
                  load_weights(wAg[hgp * D:(hgp + 1) * D], 1024)
                  if hgp == 0:
                      load_weights(wAx, 448, 1024)
                  for grp in range(lim[0]):
                    if hgp == 0:
                        setx(grp)
                        norm_transpose_group(x_all, grp)
                        P.dma(STQ, dm(xnc[grp * 128:(grp + 1) * 128, :], X["t"][:].rearrange("p k t -> p (k t)")),
                              reads=[X["d"]], writes=[d_xnc[grp]])
                    else:
                        xb_, dxb_ = (xnT, d_xnT) if grp % 2 == 0 else (xnT_b, d_xnT_b)
                        P.dma("sp", dm(xb_[:].rearrange("p k t -> p (k t)"), xnc[grp * 128:(grp + 1) * 128, :]),
                              reads=[d_xnc[grp]], writes=[dxb_])
                        fm_proj(0, 8, zfa[hgp * 1024:(hgp + 1) * 1024], grp * 512, xb_, dxb_)
                        continue
                    fm_proj(0, 8, zfa[hgp * 1024:(hgp + 1) * 1024], grp * 512)
                    fm_proj(1024, 1, zfa[4096:4224], grp * 512)
                    for r4 in range(4):
                        for kc in range(KC):
                            P.op("pe", mm(banks[6][:, 0:NTA], X["t"][:, kc, r4 * 128:(r4 + 1) * 128],
                                          Wb[:, kc, NFA:NFA + NTA], start=(kc == 0), stop=(kc == KC - 1)),
                                 reads=[d_Wb, X["d"]], writes=[bdep[6]])
                        pt = banks[6]
                        P.op("act", act(junk[:, 0:256], pt[:, 0:256], AF.Square, accum=st[:, 3:4]),
                             reads=[bdep[6]], writes=[d_junk, d_st])
                        P.op("act", act(junk[:, 0:64], pt[:, 256:320], AF.Square, accum=st[:, 4:5]),
                             reads=[bdep[6]], writes=[d_junk, d_st])
                        P.op("act", act(st[:, 3:4], st[:, 3:4], AF.Sqrt, bias=EPS, scale=1.0 / 256), reads=[d_st],
                             writes=[d_st])
                        P.op("act", act(st[:, 4:5], st[:, 4:5], AF.Sqrt, bias=EPS, scale=1.0 / 64), reads=[d_st],
                             writes=[d_st])
                        P.op("dve", rcp(st[:, 5:7], st[:, 3:5]), reads=[d_st], writes=[d_st])
                        P.op("dve", stt(ckv_f[:], pt[:, 0:256], st[:, 5:6], rc[:, 0:256], ALU.mult, ALU.mult),
                             reads=[bdep[6], d_st, d_rc], writes=[d_ckv_f])
                        P.op("pool", cp(ckv_st[:, r4, :], ckv_f[:]), reads=[d_ckv_f], writes=[d_ckv_st])
                        P.op("dve", stt(ki2[:, 0:64], pt[:, 256:320], st[:, 6:7], rc[:, 256:320], ALU.mult, ALU.mult),
                             reads=[bdep[6], d_st, d_rc], writes=[d_ki2])
                        P.op("dve", stt(ki2[:, 64:128], pt[:, 256:320], st[:, 6:7], rc[:, 256:320], ALU.mult,
                                        ALU.mult), reads=[bdep[6], d_st, d_rc], writes=[d_ki2])
                        for h2 in range(0 if 'tr' in SKIP else 2):
                            P.op("pe", tr(pbf32[:, h2 * 128:(h2 + 1) * 128], ckv_f[:, h2 * 128:(h2 + 1) * 128],
                                          idf[:]), reads=[d_ckv_f, d_idf], writes=[d_pbf])
                        if 'tr' not in SKIP:
                            for k2 in range(2):
                                P.op("act", act(ckvT_st[:, k2, r4 * 128:(r4 + 1) * 128],
                                                pbf32[:, k2 * 128:(k2 + 1) * 128], AF.Copy),
                                     reads=[d_pbf], writes=[d_ckvT_st])
                        if 'ki' not in SKIP:
                            P.op("pe", tr(pbf32[:, 256:384], ki2[:], idf[:]), reads=[d_ki2, d_idf], writes=[d_b6b])
                            P.op("act", act(kiT_st[:, r4 * 128:(r4 + 1) * 128], pbf32[:, 256:384], AF.Copy),
                                 reads=[d_b6b], writes=[d_kiT_st])
                    t0 = grp * 512
                    P.dma(STQ, dm(ckv_tok[t0:t0 + 512, :].rearrange("(r p) c -> p r c", p=128), ckv_st[:]),
                          reads=[d_ckv_st])
                    for k2 in range(0 if 'trd' in SKIP else 2):
                        P.dma(STQ, dm(ckvT[k2 * 128:(k2 + 1) * 128, t0:t0 + 512], ckvT_st[:, k2, :]),
                              reads=[d_ckvT_st])
                    P.dma(STQ, dm(kiT2[:, t0:t0 + 512], kiT_st[:]), reads=[d_kiT_st])

                load_weights(wO, NTO + NFO)
                for grp in range(lim[1]):
                    setx(grp)
                    norm_transpose_group(x_own, grp)
                    fm_proj(NTO, NFO // 128, gdsT, grp * 512)
                    for r4 in range(4):
                        for kc in range(KC):
                            P.op("pe", mm(banks[6][:, 0:NTO], X["t"][:, kc, r4 * 128:(r4 + 1) * 128],
                                          Wb[:, kc, 0:NTO], start=(kc == 0), stop=(kc == KC - 1)),
                                 reads=[d_Wb, X["d"]], writes=[bdep[6]])
                        pt = banks[6]
                        P.op("act", act(junk[:, 0:384], pt[:, 0:384], AF.Square, accum=st[:, 3:4]),
                             reads=[bdep[6]], writes=[d_junk, d_st])
                        P.op("act", act(st[:, 3:4], st[:, 3:4], AF.Sqrt, bias=EPS, scale=1.0 / 384), reads=[d_st],
                             writes=[d_st])
                        P.op("dve", rcp(st[:, 5:6], st[:, 3:4]), reads=[d_st], writes=[d_st])
                        P.op("dve", stt(qln[:], pt[:, 0:384], st[:, 5:6], rc[:, 320:704], ALU.mult, ALU.mult),
                             reads=[bdep[6], d_st, d_rc], writes=[d_qln])
                        P.op("dve", ts(wi_st[:, r4, :], pt[:, 384:400], 1.0 / 32.0), reads=[bdep[6]],
                             writes=[d_wi_st])
                        for h3 in range(3):
                            P.op("pe", tr(pbf32[:, h3 * 128:(h3 + 1) * 128], qln[:, h3 * 128:(h3 + 1) * 128], idf[:]),
                                 reads=[d_qln, d_idf], writes=[d_pbf])
                        P.op("act", act(qlT_st[:, :, r4 * 128:(r4 + 1) * 128],
                                        pbf32[:, 0:384].rearrange("p (k t) -> p k t", k=3), AF.Copy),
                             reads=[d_pbf], writes=[d_qlT_st])
                    t0 = grp * 512
                    for k3 in range(3):
                        P.dma(STQ, dm(qlT[k3 * 128:(k3 + 1) * 128, t0:t0 + 512], qlT_st[:, k3, :]),
                              reads=[d_qlT_st])
                    P.dma(STQ, dm(widx[t0:t0 + 512, :].rearrange("(r p) c -> p r c", p=128), wi_st[:]),
                          reads=[d_wi_st])
                P.barrier()
                P.flush()

        if 2 in phases:
            with ExitStack() as es:
                sb = lambda name, shape, dt=F32: es.enter_context(nc.sbuf_tensor(name, list(shape), dt))
                NH = 4
                C = 64
                cst = {}
                for nm, src, shp in (("idf", ident_f, [128, 128]), ("pp", pp[0:128], [128, 24]),
                                     ("lw", lora_w[0:128], [128, 256]),
                                     ("w0", w0_row[0:1], [1, 256]), ("tri", tri, [128, 256]), ("ob", ones_blk, [128, 128]),
                                     ("mAT", maskAT4, [128, 512]), ("mNN", maskNN, [64, 512]), ("id4", id4, [64, 512]),
                                     ("sel4", sel4, [128, 4])):
                    t_ = sb("c_" + nm, shp)
                    dd = Dep()
                    P.dma("sp", dm(t_[:], src), writes=[dd])
                    cst[nm] = (t_, dd)
                idf, d_idf = cst["idf"]; ppt, d_pp = cst["pp"]; lwt, d_lw = cst["lw"]; w0t, d_w0 = cst["w0"]
                trit, d_tri = cst["tri"]; obt, d_ob = cst["ob"]; mAT, d_mAT = cst["mAT"]; mNN, d_mNN = cst["mNN"]
                i4, d_i4 = cst["id4"]; s4t, d_s4 = cst["sel4"]
                om = sb("om", [128, 24]); d_om = Dep()

                onesr = sb("onesr", [1, 128]); d_onesr = Dep()
                P.op("dve", mset(onesr[:], 1.0), writes=[d_onesr])
                zseg = sb("zseg", [128, 9, SEGT + 1]); d_zseg = Dep()
                zs = sb("zs", [128, 7, SEGT]); d_zs = [Dep() for _ in range(7)]
                t1 = sb("t1", [128, SEGT]); d_t1 = Dep()
                t2 = sb("t2", [128, SEGT]); d_t2 = Dep()
                t3 = sb("t3", [128, SEGT]); d_t3 = Dep()
                Wt2 = [[sb("W%d_%d" % (p_, k_), [128, SEGT]) for p_ in range(2)] for k_ in range(2)]
                d_W2 = [[Dep(), Dep()], [Dep(), Dep()]]
                Wi = [sb("Wi%d" % p_, [128, SEGT]) for p_ in range(2)]; d_Wi = [Dep(), Dep()]
                Wp = [sb("Wp%d" % p_, [128, SEGT]) for p_ in range(2)]; d_Wp = [Dep(), Dep()]
                a_sb = [sb("a%d" % p_, [128, SEGT]) for p_ in range(2)]; d_a = [Dep(), Dep()]
                AR2 = [[sb("AR%d_%d" % (p_, k_), [128, SEG, 2, C], BF16) for p_ in range(2)] for k_ in range(2)]
                d_AR2 = [[Dep(), Dep()], [Dep(), Dep()]]
                BK = [sb("BK%d" % p_, [128, SEG, 2, C], BF16) for p_ in range(2)]; d_BK = [Dep(), Dep()]
                BKh = [sb("BKh%d" % p_, [128, SEG, 2, C]) for p_ in range(2)]; d_BKh = [Dep(), Dep()]
                ARo2 = [[sb("ARo%d_%d" % (p_, k_), [64, SEG, 2, C], BF16) for p_ in range(2)] for k_ in range(2)]
                d_ARo2 = [[Dep(), Dep()], [Dep(), Dep()]]
                BKo = [sb("BKo%d" % p_, [64, SEG, 2, C], BF16) for p_ in range(2)]; d_BKo = [Dep(), Dep()]
                BKho = [sb("BKho%d" % p_, [64, SEG, 2, C]) for p_ in range(2)]; d_BKho = [Dep(), Dep()]
                WCo2 = [[sb("WCo%d_%d" % (p_, k_), [64, SEG]) for p_ in range(2)] for k_ in range(2)]
                d_WCo2 = [[Dep(), Dep()], [Dep(), Dep()]]
                bonT2 = [[sb("bon%d_%d" % (p_, k_), [128, SEGT]) for p_ in range(2)] for k_ in range(2)]
                d_bon2 = [[Dep(), Dep()], [Dep(), Dep()]]
                sgT2 = [[sb("sgT%d_%d" % (p_, k_), [128, SEGT]) for p_ in range(2)] for k_ in range(2)]
                d_sgT2 = [[Dep(), Dep()], [Dep(), Dep()]]
                WC4_2 = [sb("WC4_%d" % k_, [64, SEG, 4]) for k_ in range(2)]; d_WC4_2 = [Dep(), Dep()]
                tmpH = sb("tmpH", [64, 4, 64]); d_tmpH = Dep()
                bufs = [(Wt2[k_], d_W2[k_], WCo2[k_], d_WCo2[k_], AR2[k_], d_AR2[k_], ARo2[k_], d_ARo2[k_], bonT2[k_],
                         d_bon2[k_], sgT2[k_], d_sgT2[k_]) for k_ in range(2)]
                sig4 = sb("sig4", [128, SEGT // 128, 256]); d_sig = Dep()
                lnt4 = sb("lnt4", [64, 4, 512]); d_ln = Dep()
                P.dma("sp", dm(lnt4[:], lnrow.rearrange("(g p) c -> p g c", p=64)), writes=[d_ln])
                UV = sb("UV", [128, SEG, NH, C], BF16); d_UV = [Dep() for _ in range(SEG)]
                BKt = sb("BKt", [128, SEG, NH, C], BF16); d_BKt = [Dep() for _ in range(SEG)]
                ATs = sb("ATs", [128, SEG, NH, 128], BF16); d_ATs = [Dep() for _ in range(SEG)]
                TT = sb("TT", [64, SEG, NH, 128], BF16); d_TT = [Dep() for _ in range(SEG)]
                NNl = [sb("NN%d" % k_, [64, 2, NH, C], BF16) for k_ in range(2)]; d_NNl = [Dep(), Dep()]
                PQl = [sb("PQ%d" % k_, [64, 2, NH, C], BF16) for k_ in range(2)]; d_PQl = [Dep(), Dep()]
                Xs = sb("Xs", [64, NH, C], BF16); d_Xs = Dep()
                Hb2 = [sb("Hb%d" % k_, [64, NH, C], BF16) for k_ in range(2)]; d_Hb2 = [Dep(), Dep()]
                Hs = sb("Hs", [64, NH, C]); d_Hs = Dep()
                Ysb = sb("Ysb", [64, SEG, NH * C]); d_Y = [Dep() for _ in range(SEG)]
                ysq = sb("ysq", [64, NH * C]); d_ysq = Dep()
                stt_ = sb("stt_", [64, 8]); d_stt = Dep()
                oT = sb("oT", [128, SEGT]); d_oT = Dep()
                osel = sb("osel", [128, 128]); d_osel = Dep()
                P.op("dve", mset(TT[:], 0.0), writes=d_TT)

                def hAP(tiles, shifted, h):
                    p_, q_ = h // 2, h % 2
                    return (tiles[p_] if q_ == 0 else shifted[p_]), p_, q_

                def emit_load(PP, hg, seg):
                    tok0 = seg * SEGT
                    if seg == 0 and hg > 0:
                        PP.dma("sp", dm(ppt[:], pp[hg * 128:(hg + 1) * 128]), writes=[d_pp])
                        PP.dma("sp", dm(lwt[:], lora_w[hg * 128:(hg + 1) * 128]), writes=[d_lw])
                        PP.dma("sp", dm(w0t[:], w0_row[hg:hg + 1]), writes=[d_w0])
                    zv = zfa[hg * 1024:(hg + 1) * 1024].rearrange("(c p) t -> p c t", p=128)
                    zl = zfa[4096:4224]
                    if seg == 0:
                        PP.op("dve", mset(zseg[:, :, 0:1], 0.0), writes=[d_zseg])
                        PP.dma("sp", dm(zseg[:, 0:8, 1:SEGT + 1], zv[:, :, 0:SEGT]), writes=[d_zseg])
                        PP.dma("sp", dm(zseg[:, 8, 1:SEGT + 1], zl[:, 0:SEGT]), writes=[d_zseg])
                    else:
                        PP.dma("sp", dm(zseg[:, 0:8, :], zv[:, :, tok0 - 1:tok0 + SEGT]), writes=[d_zseg])
                        PP.dma("sp", dm(zseg[:, 8, :], zl[:, tok0 - 1:tok0 + SEGT]), writes=[d_zseg])

                def emit_prep(PP, hg, seg, par):
                    Wt, d_W, WCo, d_WCo, AR, d_AR, ARo, d_ARo, bonT, d_bon, sgT, d_sgT = bufs[par]
                    tok0 = seg * SEGT
                    if seg == 0:
                        PP.op("dve", ts(om[:], ppt[:], -1.0, 1.0, ALU.mult, ALU.add), reads=[d_pp], writes=[d_om])
                    for zi, ch in enumerate((0, 1, 2, 3, 4, 5, 8)):
                        tb, dtb = (t1, d_t1) if zi % 2 == 0 else (t2, d_t2)
                        PP.op("pool", ts(tb[:], zseg[:, ch, 0:SEGT], ppt[:, ch:ch + 1]),
                             reads=[d_zseg, d_pp], writes=[dtb])
                        PP.op("dve", stt(zs[:, zi, :], zseg[:, ch, 1:SEGT + 1], om[:, ch:ch + 1], tb[:], ALU.mult,
                                        ALU.add), reads=[d_zseg, d_om, dtb], writes=[d_zs[zi]])
                    PP.op("act", act(zs[0:64, 6, :], zs[0:64, 6, :], AF.Tanh), reads=[d_zs[6]], writes=[d_zs[6]])
                    for tl in range(SEGT // 128):
                        PP.op("pe", mm(banks[4][:, 0:256], zs[0:64, 6, tl * 128:(tl + 1) * 128], lwt[0:64, :],
                                       start=True, stop=False), reads=[d_zs[6], d_lw], writes=[bdep[4]])
                        PP.op("pe", mm(banks[4][:, 0:256], onesr[0:1, :], w0t[0:1, :], start=False, stop=True),
                              reads=[d_onesr, d_w0], writes=[bdep[4]])
                        PP.op("act", act(sig4[:, tl, :], banks[4][:, 0:256], AF.Sigmoid), reads=[bdep[4]],
                              writes=[d_sig])
                    for p_ in range(2):
                        for tl in range(SEGT // 128):
                            for ie in range(2):
                                PP.op("pe", mm(banks[5 + ie][:, tl * 128:(tl + 1) * 128],
                                               sig4[:, tl, p_ * 128:(p_ + 1) * 128], trit[:, ie * 128:(ie + 1) * 128]),
                                      reads=[d_sig, d_tri], writes=[bdep[5 + ie]])
                        PP.op("act", act(Wt[p_][:], banks[5][:], AF.Exp), reads=[bdep[5]], writes=[d_W[p_]])
                        PP.op("act", act(Wi[p_][:], banks[5][:], AF.Exp, scale=-1.0), reads=[bdep[5]],
                              writes=[d_Wi[p_]])
                        PP.op("act", act(Wp[p_][:], banks[6][:], AF.Exp), reads=[bdep[6]], writes=[d_Wp[p_]])
                    for p_ in range(2):
                        rI, kI, vI = p_, 2 + p_, 4 + p_
                        c3 = lambda ap: ap.rearrange("p (c t) -> p c t", t=C)
                        PP.op("pe", mm(pbf32[:], lwt[64:128, p_ * 128:(p_ + 1) * 128], zs[64:128, 6, :]),
                             reads=[d_lw, d_zs[6]], writes=[d_pbf])
                        PP.op("act", act(a_sb[p_][:], pbf32[:], AF.Sigmoid, bias=ppt[:, 11 + p_:12 + p_]),
                             reads=[d_pbf, d_pp], writes=[d_a[p_]])
                        PP.op("act", act(sgT[p_][:], zseg[:, 6 + p_, 1:SEGT + 1], AF.Silu), reads=[d_zseg],
                             writes=[d_sgT[p_]])
                        PP.op("dve", ts(t1[:], zs[:, kI, :], ppt[:, 13 + p_:14 + p_]), reads=[d_zs[kI], d_pp],
                             writes=[d_t1])
                        PP.op("dve", tt(t2[:], t1[:], t1[:], ALU.mult), reads=[d_t1], writes=[d_t2])
                        PP.op("pe", mm(pbf32[:], obt[:], t2[:]), reads=[d_ob, d_t2], writes=[d_pbf])
                        PP.op("act", act(t2[:], pbf32[:], AF.Sqrt), reads=[d_pbf], writes=[d_t2])
                        PP.op("dve", ts(t2[:], t2[:], 1e-12, None, ALU.max), reads=[d_t2], writes=[d_t2])
                        PP.op("dve", rcp(t2[:], t2[:]), reads=[d_t2], writes=[d_t2])
                        PP.op("dve", tt(t1[:], t1[:], t2[:], ALU.mult), reads=[d_t1, d_t2], writes=[d_t1])
                        PP.op("dve", ts(t3[:], a_sb[p_][:], ppt[:, 15 + p_:16 + p_], om[:, 15 + p_:16 + p_], ALU.mult,
                                       ALU.add), reads=[d_a[p_], d_pp, d_om], writes=[d_t3])
                        PP.op("dve", tt(t3[:], t3[:], zs[:, kI, :], ALU.mult), reads=[d_t3, d_zs[kI]], writes=[d_t3])
                        PP.op("dve", stt(AR[p_][:, :, 0, :], c3(t1[:]), -1.0, c3(Wp[p_][:]), ALU.mult, ALU.mult),
                             reads=[d_t1, d_Wp[p_]], writes=[d_AR[p_]])
                        PP.op("pool", tt(AR[p_][:, :, 1, :], c3(zs[:, rI, :]), c3(Wt[p_][:]), ALU.mult),
                             reads=[d_zs[rI], d_W[p_]], writes=[d_AR[p_]])
                        PP.op("pool", tt(BK[p_][:, :, 0, :], c3(t3[:]), c3(Wi[p_][:]), ALU.mult),
                             reads=[d_t3, d_Wi[p_]], writes=[d_BK[p_]])
                        PP.op("dve", tt(t2[:], t1[:], a_sb[p_][:], ALU.mult), reads=[d_t1, d_a[p_]], writes=[d_t2])
                        PP.op("dve", tt(BK[p_][:, :, 1, :], c3(t2[:]), c3(Wi[p_][:]), ALU.mult),
                             reads=[d_t2, d_Wi[p_]], writes=[d_BK[p_]])
                        for c in range(SEG):
                            eng = "dve" if c % 2 == 0 else "pool"
                            PP.op(eng, ts(BKh[p_][:, c, :, :], BK[p_][:, c, :, :],
                                         Wt[p_][:, c * C + C - 1:c * C + C]), reads=[d_BK[p_], d_W[p_]],
                                 writes=[d_BKh[p_]])
                        PP.op("dve", stt(t2[:], zs[:, rI, :], ppt[:, 17 + p_:18 + p_], t3[:], ALU.mult, ALU.mult),
                             reads=[d_zs[rI], d_pp, d_t3], writes=[d_t2])
                        PP.op("pe", mm(pbf32[:], obt[:], t2[:]), reads=[d_ob, d_t2], writes=[d_pbf])
                        PP.op("dve", tt(bonT[p_][:], pbf32[:], zs[:, vI, :], ALU.mult), reads=[d_pbf, d_zs[vI]],
                             writes=[d_bon[p_]])
                        PP.dma("sp", dm(ARo[p_][:], AR[p_][64:128]), reads=[d_AR[p_]], writes=[d_ARo[p_]])
                        PP.dma("sp", dm(BKo[p_][:], BK[p_][64:128]), reads=[d_BK[p_]], writes=[d_BKo[p_]])
                        PP.dma("sp", dm(BKho[p_][:], BKh[p_][64:128]), reads=[d_BKh[p_]], writes=[d_BKho[p_]])
                        PP.dma("sp", (lambda e, p_=p_: e.dma_start(
                            out=WCo[p_][:], in_=Wt[p_][64:128, :].rearrange("p (c t) -> p c t", t=C)[:, :, C - 1],
                            allow_slow_non_contiguous=True)),
                              reads=[d_W[p_]], writes=[d_WCo[p_]])
                    for p_ in range(2):
                        PP.op("pool", cp(WC4_2[par][:, :, 2 * p_],
                                         Wt[p_][0:64, :].rearrange("p (c t) -> p c t", t=C)[:, :, C - 1]),
                              reads=[d_W[p_]], writes=[d_WC4_2[par]])
                        PP.op("pool", cp(WC4_2[par][:, :, 2 * p_ + 1], WCo[p_][:]), reads=[d_WCo[p_]],
                              writes=[d_WC4_2[par]])


                def emit_out(PP, hg, seg, par):
                    tok0 = seg * SEGT
                    Wt, d_W, WCo, d_WCo, AR, d_AR, ARo, d_ARo, bonT, d_bon, sgT, d_sgT = bufs[par]
                    for c in range(SEG):
                        yv = Ysb[:, c, :].rearrange("p (h v) -> p h v", h=NH)
                        PP.op("dve", lambda e, yv=yv: e.tensor_reduce(out=stt_[:, 0:4], in_=yv, axis=AX.X, op=ALU.add),
                             reads=[d_Y[c]], writes=[d_stt])
                        PP.op("pool", tt(ysq[:], Ysb[:, c, :], Ysb[:, c, :], ALU.mult), reads=[d_Y[c]], writes=[d_ysq])
                        PP.op("dve", lambda e: e.tensor_reduce(out=stt_[:, 4:8],
                                                              in_=ysq[:].rearrange("p (h v) -> p h v", h=NH),
                                                              axis=AX.X, op=ALU.add), reads=[d_ysq], writes=[d_stt])
                        PP.op("dve", ts(stt_[:, 0:8], stt_[:, 0:8], 1.0 / C), reads=[d_stt], writes=[d_stt])
                        PP.op("dve", tt(ysq[:, 0:4], stt_[:, 0:4], stt_[:, 0:4], ALU.mult), reads=[d_stt],
                             writes=[d_ysq])
                        PP.op("dve", tt(stt_[:, 4:8], stt_[:, 4:8], ysq[:, 0:4], ALU.subtract), reads=[d_stt, d_ysq],
                             writes=[d_stt])
                        PP.op("act", act(stt_[:, 4:8], stt_[:, 4:8], AF.Sqrt, bias=GN_EPS), reads=[d_stt],
                             writes=[d_stt])
                        PP.op("dve", rcp(stt_[:, 4:8], stt_[:, 4:8]), reads=[d_stt], writes=[d_stt])
                        for h in range(NH):
                            PP.op("dve", ts(Ysb[:, c, h * C:(h + 1) * C], Ysb[:, c, h * C:(h + 1) * C],
                                           stt_[:, h:h + 1], stt_[:, 4 + h:5 + h], ALU.subtract, ALU.mult),
                                 reads=[d_Y[c], d_stt], writes=[d_Y[c]])
                        PP.op("pool", tt(Ysb[:, c, :], Ysb[:, c, :], lnt4[:, hg, 0:256], ALU.mult), reads=[d_Y[c], d_ln],
                             writes=[d_Y[c]])
                        PP.op("pool", tt(Ysb[:, c, :], Ysb[:, c, :], lnt4[:, hg, 256:512], ALU.add), reads=[d_Y[c], d_ln],
                             writes=[d_Y[c]])
                    for p_ in range(2):
                        b_ = 4 + p_
                        for c in range(SEG):
                            PP.op("pe", tr(banks[b_][:, c * C:(c + 1) * C], Ysb[:, c, p_ * 128:(p_ + 1) * 128],
                                          idf[0:64, 0:64]), reads=[d_Y[c], d_idf], writes=[bdep[b_]])
                        PP.op("dve", tt(oT[:], banks[b_][:], bonT[p_][:], ALU.add), reads=[bdep[b_], d_bon[p_]],
                             writes=[d_oT])
                        PP.op("dve", tt(oT[:], oT[:], sgT[p_][:], ALU.mult), reads=[d_oT, d_sgT[p_]], writes=[d_oT])
                        if dbg:
                            PP.dma("sp", dm(orw[hg * 256 + p_ * 128:hg * 256 + (p_ + 1) * 128, tok0:tok0 + SEGT], oT[:]),
                                  reads=[d_oT])
                        PP.op("dve", ts(osel[:], oT[:, 0:128], s4t[:, 0:1]), reads=[d_oT, d_s4], writes=[d_osel])
                        for jj in range(1, 4):
                            PP.op("dve", stt(osel[:], oT[:, jj * 128:(jj + 1) * 128], s4t[:, jj:jj + 1], osel[:], ALU.mult,
                                            ALU.add), reads=[d_oT, d_s4, d_osel], writes=[d_osel])
                        PP.dma("sp", dm(mixT[hg * 256 + p_ * 128:hg * 256 + (p_ + 1) * 128, seg * 128:(seg + 1) * 128],
                                       osel[:]), reads=[d_osel])

                eq = Defer(P)
                per_e = 0
                units = [(a_, b_) for a_ in range(4) for b_ in range(T // SEGT)]
                emit_load(P, units[0][0], units[0][1])
                emit_prep(P, units[0][0], units[0][1], 0)
                for n_, (hg, seg) in enumerate(units):
                    tok0 = seg * SEGT
                    par = n_ % 2
                    Wt, d_W, WCo, d_WCo, AR, d_AR, ARo, d_ARo, bonT, d_bon, sgT, d_sgT = bufs[par]
                    if n_ + 1 < len(units):
                        emit_load(P, units[n_ + 1][0], units[n_ + 1][1])
                    nn2 = lambda ap: ap.rearrange("p (a h v) -> p a h v", a=2, h=NH)
                    sqb = [(banks[4], bdep[4]), (banks[6], bdep[6])]
                    pqb = [(banks[5], bdep[5]), (pbf32, d_pbf)]

                    def pre0(c, NN, d_NN, PQ, d_PQ):
                        for p_ in range(2):
                            P.op("pe", tr(banks[0][0:64, p_ * 128:(p_ + 1) * 128], zs[:, 4 + p_, c * C:(c + 1) * C],
                                          idf[:]), reads=[d_zs[4 + p_], d_idf], writes=[bdep[0]])
                        P.op("act", act(UV[0:64, c, :, :], banks[0][0:64, 0:256].rearrange("p (h v) -> p h v", h=NH),
                                        AF.Copy), reads=[bdep[0]], writes=[d_UV[c]])
                        for h in range(NH):
                            tl_, p_, q_ = hAP(BKh, BKho, h)
                            P.op("pe", tr(banks[1][:, h * C:(h + 1) * C],
                                          tl_[0:64, c, :, :].rearrange("p a t -> p (a t)"), idf[0:64, 0:64]),
                                 reads=[d_BKh[p_] if q_ == 0 else d_BKho[p_], d_idf], writes=[bdep[1]])
                        P.op("act", act(BKt[:, c, :, :], banks[1][:, 0:256].rearrange("p (h v) -> p h v", h=NH),
                                        AF.Copy), reads=[bdep[1]], writes=[d_BKt[c]])
                        for h in range(NH):
                            bk_, p_, q_ = hAP(BK, BKo, h)
                            ar_, _, _ = hAP(AR, ARo, h)
                            rd = [d_BK[p_] if q_ == 0 else d_BKo[p_], d_AR[p_] if q_ == 0 else d_ARo[p_]]
                            P.op("pe", mm(banks[2][:, h * 128:(h + 1) * 128],
                                          bk_[0:64, c, :, :].rearrange("p a t -> p (a t)"),
                                          ar_[0:64, c, :, :].rearrange("p a t -> p (a t)")), reads=rd,
                                 writes=[bdep[2]])
                            P.op("pe", mm(banks[3][0:64, h * C:(h + 1) * C], ar_[0:64, c, 0, :], bk_[0:64, c, 1, :]),
                                 reads=rd, writes=[bdep[3]])
                            P.op("pe", mm(banks[3][0:64, 256 + h * C:256 + (h + 1) * C], bk_[0:64, c, 1, :],
                                          ar_[0:64, c, 0, :]), reads=rd, writes=[bdep[3]])
                        P.op("dve", tt(ATs[:, c, :, :], banks[2][:].rearrange("p (h v) -> p h v", h=NH),
                                       mAT[:].rearrange("p (h v) -> p h v", h=NH), ALU.mult),
                             reads=[bdep[2], d_mAT], writes=[d_ATs[c]])
                        P.op("dve", tt(NN[:], nn2(banks[3][0:64, :]), nn2(mNN[:]), ALU.mult),
                             reads=[bdep[3], d_mNN], writes=[d_NN])
                        P.op("pool", tt(PQ[:, 0, :, :], NN[:, 1, :, :], nn2(i4[:])[:, 0, :, :], ALU.add),
                             reads=[d_NN, d_i4], writes=[d_PQ])
                        P.op("pool", tt(PQ[:, 1, :, :], NN[:, 0, :, :], nn2(i4[:])[:, 1, :, :], ALU.add),
                             reads=[d_NN, d_i4], writes=[d_PQ])

                    for c2 in range(0, SEG, 2):
                        pair = [(c2 + k_, NNl[k_], d_NNl[k_], PQl[k_], d_PQl[k_], sqb[k_], pqb[k_]) for k_ in range(2)]
                        for (c, NN, d_NN, PQ, d_PQ, _, _) in pair:
                            pre0(c, NN, d_NN, PQ, d_PQ)
                        for lev in range(1, 6):
                            last = lev == 5
                            for (c, NN, d_NN, PQ, d_PQ, (sq, d_sq), _) in pair:
                                for h in range(NH):
                                    if not last:
                                        P.op("pe", mm(sq[0:64, h * C:(h + 1) * C], NN[:, 1, h, :], NN[:, 0, h, :]),
                                             reads=[d_NN], writes=[d_sq])
                                    P.op("pe", mm(sq[0:64, 256 + h * C:256 + (h + 1) * C], NN[:, 0, h, :],
                                                  NN[:, 1, h, :]), reads=[d_NN], writes=[d_sq])
                            for (c, NN, d_NN, PQ, d_PQ, (sq, d_sq), _) in pair:
                                if not last:
                                    P.op("act", act(NN[:], nn2(sq[0:64, :]), AF.Copy), reads=[d_sq], writes=[d_NN])
                                else:
                                    P.op("act", act(NN[:, 1, :, :], nn2(sq[0:64, :])[:, 1, :, :], AF.Copy),
                                         reads=[d_sq], writes=[d_NN])
                            for (c, NN, d_NN, PQ, d_PQ, _, (pq, d_pq)) in pair:
                                for h in range(NH):
                                    P.op("pe", mm(pq[0:64, h * C:(h + 1) * C], PQ[:, 1, h, :], NN[:, 1, h, :]),
                                         reads=[d_PQ, d_NN], writes=[d_pq])
                                    if not last:
                                        P.op("pe", mm(pq[0:64, 256 + h * C:256 + (h + 1) * C], PQ[:, 0, h, :],
                                                      NN[:, 0, h, :]), reads=[d_PQ, d_NN], writes=[d_pq])
                            for (c, NN, d_NN, PQ, d_PQ, _, (pq, d_pq)) in pair:
                                if not last:
                                    P.op("dve", tt(PQ[:], PQ[:], nn2(pq[0:64, :]), ALU.add), reads=[d_PQ, d_pq],
                                         writes=[d_PQ])
                                else:
                                    P.op("dve", tt(TT[:, c, :, C:2 * C], PQ[:, 0, :, :], nn2(pq[0:64, :])[:, 0, :, :],
                                                   ALU.add), reads=[d_PQ, d_pq], writes=[d_TT[c]])
                        eq.run(per_e)
                    eq.run()

                    dq = Defer(P)
                    if n_ + 1 < len(units):
                        emit_prep(dq, units[n_ + 1][0], units[n_ + 1][1], 1 - par)
                    per = (len(dq.q) + SEG - 1) // SEG
                    if seg == 0:
                        P.op("dve", mset(Hs[:], 0.0), writes=[d_Hs])
                        P.op("dve", mset(Hb2[0][:], 0.0), writes=[d_Hb2[0]])
                    for c in range(SEG):
                        Hb, d_Hb = Hb2[c % 2], d_Hb2[c % 2]
                        Hbn, d_Hbn = Hb2[(c + 1) % 2], d_Hb2[(c + 1) % 2]
                        for h in range(NH):
                            ar_, p_, q_ = hAP(AR, ARo, h)
                            rd = [d_AR[p_] if q_ == 0 else d_ARo[p_], d_Hb]
                            P.op("pe", mm(banks[0][0:64, h * C:(h + 1) * C], ar_[0:64, c, 0, :], Hb[:, h, :],
                                          start=True, stop=False), reads=rd, writes=[bdep[0]])
                            P.op("pe", mm(banks[0][0:64, h * C:(h + 1) * C], ATs[0:64, c, h, 0:C], UV[0:64, c, h, :],
                                          start=False, stop=True), reads=[d_ATs[c], d_UV[c]], writes=[bdep[0]])
                        P.op("act", act(Xs[:], banks[0][0:64, 0:256].rearrange("p (h v) -> p h v", h=NH), AF.Copy),
                             reads=[bdep[0]], writes=[d_Xs])
                        P.op("dve", tt(tmpH[:], Hs[:], WC4_2[par][:, c, :].unsqueeze(2).to_broadcast([64, NH, C]), ALU.mult),
                             reads=[d_Hs, d_WC4_2[par]], writes=[d_tmpH])
                        for h in range(NH):
                            P.op("pe", mm(banks[1][:, h * C:(h + 1) * C], TT[:, c, h, :], Xs[:, h, :]),
                                 reads=[d_TT[c], d_Xs], writes=[bdep[1]])
                        P.op("act", act(UV[64:128, c, :, :],
                                        banks[1][64:128, 0:256].rearrange("p (h v) -> p h v", h=NH), AF.Copy),
                             reads=[bdep[1]], writes=[d_UV[c]])
                        for h in range(NH):
                            P.op("pe", mm(banks[3][0:64, h * C:(h + 1) * C], BKt[:, c, h, :], UV[:, c, h, :]),
                                 reads=[d_BKt[c], d_UV[c]], writes=[bdep[3]])
                        for h in range(NH):
                            ar_, p_, q_ = hAP(AR, ARo, h)
                            rd = [d_AR[p_] if q_ == 0 else d_ARo[p_], d_Hb]
                            P.op("pe", mm(banks[2][0:64, h * C:(h + 1) * C], ar_[0:64, c, 1, :], Hb[:, h, :],
                                          start=True, stop=False), reads=rd, writes=[bdep[2]])
                            P.op("pe", mm(banks[2][0:64, h * C:(h + 1) * C], ATs[:, c, h, C:2 * C], UV[:, c, h, :],
                                          start=False, stop=True), reads=[d_ATs[c], d_UV[c]], writes=[bdep[2]])
                        P.op("act", act(Ysb[:, c, :], banks[2][0:64, 0:256], AF.Copy), reads=[bdep[2]],
                             writes=[d_Y[c]])
                        st3 = banks[3][0:64, 0:256].rearrange("p (h v) -> p h v", h=NH)
                        P.op("dve", tt(Hbn[:], tmpH[:], st3, ALU.add), reads=[d_tmpH, bdep[3]], writes=[d_Hbn])
                        P.op("dve", tt(Hs[:], tmpH[:], st3, ALU.add), reads=[d_tmpH, bdep[3]], writes=[d_Hs])
                        dq.run(per)

                    dq.run()
                    eq = Defer(P)
                    emit_out(eq, hg, seg, par)
                    per_e = (len(eq.q) + 3) // 4
                eq.run()
                P.barrier()
                P.flush()

        if 3 in phases:
            with ExitStack() as es:
                sb = lambda name, shape, dt=F32: es.enter_context(nc.sbuf_tensor(name, list(shape), dt))
                idf = sb("idf3", [128, 128]); d_idf = Dep()
                P.dma("sp", dm(idf[:], ident_f), writes=[d_idf])
                ckvT_sb = sb("ckvT_sb", [128, 2, T], BF16); d_ckvT = Dep()
                ckv_sb = sb("ckv_sb", [128, NB, 256], BF16); d_ckv = Dep()
                kiTb = sb("kiTb", [128, T], BF16); d_kiT = Dep()
                stage = sb("stage3", [128, T]); d_stage = Dep()
                wuq_b = sb("wuq_b", [128, 3, 1024], BF16); d_wuq = Dep()
                iwq_b = sb("iwq_b", [128, 3, 1024], BF16); d_iwq = Dep()
                ukT_b = sb("ukT_b", [128, 16, 256], BF16); d_uk = Dep()
                uvp_b = sb("uvp_b", [128, 2, 16, 128], BF16); d_uv = Dep()
                b15 = sb("b15", [128, 2048]); d_b15 = Dep()
                hc = sb("hc", [128, NIT]); d_hc = Dep()
                ones_bf = sb("ones_bf", [128, 128], BF16); d_ones = Dep()
                P.op("dve", mset(ones_bf[:], 1.0), writes=[d_ones])
                for k2 in range(2):
                    P.dma("sp", dm(ckvT_sb[:, k2, :], ckvT[k2 * 128:(k2 + 1) * 128, :]), writes=[d_ckvT])
                P.dma("sp", dm(ckv_sb[:], ckv_tok.rearrange("(kb p) r -> p kb r", p=128)), writes=[d_ckv])
                P.dma("sp", dm(b15[:], b15rep), writes=[d_b15])
                P.dma("sp", dm(hc[:], halfc), writes=[d_hc])
                P.dma("sp", dm(stage[:], kiT2), writes=[d_stage])
                P.op("dve", cp(kiTb[:], stage[:]), reads=[d_stage], writes=[d_kiT])
                for wsrc, wdst, dd in ((w_uq, wuq_b, d_wuq), (iw_q, iwq_b, d_iwq)):
                    P.dma("sp", dm(stage[:, 0:3072].rearrange("p (k n) -> p k n", k=3),
                                   wsrc.rearrange("(k p) n -> p k n", p=128)), writes=[d_stage])
                    P.op("dve", cp(wdst[:], stage[:, 0:3072].rearrange("p (k n) -> p k n", k=3)), reads=[d_stage],
                         writes=[dd])
                P.dma("sp", dm(stage[:, 0:4096], ukT), writes=[d_stage])
                P.op("dve", cp(ukT_b[:], stage[:, 0:4096].rearrange("p (a r) -> p a r", a=16)), reads=[d_stage],
                     writes=[d_uk])
                P.dma("sp", dm(stage[:, 0:4096], uvp), writes=[d_stage])
                P.op("dve", cp(uvp_b[:], stage[:, 0:4096].rearrange("p (c h m) -> p c h m", c=2, h=16)),
                     reads=[d_stage], writes=[d_uv])

                qlb = sb("qlb", [128, 3, 128], BF16); d_qlb = Dep()
                wi = sb("wi", [128, 48]); d_wi = Dep()
                qT_sb = sb("qT_sb", [128, 16, 128], BF16); d_qT = Dep()
                qiT_sb = sb("qiT_sb", [128, 16, 128], BF16); d_qiT = Dep()
                P.op("dve", mset(qT_sb[:], 0.0), writes=[d_qT])
                P.op("dve", mset(qiT_sb[:], 0.0), writes=[d_qiT])
                qaT = sb("qaT", [128, 2, 16, 128], BF16); d_qaT = Dep()
                acc = sb("acc", [128, T]); d_acc = Dep()
                rl = [sb("rl%d" % k, [128, 512]) for k in range(2)]; d_rl = [Dep(), Dep()]
                cj = sb("cj", [128, T], BF16); d_cj = Dep()
                msk = stage; d_msk = d_stage
                maskT = sb("maskT", [128, NB, 128], BF16); d_maskT = Dep()
                bs = sb("bs", [128, 8]); d_bs = Dep()
                hk = sb("hk", [128, NIT]); d_hk = Dep()
                mad = sb("mad", [128, 640]); d_mad = Dep()
                bv = [sb("bv%d" % k, [128, 512]) for k in range(2)]; d_bv = [Dep(), Dep()]
                lg = sb("lg", [128, 512]); d_lg = Dep()
                e_sb = [sb("e_sb%d" % k, [128, 512], BF16) for k in range(2)]; d_e = [Dep(), Dep()]
                pT = [sb("pT%d" % k, [128, 4, 128], BF16) for k in range(2)]; d_pT = [Dep(), Dep()]
                rs = sb("rs", [128, 512]); d_rs = Dep()
                on = sb("on", [128, 2, 4, 128], BF16); d_on = Dep()
                gd = sb("gd", [128, 128]); d_gd = Dep()
                mo = sb("mo", [128, 128]); d_mo = Dep()

                for i in range(OWNB):
                    L = 512 * (i + 1)
                    nkb = 4 * (i + 1)
                    wk0 = max(4 * i - 1, 0)
                    P.dma("sp", dm(qlb[:], qlT[:, i * 128:(i + 1) * 128].rearrange("(k p) t -> p k t", p=128)),
                          writes=[d_qlb])
                    P.dma("sp", dm(wi[:, 0:16], widx[i * 128:(i + 1) * 128, :]), writes=[d_wi])
                    P.op("act", act(wi[:, 32:48], wi[:, 0:16], AF.Sign), reads=[d_wi], writes=[d_wi])
                    P.op("dve", tt(wi[:, 16:32], wi[:, 0:16], wi[:, 32:48], ALU.mult), reads=[d_wi], writes=[d_wi])
                    for W_, dW, dst, ddst in ((wuq_b, d_wuq, qT_sb, d_qT), (iwq_b, d_iwq, qiT_sb, d_qiT)):
                        for hb in range(4):
                            b_ = hb % 2
                            for hl in range(4):
                                h = hb * 4 + hl
                                for kc in range(3):
                                    P.op("pe", mm(banks[b_][0:64, hl * 128:(hl + 1) * 128], W_[:, kc, h * 64:(h + 1) * 64],
                                                  qlb[:, kc, :], start=(kc == 0), stop=(kc == 2)),
                                         reads=[dW, d_qlb], writes=[bdep[b_]])
                            P.op("act", act(dst[0:64, hb * 4:(hb + 1) * 4, :],
                                            banks[b_][0:64, :].rearrange("p (k t) -> p k t", k=4), AF.Copy),
                                 reads=[bdep[b_]], writes=[ddst])
                    if K3STOP < 1:
                        continue
                    for rc in range(2):
                        for hb in range(4):
                            b_ = 5 + (hb % 2)
                            for hl in range(4):
                                h = hb * 4 + hl
                                P.op("pe", mm(banks[b_][:, hl * 128:(hl + 1) * 128],
                                              ukT_b[:, h, rc * 128:(rc + 1) * 128], qT_sb[:, h, :]),
                                     reads=[d_uk, d_qT], writes=[bdep[b_]])
                            P.op("dve", ts(qaT[:, rc, hb * 4:(hb + 1) * 4, :],
                                           banks[b_][:].rearrange("p (k t) -> p k t", k=4), 0.125),
                                 reads=[bdep[b_]], writes=[d_qaT])
                    if K3STOP < 2:
                        continue
                    for st_ in range(i + 1):
                        for h in range(16):
                            b_ = h % 2
                            P.op("pe", mm(banks[b_][:], qiT_sb[:, h, :], kiTb[:, st_ * 512:(st_ + 1) * 512]),
                                 reads=[d_qiT, d_kiT], writes=[bdep[b_]])
                            P.op("act", act(rl[b_][:], banks[b_][:], AF.Relu, scale=wi[:, 16 + h:17 + h]),
                                 reads=[bdep[b_], d_wi], writes=[d_rl[b_]])
                            a_ = acc[:, st_ * 512:(st_ + 1) * 512]
                            if h == 0:
                                P.op("dve", ts(a_, rl[b_][:], wi[:, 32:33]), reads=[d_rl[b_], d_wi], writes=[d_acc])
                            else:
                                P.op("dve", stt(a_, rl[b_][:], wi[:, 32 + h:33 + h], a_, ALU.mult, ALU.add),
                                     reads=[d_rl[b_], d_wi, d_acc], writes=[d_acc])
                    if K3STOP < 3:
                        continue
                    P.op("dve", lambda e, L=L: e.tensor_reduce(out=bs[:, 0:1], in_=acc[:, 0:L], axis=AX.X, op=ALU.max),
                         reads=[d_acc], writes=[d_bs])
                    P.op("dve", lambda e, L=L: e.tensor_reduce(out=bs[:, 1:2], in_=acc[:, 0:L], axis=AX.X, op=ALU.min),
                         reads=[d_acc], writes=[d_bs])
                    P.op("dve", ts(bs[:, 3:4], bs[:, 1:2], -1.0, None, ALU.add), reads=[d_bs], writes=[d_bs])
                    P.op("dve", tt(bs[:, 2:3], bs[:, 0:1], bs[:, 1:2], ALU.subtract), reads=[d_bs], writes=[d_bs])
                    P.op("dve", ts(bs[:, 2:3], bs[:, 2:3], 2.0, None, ALU.add), reads=[d_bs], writes=[d_bs])
                    P.op("dve", ts(hk[:], hc[:], bs[:, 2:3]), reads=[d_hc, d_bs], writes=[d_hk])
                    P.dma("sp", dm(mad[:], madd[i * 128:(i + 1) * 128, :]), writes=[d_mad])
                    wcol0 = 128 if i == 0 else 0
                    P.op("dve", tt(acc[:, wk0 * 128:L], acc[:, wk0 * 128:L], mad[:, wcol0:640], ALU.add),
                         reads=[d_acc, d_mad], writes=[d_acc])
                    P.op("dve", tt(bs[:, 4:5], bs[:, 3:4], hk[:, 0:1], ALU.add), reads=[d_bs, d_hk], writes=[d_bs])
                    for it in range(NIT):
                        P.op("dve", tsa(cj[:, 0:L], acc[:, 0:L], bs[:, 4:5], 0.0, ALU.is_ge, ALU.add, bs[:, 5:6]),
                             reads=[d_acc, d_bs], writes=[d_cj, d_bs])
                        P.op("dve", ts(bs[:, 6:7], bs[:, 5:6], KSEL - 0.5, hk[:, it:it + 1], ALU.is_ge, ALU.mult),
                             reads=[d_bs, d_hk], writes=[d_bs])
                        if it < NIT - 1:
                            P.op("dve", stt(bs[:, 4:5], bs[:, 6:7], hk[:, it + 1:it + 2], bs[:, 4:5], ALU.subtract, ALU.add),
                                 reads=[d_bs, d_hk], writes=[d_bs])
                        else:
                            P.op("dve", stt(bs[:, 3:4], bs[:, 6:7], hk[:, it:it + 1], bs[:, 4:5], ALU.subtract, ALU.add),
                                 reads=[d_bs, d_hk], writes=[d_bs])
                    P.op("dve", ts(msk[:, 0:L], acc[:, 0:L], bs[:, 3:4], None, ALU.is_ge), reads=[d_acc, d_bs],
                         writes=[d_msk])
                    if K3STOP < 4:
                        continue
                    for kg in range(nkb // 4):
                        for k4 in range(4):
                            kb = kg * 4 + k4
                            P.op("pe", tr(banks[4][:, k4 * 128:(k4 + 1) * 128], msk[:, kb * 128:(kb + 1) * 128], idf[:]),
                                 reads=[d_msk, d_idf], writes=[bdep[4]])
                        P.op("act", act(maskT[:, kg * 4:(kg + 1) * 4, :],
                                        banks[4][:].rearrange("p (k t) -> p k t", k=4), AF.Copy),
                             reads=[bdep[4]], writes=[d_maskT])
                    if K3STOP < 5:
                        continue
                    for hq in range(4):
                        for kb in range(nkb):
                            b_ = kb % 2
                            for rc in range(2):
                                P.op("pe", mm(banks[b_][:], ckvT_sb[:, rc, kb * 128:(kb + 1) * 128],
                                              qaT[:, rc, hq * 4:(hq + 1) * 4, :].rearrange("p h t -> p (h t)"),
                                              start=(rc == 0), stop=(rc == 1)), reads=[d_ckvT, d_qaT],
                                     writes=[bdep[b_]])
                            if kb >= wk0:
                                w_ = kb - (4 * i - 1)
                                r0 = (i * 5 + w_) * 128
                                P.dma("sp", dm(bv[b_][:], biasvar[r0:r0 + 128, hq * 512:(hq + 1) * 512]),
                                      writes=[d_bv[b_]])
                                P.op("pool", tt(bv[b_][:], bv[b_][:], b15[:, hq * 512:(hq + 1) * 512], ALU.subtract),
                                     reads=[d_bv[b_], d_b15], writes=[d_bv[b_]])
                                P.op("dve", tt(lg[:], banks[b_][:], bv[b_][:], ALU.add), reads=[bdep[b_], d_bv[b_]],
                                     writes=[d_lg])
                                P.op("act", act(e_sb[b_][:], lg[:], AF.Exp), reads=[d_lg], writes=[d_e[b_]])
                            else:
                                P.op("act", act(e_sb[b_][:], banks[b_][:], AF.Exp), reads=[bdep[b_]], writes=[d_e[b_]])
                            P.op("dve", tt(pT[b_][:], e_sb[b_][:].rearrange("p (h t) -> p h t", h=4),
                                           maskT[:, kb, :].unsqueeze(1).to_broadcast([128, 4, 128]), ALU.mult),
                                 reads=[d_e[b_], d_maskT], writes=[d_pT[b_]])
                            pflat = pT[b_][:].rearrange("p h t -> p (h t)")
                            for rc in range(2):
                                P.op("pe", mm(banks[2 + rc][:], ckv_sb[:, kb, rc * 128:(rc + 1) * 128], pflat,
                                              start=(kb == 0), stop=(kb == nkb - 1)), reads=[d_ckv, d_pT[b_]],
                                     writes=[bdep[2 + rc]])
                            P.op("pe", mm(banks[5][:], ones_bf[:], pflat, start=(kb == 0), stop=(kb == nkb - 1)),
                                 reads=[d_ones, d_pT[b_]], writes=[bdep[5]])
                        P.op("dve", rcp(rs[:], banks[5][:]), reads=[bdep[5]], writes=[d_rs])
                        for rc in range(2):
                            P.op("dve", tt(on[:, rc, :, :].rearrange("p h t -> p (h t)"), banks[2 + rc][:], rs[:],
                                           ALU.mult), reads=[bdep[2 + rc], d_rs], writes=[d_on])
                        for pr in range(2):
                            for hl2 in range(2):
                                hl = pr * 2 + hl2
                                h = hq * 4 + hl
                                for rc in range(2):
                                    P.op("pe", mm(banks[6][:, pr * 128:(pr + 1) * 128], uvp_b[:, rc, h, :],
                                                  on[:, rc, hl, :], start=(hl2 == 0 and rc == 0),
                                                  stop=(hl2 == 1 and rc == 1)), reads=[d_uv, d_on], writes=[bdep[6]])
                        for pr in range(2):
                            fch = hq * 2 + pr
                            P.dma("sp", dm(gd[:], gdsT[fch * 128:(fch + 1) * 128, i * 128:(i + 1) * 128]), writes=[d_gd])
                            P.op("act", act(gd[:], gd[:], AF.Silu), reads=[d_gd], writes=[d_gd])
                            P.op("dve", tt(mo[:], banks[6][:, pr * 128:(pr + 1) * 128], gd[:], ALU.mult),
                                 reads=[bdep[6], d_gd], writes=[d_mo])
                            P.dma("sp", dm(mixT[1024 + fch * 128:1024 + (fch + 1) * 128, i * 128:(i + 1) * 128], mo[:]),
                                  reads=[d_mo])
                P.barrier()
                P.flush()

        if (2 not in phases or 3 not in phases) and not mix_ext:
            with ExitStack() as es:
                zt = es.enter_context(nc.sbuf_tensor("zt", [128, TO], F32)); d_zt = Dep()
                P.op("dve", mset(zt[:], 0.0), writes=[d_zt])
                ks = ([] if 2 in phases else list(range(8))) + ([] if 3 in phases else list(range(8, KC)))
                for k in ks:
                    P.dma("sp", dm(mixT[k * 128:(k + 1) * 128, :], zt[:]), reads=[d_zt])
                P.barrier()
                P.flush()

        if 4 in phases:
            with ExitStack() as es:
                sb = lambda name, shape, dt=F32: es.enter_context(nc.sbuf_tensor(name, list(shape), dt))
                idf = sb("idf4", [128, 128]); d_idf = Dep()
                P.dma("sp", dm(idf[:], ident_f), writes=[d_idf])
                Wf = sb("Wf", [128, KC, D], BF16); d_Wf = Dep()
                Pw = sb("Pw", [128, 2, D], BF16); d_Pw = Dep()
                stg = [sb("stg4_%d" % i, [128, D]) for i in range(2)]
                d_stg = [Dep(), Dep()]
                h_all = sb("h_all", [128, OWNB, D]); d_h = [Dep() for _ in range(OWNB)]
                mx = sb("mx", [128, KC, 128]); d_mx = Dep()
                mxb = sb("mxb", [128, KC, 128], BF16); d_mxb = Dep()
                xt = sb("xt4", [128, D]); d_xt = Dep()
                fr = sb("fr", [128, D]); d_fr = Dep()
                sg = sb("sg", [128, 1024]); d_sg = Dep()
                pt_ = sb("pt4", [128, 256]); d_pt = Dep()
                pTb = sb("pTb", [128, 2, 128], BF16); d_pTb = Dep()
                st = sb("st4", [128, 4]); d_st = Dep()
                junk = sb("junk4", [128, D], BF16); d_junk = Dep()
                P.dma("sp", dm(fr[:], fin_row), writes=[d_fr])

                def load_w(wsrc, nk, dst, d_dst):
                    for kc in range(nk):
                        s_ = kc % 2
                        P.dma("sp", dm(stg[s_][:], wsrc[kc * 128:(kc + 1) * 128, :]), writes=[d_stg[s_]])
                        eng = "dve" if kc % 2 == 0 else "pool"
                        P.op(eng, cp(dst[:, kc, :], stg[s_][:]), reads=[d_stg[s_]], writes=[d_dst])

                load_w(w_out, KC, Wf, d_Wf)
                for i in range(OWNB):
                    P.dma("sp", dm(mx[:], mixT[:, i * 128:(i + 1) * 128].rearrange("(k p) t -> p k t", p=128)),
                          writes=[d_mx])
                    P.op("pool", cp(mxb[:], mx[:]), reads=[d_mx], writes=[d_mxb])
                    P.dma("sp", dm(xt[:], x_own[i * 128:(i + 1) * 128, :]), writes=[d_xt])
                    for n4 in range(4):
                        for kc in range(KC):
                            P.op("pe", mm(banks[n4][:], mxb[:, kc, :], Wf[:, kc, n4 * 512:(n4 + 1) * 512],
                                          start=(kc == 0), stop=(kc == KC - 1)), reads=[d_mxb, d_Wf],
                                 writes=[bdep[n4]])
                        P.op("dve", tt(h_all[:, i, n4 * 512:(n4 + 1) * 512], banks[n4][:],
                                       xt[:, n4 * 512:(n4 + 1) * 512], ALU.add), reads=[bdep[n4], d_xt],
                             writes=[d_h[i]])
                load_w(w_gate, KC, Wf, d_Wf)
                load_w(w_ple, 2, Pw, d_Pw)
                for i in range(OWNB):
                    for g in range(4):
                        for q in range(4):
                            kc = g * 4 + q
                            P.op("pe", tr(banks[4][:, q * 128:(q + 1) * 128], h_all[:, i, kc * 128:(kc + 1) * 128],
                                          idf[:]), reads=[d_h[i], d_idf], writes=[bdep[4]])
                        P.op("act", act(mxb[:, g * 4:(g + 1) * 4, :], banks[4][:].rearrange("p (k t) -> p k t", k=4),
                                        AF.Copy), reads=[bdep[4]], writes=[d_mxb])
                    P.dma("sp", dm(pt_[:], p_own[i * 128:(i + 1) * 128, :]), writes=[d_pt])
                    for k2 in range(2):
                        P.op("pe", tr(banks[4][:, k2 * 128:(k2 + 1) * 128], pt_[:, k2 * 128:(k2 + 1) * 128], idf[:]),
                             reads=[d_pt, d_idf], writes=[bdep[4]])
                    P.op("act", act(pTb[:], banks[4][:, 0:256].rearrange("p (k t) -> p k t", k=2), AF.Copy),
                         reads=[bdep[4]], writes=[d_pTb])
                    for half in range(2):
                        for n2 in range(2):
                            n0 = half * 1024 + n2 * 512
                            for kc in range(KC):
                                P.op("pe", mm(banks[n2][:], mxb[:, kc, :], Wf[:, kc, n0:n0 + 512],
                                              start=(kc == 0), stop=(kc == KC - 1)), reads=[d_mxb, d_Wf],
                                     writes=[bdep[n2]])
                            P.op("act", act(sg[:, n2 * 512:(n2 + 1) * 512], banks[n2][:], AF.Sigmoid),
                                 reads=[bdep[n2]], writes=[d_sg])
                            for k2 in range(2):
                                P.op("pe", mm(banks[2 + n2][:], pTb[:, k2, :], Pw[:, k2, n0:n0 + 512],
                                              start=(k2 == 0), stop=(k2 == 1)), reads=[d_pTb, d_Pw],
                                     writes=[bdep[2 + n2]])
                            P.op("dve", tt(sg[:, n2 * 512:(n2 + 1) * 512], banks[2 + n2][:],
                                           sg[:, n2 * 512:(n2 + 1) * 512], ALU.mult), reads=[bdep[2 + n2], d_sg],
                                 writes=[d_sg])
                            P.op("pool", tt(h_all[:, i, n0:n0 + 512], h_all[:, i, n0:n0 + 512],
                                            sg[:, n2 * 512:(n2 + 1) * 512], ALU.add), reads=[d_sg, d_h[i]],
                                 writes=[d_h[i]])
                    P.op("act", act(junk[:], h_all[:, i, :], AF.Square, accum=st[:, 0:1]), reads=[d_h[i]],
                         writes=[d_junk, d_st])
                    P.op("act", act(st[:, 1:2], st[:, 0:1], AF.Sqrt, bias=EPS, scale=1.0 / D), reads=[d_st],
                         writes=[d_st])
                    P.op("dve", rcp(st[:, 2:3], st[:, 1:2]), reads=[d_st], writes=[d_st])
                    P.op("dve", stt(h_all[:, i, :], h_all[:, i, :], st[:, 2:3], fr[:], ALU.mult, ALU.mult),
                         reads=[d_h[i], d_st, d_fr], writes=[d_h[i]])
                    P.dma("sp", dm(out[i * 128:(i + 1) * 128, :], h_all[:, i, :]), reads=[d_h[i]])
                P.barrier()
                P.flush()

        P.barrier()
        P.flush()
    return nc


def _bucket_table():
    rel = np.arange(-4095, 4096)
    n = np.abs(rel)
    nf = np.maximum(n, 1).astype(np.float32)
    large = 8 + (np.log(nf / np.float32(8.0)) / np.float32(math.log(16.0)) * np.float32(8.0)).astype(np.int32)
    large = np.minimum(large, 15)
    return np.where(rel > 0, 16, 0) + np.where(n < 8, n, large)


def host_prep(inp, c):
    b, j = c // 4, c % 4
    f32 = np.float32
    w_in = inp["w_in"][0]
    heads = [4 * j + i for i in range(4)]
    hc = lambda base: np.concatenate([np.arange(base + h * 64, base + (h + 1) * 64) for h in heads])
    gcols = lambda g: np.concatenate([np.arange(base + h * 64, base + (h + 1) * 64)
                                      for base in (0, 1024, 2048, 3200) for h in range(4 * g, 4 * g + 4)])
    ta_cols = np.arange(4608, 4928)
    to_cols = np.concatenate([np.arange(4224, 4608), np.arange(4928, 4944)])
    fo_cols = np.arange(4944, 5968)
    toks = np.concatenate([np.arange(bk * 128, (bk + 1) * 128) for bk in own_blocks(j)])
    m = {}
    m["x_all"] = np.ascontiguousarray(inp["x"][b])
    m["x_own"] = np.ascontiguousarray(inp["x"][b][toks])
    m["wAg"] = np.ascontiguousarray(np.concatenate([w_in[:, gcols(g)] for g in range(4)], 0))
    m["wAx"] = np.ascontiguousarray(w_in[:, np.concatenate([np.arange(3072, 3200), ta_cols])])
    m["wO"] = np.ascontiguousarray(w_in[:, np.concatenate([to_cols, fo_cols])])
    m["g_col"] = np.ascontiguousarray(inp["norm_g"][0].reshape(KC, 128).T)
    m["ident_f"] = np.eye(128, dtype=f32)
    row = np.concatenate([inp["ds_kv_norm_g"][0], inp["idx_k_norm_g"][0], inp["ds_q_norm_g"][0]])
    m["rowc"] = np.ascontiguousarray(np.broadcast_to(row[None, :], (128, 704))).astype(f32)
    mu = inp["rw_mu"][0]
    ppm = np.zeros((4, 128, 24), f32)
    lwm = np.zeros((4, 128, 256), f32)
    w0m = np.zeros((4, 256), f32)
    lnm = np.zeros((4, 64, 512), f32)
    for g in range(4):
        own = np.arange(g * 256, (g + 1) * 256)
        gc_ = gcols(g)
        for ch in range(6):
            ppm[g, :, ch] = mu[gc_[ch * 128:(ch + 1) * 128]]
        ppm[g, :, 8] = mu[3072:3200]
        for p_ in range(2):
            oc = own[p_ * 128:(p_ + 1) * 128]
            ppm[g, :, 11 + p_] = inp["rw_a0"][0][oc]
            ppm[g, :, 13 + p_] = inp["rw_k_k"][0][oc]
            ppm[g, :, 15 + p_] = inp["rw_k_a"][0][oc]
            ppm[g, :, 17 + p_] = inp["rw_r_k"][0].reshape(-1)[oc]
        lwm[g, 0:64] = inp["rw_w_up"][0][:, own]
        lwm[g, 64:128] = inp["rw_a_up"][0][:, own]
        w0m[g] = inp["rw_w0"][0][own]
        lnm[g] = np.concatenate([inp["rw_ln_g"][0][own], inp["rw_ln_b"][0][own]])[None, :]
    m["pp"] = ppm.reshape(4 * 128, 24)
    m["lora_w"] = lwm.reshape(4 * 128, 256)
    m["w0_row"] = w0m
    m["lnrow"] = lnm.reshape(4 * 64, 512)
    s4 = np.zeros((128, 4), f32); s4[:, j] = 1.0
    m["sel4"] = s4
    ii = np.arange(128)
    same = (ii[:, None] // 64) == (ii[None, :] // 64)
    cdec = -math.exp(-0.5)
    m["tri"] = np.concatenate([(same & (ii[:, None] <= ii[None, :])), (same & (ii[:, None] < ii[None, :]))],
                              1).astype(f32) * f32(cdec)
    m["ones_blk"] = same.astype(f32)
    i6 = np.arange(64)
    lt = (i6[:, None] < i6[None, :]).astype(f32)
    le = (i6[:, None] <= i6[None, :]).astype(f32)
    mat = np.block([[lt, le], [lt, le]])
    m["maskAT4"] = np.ascontiguousarray(np.tile(mat, (1, 4)))
    m["maskNN"] = np.ascontiguousarray(np.concatenate([np.tile(lt.T, (1, 4)), np.tile(lt, (1, 4))], 1))
    m["id4"] = np.ascontiguousarray(np.tile(np.eye(64, dtype=f32), (1, 8)))
    m["w_uq"] = np.ascontiguousarray(inp["ds_w_uq"][0])
    m["iw_q"] = np.ascontiguousarray(inp["idx_w_q"][0])
    wuk = inp["ds_w_uk"][0]
    ukm = np.zeros((128, 16, 256), f32)
    ukm[0:64] = wuk.transpose(2, 1, 0)
    m["ukT"] = ukm.reshape(128, 16 * 256)
    wuv = inp["ds_w_uv"][0]
    uvpm = np.zeros((128, 2, 16, 128), f32)
    for h in range(16):
        q_ = h % 2
        uvpm[:, :, h, q_ * 64:(q_ + 1) * 64] = wuv[:, h, :].reshape(2, 128, 64).transpose(1, 0, 2)
    m["uvp"] = uvpm.reshape(128, 4096)
    bt = _bucket_table()
    maddm = np.full((OWNB, 128, 640), -1e30, f32)
    biasm = np.zeros((OWNB, 5, 128, 16, 128), f32)
    rb = inp["rel_bias"]
    for i in range(OWNB):
        blk = 4 * i + j
        tpos = blk * 128 + np.arange(128)
        limit = (tpos // 64 + 1) * 64
        for w_ in range(5):
            kb = 4 * i - 1 + w_
            if kb < 0:
                continue
            spos = kb * 128 + np.arange(128)
            adm = spos[None, :] < limit[:, None]
            maddm[i, :, w_ * 128:(w_ + 1) * 128] = np.where(adm, 0.0, -1e30)
            rel = spos[:, None] - tpos[None, :]
            biasm[i, w_] = rb[bt[rel + 4095]].transpose(0, 2, 1)
    m["madd"] = maddm.reshape(OWNB * 128, 640)
    m["biasvar"] = biasm.reshape(OWNB * 5 * 128, 2048)
    m["b15rep"] = np.ascontiguousarray(np.broadcast_to(np.repeat(rb[15], 128)[None, :], (128, 2048))).astype(f32)
    m["halfc"] = np.ascontiguousarray(np.broadcast_to((0.5 ** np.arange(1, NIT + 1))[None, :], (128, NIT))).astype(f32)
    m["p_own"] = np.ascontiguousarray(inp["p"][0, b][toks])
    m["w_out"] = np.ascontiguousarray(inp["w_out"][0])
    m["w_gate"] = np.ascontiguousarray(inp["ple_gate_w"][0])
    m["w_ple"] = np.ascontiguousarray(inp["ple_w"][0])
    m["fin_row"] = np.ascontiguousarray(np.broadcast_to(inp["final_g"][None, :], (128, D))).astype(f32)
    return m


def kernel(**inputs):
    inp = {k: np.asarray(v) for k, v in inputs.items()}
    nc = build_nc()
    in_maps = [host_prep(inp, c) for c in range(NCORES)]
    res = run_bass_kernel_spmd(nc, in_maps, core_ids=list(range(NCORES)))
    out = np.zeros((2, T, D), np.float32)
    for c in range(NCORES):
        b, j = c // 4, c % 4
        o = res.results[c]["out"]
        for i, bk in enumerate(own_blocks(j)):
            out[b, bk * 128:(bk + 1) * 128] = o[i * 128:(i + 1) * 128]
    return out
```

```python
import math, os
SKIP = set(os.environ.get('KSKIP', '').split(','))
K3STOP = int(os.environ.get('K3STOP', '9'))
from contextlib import ExitStack
import numpy as np
import ml_dtypes
import concourse.bass as bass
import concourse.mybir as mybir
from concourse.bass_utils import run_bass_kernel_spmd

F32 = mybir.dt.float32
BF16 = mybir.dt.bfloat16
ALU = mybir.AluOpType
AF = mybir.ActivationFunctionType
AX = mybir.AxisListType

NCORES = 8
T = 4096
D = 2048
KC = 16
NB = 32
OWNB = 8
TO = 1024
EPS = 1e-6
GN_EPS = 64e-5
NFA = 1152
NZ = 4224
NTA = 320
NTO = 400
NFO = 1024
SEG = 8
SEGT = SEG * 64
NIT = 14
KSEL = 256


class Dep:
    __slots__ = ("w", "r")

    def __init__(self):
        self.w = None
        self.r = []


class Prog:
    ENGS = ("pe", "act", "dve", "pool", "sp")

    def __init__(self, nc, es, ndsem=24):
        self.nc = nc
        self.q = {e: [] for e in self.ENGS}
        self.sem = {e: es.enter_context(nc.semaphore("s_" + e)) for e in self.ENGS}
        self.cnt = {e: 0 for e in self.ENGS}
        self.real = {e: 0 for e in self.ENGS}
        self.known = {e: {} for e in self.ENGS}
        self.dsem = [es.enter_context(nc.semaphore("d%d" % i)) for i in range(ndsem)]
        self.dcnt = [0] * ndsem
        self.dnext = 0

    def _waits(self, eng, deps):
        need = {}
        kn = self.known[eng]
        for ev in deps:
            if ev is None:
                continue
            s, v = ev
            k = id(s)
            if kn.get(k, 0) >= v:
                continue
            if k not in need or need[k][1] < v:
                need[k] = (s, v)
        for k, (s, v) in need.items():
            kn[k] = v
        return list(need.values())

    def _deps(self, eng, reads, writes):
        deps = []
        for t in reads:
            deps.append(t.w)
        for t in writes:
            deps.append(t.w)
            deps.extend(t.r)
        if eng == "pe":
            ps = self.sem["pe"]
            deps = [d for d in deps if d is not None and d[0] is not ps]
        return deps

    def _post(self, ev, reads, writes):
        for t in reads:
            t.r.append(ev)
            if len(t.r) > 48:
                best = {}
                for (s, v) in t.r:
                    if id(s) not in best or best[id(s)][1] < v:
                        best[id(s)] = (s, v)
                t.r = list(best.values())
        for t in writes:
            t.w = ev
            t.r = []

    def op(self, eng, fn, reads=(), writes=()):
        waits = self._waits(eng, self._deps(eng, reads, writes))
        self.cnt[eng] += 1
        ev = (self.sem[eng], self.cnt[eng])
        self.q[eng].append((waits, fn, ev, 1))
        self._post(ev, reads, writes)
        return ev

    def dma(self, eng, fn, reads=(), writes=()):
        i = self.dnext
        self.dnext = (i + 1) % len(self.dsem)
        deps = self._deps(eng, reads, writes)
        if self.dcnt[i] > 0:
            deps.append((self.dsem[i], 16 * self.dcnt[i]))
        waits = self._waits(eng, deps)
        self.dcnt[i] += 1
        ev = (self.dsem[i], 16 * self.dcnt[i])
        self.q[eng].append((waits, fn, ev, 16))
        self._post(ev, reads, writes)
        return ev

    def barrier(self):
        evs = [(self.sem[e], self.cnt[e]) for e in self.ENGS if self.cnt[e] > 0]
        evs += [(self.dsem[i], 16 * self.dcnt[i]) for i in range(len(self.dsem)) if self.dcnt[i] > 0]
        for e in self.ENGS:
            waits = self._waits(e, evs)
            if waits:
                self.q[e].append((waits, None, None, 0))

    def flush(self):
        nc = self.nc
        sem2eng = {id(self.sem[e]): e for e in self.ENGS}
        needed = {e: set() for e in self.ENGS}
        for e in self.ENGS:
            for waits, fn, ev, inc in self.q[e]:
                for s_, v in waits:
                    if id(s_) in sem2eng:
                        needed[sem2eng[id(s_)]].add(v)
        newval = {e: {} for e in self.ENGS}
        for e in self.ENGS:
            c = self.real[e]
            for waits, fn, ev, inc in self.q[e]:
                if fn is None or inc != 1:
                    continue
                if ev[1] in needed[e]:
                    c += 1
                    newval[e][ev[1]] = c
            self.real[e] = c

        def mk(eng):
            items = self.q[eng]

            def body(e):
                for waits, fn, ev, inc in items:
                    for s_, v in waits:
                        if id(s_) in sem2eng:
                            v = newval[sem2eng[id(s_)]][v]
                        e.wait_ge(s_, v)
                    if fn is not None:
                        ins = fn(e)
                        if inc != 1:
                            ins.then_inc(ev[0], inc)
                        elif ev[1] in newval[eng]:
                            ins.then_inc(ev[0], 1)
            return body

        with nc.Block() as block:
            block.tensor(mk("pe"))
            block.scalar(mk("act"))
            block.vector(mk("dve"))
            block.gpsimd(mk("pool"))
            block.sync(mk("sp"))
        self.q = {e: [] for e in self.ENGS}


class Defer:
    def __init__(self, P):
        self.P = P
        self.q = []

    def op(self, *a, **k):
        self.q.append(("op", a, k))

    def dma(self, *a, **k):
        self.q.append(("dma", a, k))

    def run(self, n=None):
        n = len(self.q) if n is None else min(n, len(self.q))
        for _ in range(n):
            kind, a, k = self.q.pop(0)
            getattr(self.P, kind)(*a, **k)


def ts(out, in0, s1, s2=None, op0=ALU.mult, op1=None):
    if op1 is None:
        return lambda e: e.tensor_scalar(out=out, in0=in0, scalar1=s1, scalar2=None, op0=op0)
    return lambda e: e.tensor_scalar(out=out, in0=in0, scalar1=s1, scalar2=s2, op0=op0, op1=op1)


def tsa(out, in0, s1, s2, op0, op1, accum):
    return lambda e: e.tensor_scalar(out=out, in0=in0, scalar1=s1, scalar2=s2, op0=op0, op1=op1, accum_out=accum)


def tt(out, a, b, op):
    return lambda e: e.tensor_tensor(out=out, in0=a, in1=b, op=op)


def stt(out, in0, s, in1, op0, op1):
    return lambda e: e.scalar_tensor_tensor(out=out, in0=in0, scalar=s, in1=in1, op0=op0, op1=op1)


def act(out, in_, f, bias=None, scale=None, accum=None):
    kw = {}
    if bias is not None:
        kw["bias"] = bias
    if scale is not None:
        kw["scale"] = scale
    if accum is not None:
        kw["accum_out"] = accum
    return lambda e: e.activation(out=out, in_=in_, func=f, **kw)


def mm(out, lhsT, rhs, start=True, stop=True):
    return lambda e: e.matmul(out=out, lhsT=lhsT, rhs=rhs, start=start, stop=stop)


def tr(out, in_, ident):
    return lambda e: e.transpose(out=out, in_=in_, identity=ident)


def cp(out, in_):
    return lambda e: e.tensor_copy(out=out, in_=in_)


def dm(out, in_):
    return lambda e: e.dma_start(out=out, in_=in_)


def rcp(out, in_):
    return lambda e: e.reciprocal(out=out, in_=in_)


def mset(ap, v):
    return lambda e: e.memset(ap, v)


def own_blocks(j):
    return [4 * i + j for i in range(8)]


def build_nc(dbg=False, phases=(1, 2, 3, 4), lim=(8, 2), mix_ext=False, zfa_ext=False, p1_ext=False):
    nc = bass.Bass("TRN2", target_bir_lowering=False)
    ext = lambda name, shape, dt=F32: nc.dram_tensor(name, list(shape), dt, kind="ExternalInput").ap()
    scr_kind = "ExternalOutput" if dbg else "Internal"
    scr = lambda name, shape, dt=F32: nc.dram_tensor(name, list(shape), dt, kind=scr_kind).ap()

    x_own = ext("x_own", [TO, D])
    ident_f = ext("ident_f", [128, 128])
    if 1 in phases:
        x_all = ext("x_all", [T, D])
        wAg = ext("wAg", [4 * D, 1024])
        wAx = ext("wAx", [D, 448])
        wO = ext("wO", [D, NTO + NFO])
        g_col = ext("g_col", [128, KC])
        rowc = ext("rowc", [128, 256 + 64 + 384])

    zfa = (ext if zfa_ext else scr)("zfa", [NZ, T])
    xnc = nc.dram_tensor("xnc", [8 * 128, KC * 512], BF16, kind="Internal").ap()
    d_xnc = [Dep() for _ in range(8)]
    scr1 = ext if p1_ext else scr
    ckv_tok = scr1("ckv_tok", [T, 256], BF16)
    ckvT = scr1("ckvT", [256, T], BF16)
    kiT2 = scr1("kiT2", [128, T])
    qlT = scr1("qlT", [384, TO], BF16)
    widx = scr1("widx", [TO, 16])
    gdsT = scr1("gdsT", [NFO, TO])
    if 3 in phases:
        w_uq = ext("w_uq", [384, 1024])
        iw_q = ext("iw_q", [384, 1024])
        ukT = ext("ukT", [128, 16 * 256])
        uvp = ext("uvp", [128, 2 * 16 * 128])
        madd = ext("madd", [OWNB * 128, 640])
        biasvar = ext("biasvar", [OWNB * 5 * 128, 2048])
        b15rep = ext("b15rep", [128, 2048])
        halfc = ext("halfc", [128, NIT])
    orw = scr("orw", [1024, T])
    if 2 in phases:
        pp = ext("pp", [4 * 128, 24])
        lora_w = ext("lora_w", [4 * 128, 256])
        w0_row = ext("w0_row", [4, 256])
        sel4 = ext("sel4", [128, 4])
        tri = ext("tri", [128, 256])
        ones_blk = ext("ones_blk", [128, 128])
        maskAT4 = ext("maskAT4", [128, 512])
        maskNN = ext("maskNN", [64, 512])
        id4 = ext("id4", [64, 512])
        lnrow = ext("lnrow", [4 * 64, 512])
    mixT = (ext if mix_ext else scr)("mixT", [D, TO])
    if 4 in phases:
        p_own = ext("p_own", [TO, 256])
        w_out = ext("w_out", [D, D])
        w_gate = ext("w_gate", [D, D])
        w_ple = ext("w_ple", [256, D])
        fin_row = ext("fin_row", [128, D])
        out = nc.dram_tensor("out", [TO, D], F32, kind="ExternalOutput").ap()

    with ExitStack() as top:
        P = Prog(nc, top)
        banks = [top.enter_context(nc.psum_tensor("bank%d" % i, [128, 512], F32)) for i in range(7)]
        bdep = [Dep() for _ in range(7)]
        pbf = top.enter_context(nc.psum_tensor("pbf", [128, 1024], BF16))
        d_pbf = Dep()
        d_b6b = d_pbf
        pbf32 = pbf.bitcast(F32)

        if 1 in phases:
            STQ = "pool"
            with ExitStack() as es:
                sb = lambda name, shape, dt=F32: es.enter_context(nc.sbuf_tensor(name, list(shape), dt))
                idf = sb("idf", [128, 128]); d_idf = Dep()
                idb = sb("idb", [128, 128], BF16); d_idb = Dep()
                gc = sb("gc", [128, KC]); d_gc = Dep()
                rc = sb("rc", [128, 704]); d_rc = Dep()
                P.dma("sp", dm(idf[:], ident_f), writes=[d_idf])
                P.dma("sp", dm(gc[:], g_col), writes=[d_gc])
                P.dma("sp", dm(rc[:], rowc), writes=[d_rc])
                P.op("dve", cp(idb[:], idf[:]), reads=[d_idf], writes=[d_idb])
                Wb = sb("Wb", [128, KC, NFA + NTA], BF16); d_Wb = Dep()
                stg = [sb("stg%d" % i, [128, NFA + NTA]) for i in range(2)]
                d_stg = [Dep(), Dep()]
                xt = [sb("xt%d" % i, [128, D]) for i in range(2)]
                d_xt = [Dep(), Dep()]
                junk = sb("junk", [128, D], BF16); d_junk = Dep()
                st = sb("st", [128, 8]); d_st = Dep()
                xs = sb("xs", [128, D]); d_xs = Dep()
                xnT = sb("xnT", [128, KC, 512], BF16); d_xnT = Dep()
                xnT_b = sb("xnT_b", [128, KC, 512], BF16); d_xnT_b = Dep()
                xs_b = sb("xs_b", [128, D]); d_xs_b = Dep()
                xs_l = [(xs, d_xs), (xs_b, d_xs_b)]
                X = {"t": xnT, "d": d_xnT}

                def setx(k):
                    X["t"], X["d"] = (xnT, d_xnT) if k % 2 == 0 else (xnT_b, d_xnT_b)

                zst = [sb("zst%d" % i, [128, 512]) for i in range(2)]
                d_zst = [Dep(), Dep()]
                ckv_st = sb("ckv_st", [128, 4, 256], BF16); d_ckv_st = Dep()
                ckvT_st = sb("ckvT_st", [128, 2, 512], BF16); d_ckvT_st = Dep()
                ki2 = sb("ki2", [128, 128]); d_ki2 = Dep()
                kiT_st = sb("kiT_st", [128, 512]); d_kiT_st = Dep()
                qln = sb("qln", [128, 384]); d_qln = Dep()
                ckv_f = sb("ckv_f", [128, 256]); d_ckv_f = Dep()
                qlT_st = sb("qlT_st", [128, 3, 512], BF16); d_qlT_st = Dep()
                wi_st = sb("wi_st", [128, 4, 16]); d_wi_st = Dep()

                def load_weights(wsrc, ncols, col0=0):
                    for kc in range(KC):
                        s = kc % 2
                        P.dma("sp", dm(stg[s][:, 0:ncols], wsrc[kc * 128:(kc + 1) * 128, :]), writes=[d_stg[s]])
                        eng = "dve" if kc % 2 == 0 else "pool"
                        P.op(eng, ts(Wb[:, kc, col0:col0 + ncols], stg[s][:, 0:ncols], gc[:, kc:kc + 1]),
                             reads=[d_stg[s], d_gc], writes=[d_Wb])

                def norm_transpose_group(xsrc, grp):
                    for r4 in range(4):
                        rt = grp * 4 + r4
                        s = rt % 2
                        P.dma("sp", dm(xt[s][:], xsrc[rt * 128:(rt + 1) * 128, :]), writes=[d_xt[s]])
                        P.op("act", act(junk[:], xt[s][:], AF.Square, accum=st[:, 0:1]), reads=[d_xt[s]],
                             writes=[d_junk, d_st])
                        P.op("act", act(st[:, 1:2], st[:, 0:1], AF.Sqrt, bias=EPS, scale=1.0 / D), reads=[d_st],
                             writes=[d_st])
                        P.op("dve", rcp(st[:, 2:3], st[:, 1:2]), reads=[d_st], writes=[d_st])
                        xsc, d_xsc = xs_l[s]
                        P.op("dve", ts(xsc[:], xt[s][:], st[:, 2:3]), reads=[d_xt[s], d_st], writes=[d_xsc])
                        for g in range(4):
                            for q in range(4):
                                kc = g * 4 + q
                                P.op("pe", tr(banks[g][:, q * 128:(q + 1) * 128], xsc[:, kc * 128:(kc + 1) * 128],
                                              idf[:]), reads=[d_xsc, d_idf], writes=[bdep[g]])
                            eng = "act" if g % 2 == 0 else "dve"
                            o = X["t"][:, g * 4:(g + 1) * 4, r4 * 128:(r4 + 1) * 128]
                            i = banks[g][:].rearrange("p (k t) -> p k t", k=4)
                            if eng == "act":
                                P.op("act", act(o, i, AF.Copy), reads=[bdep[g]], writes=[X["d"]])
                            else:
                                P.op("dve", cp(o, i), reads=[bdep[g]], writes=[X["d"]])

                def fm_proj(col0, nchunk, dst, tok0, xb=None, dxb=None):
                    xb = X["t"] if xb is None else xb
                    dxb = X["d"] if dxb is None else dxb
                    for fcn in range(nchunk):
                        b = 4 + (fcn % 2)
                        for kc in range(KC):
                            P.op("pe", mm(banks[b][:], Wb[:, kc, col0 + fcn * 128:col0 + (fcn + 1) * 128],
                                          xb[:, kc, :], start=(kc == 0), stop=(kc == KC - 1)),
                                 reads=[d_Wb, dxb], writes=[bdep[b]])
                        s = fcn % 2
                        if s == 0:
                            P.op("act", act(zst[s][:], banks[b][:], AF.Copy), reads=[bdep[b]], writes=[d_zst[s]])
                        else:
                            P.op("dve", cp(zst[s][:], banks[b][:]), reads=[bdep[b]], writes=[d_zst[s]])
                        P.dma(STQ, dm(dst[fcn * 128:(fcn + 1) * 128, tok0:tok0 + 512], zst[s][:]),
                              reads=[d_zst[s]])

                for hgp in range(4):
                  load_weights(wAg[hgp * D:(hgp + 1) * D], 1024)
                  if hgp == 0:
                      load_weights(wAx, 448, 1024)
                  for grp in range(lim[0]):
                    if hgp == 0:
                        setx(grp)
                        norm_transpose_group(x_all, grp)
                        P.dma(STQ, dm(xnc[grp * 128:(grp + 1) * 128, :], X["t"][:].rearrange("p k t -> p (k t)")),
                              reads=[X["d"]], writes=[d_xnc[grp]])
                    else:
                        xb_, dxb_ = (xnT, d_xnT) if grp % 2 == 0 else (xnT_b, d_xnT_b)
                        P.dma("sp", dm(xb_[:].rearrange("p k t -> p (k t)"), xnc[grp * 128:(grp + 1) * 128, :]),
                              reads=[d_xnc[grp]], writes=[dxb_])
                        fm_proj(0, 8, zfa[hgp * 1024:(hgp + 1) * 1024], grp * 512, xb_, dxb_)
                        continue
                    fm_proj(0, 8, zfa[hgp * 1024:(hgp + 1) * 1024], grp * 512)
                    fm_proj(1024, 1, zfa[4096:4224], grp * 512)
                    for r4 in range(4):
                        for kc in range(KC):
                            P.op("pe", mm(banks[6][:, 0:NTA], X["t"][:, kc, r4 * 128:(r4 + 1) * 128],
                                          Wb[:, kc, NFA:NFA + NTA], start=(kc == 0), stop=(kc == KC - 1)),
                                 reads=[d_Wb, X["d"]], writes=[bdep[6]])
                        pt = banks[6]
                        P.op("act", act(junk[:, 0:256], pt[:, 0:256], AF.Square, accum=st[:, 3:4]),
                             reads=[bdep[6]], writes=[d_junk, d_st])
                        P.op("act", act(junk[:, 0:64], pt[:, 256:320], AF.Square, accum=st[:, 4:5]),
                             reads=[bdep[6]], writes=[d_junk, d_st])
                        P.op("act", act(st[:, 3:4], st[:, 3:4], AF.Sqrt, bias=EPS, scale=1.0 / 256), reads=[d_st],
                             writes=[d_st])
                        P.op("act", act(st[:, 4:5], st[:, 4:5], AF.Sqrt, bias=EPS, scale=1.0 / 64), reads=[d_st],
                             writes=[d_st])
                        P.op("dve", rcp(st[:, 5:7], st[:, 3:5]), reads=[d_st], writes=[d_st])
                        P.op("dve", stt(ckv_f[:], pt[:, 0:256], st[:, 5:6], rc[:, 0:256], ALU.mult, ALU.mult),
                             reads=[bdep[6], d_st, d_rc], writes=[d_ckv_f])
                        P.op("pool", cp(ckv_st[:, r4, :], ckv_f[:]), reads=[d_ckv_f], writes=[d_ckv_st])
                        P.op("dve", stt(ki2[:, 0:64], pt[:, 256:320], st[:, 6:7], rc[:, 256:320], ALU.mult, ALU.mult),
                             reads=[bdep[6], d_st, d_rc], writes=[d_ki2])
                        P.op("dve", stt(ki2[:, 64:128], pt[:, 256:320], st[:, 6:7], rc[:, 256:320], ALU.mult,
                                        ALU.mult), reads=[bdep[6], d_st, d_rc], writes=[d_ki2])
                        for h2 in range(0 if 'tr' in SKIP else 2):
                            P.op("pe", tr(pbf32[:, h2 * 128:(h2 + 1) * 128], ckv_f[:, h2 * 128:(h2 + 1) * 128],
                                          idf[:]), reads=[d_ckv_f, d_idf], writes=[d_pbf])
                        if 'tr' not in SKIP:
                            for k2 in range(2):
                                P.op("act", act(ckvT_st[:, k2, r4 * 128:(r4 + 1) * 128],
                                                pbf32[:, k2 * 128:(k2 + 1) * 128], AF.Copy),
                                     reads=[d_pbf], writes=[d_ckvT_st])
                        if 'ki' not in SKIP:
                            P.op("pe", tr(pbf32[:, 256:384], ki2[:], idf[:]), reads=[d_ki2, d_idf], writes=[d_b6b])
                            P.op("act", act(kiT_st[:, r4 * 128:(r4 + 1) * 128], pbf32[:, 256:384], AF.Copy),
                                 reads=[d_b6b], writes=[d_kiT_st])
                    t0 = grp * 512
                    P.dma(STQ, dm(ckv_tok[t0:t0 + 512, :].rearrange("(r p) c -> p r c", p=128), ckv_st[:]),
                          reads=[d_ckv_st])
                    for k2 in range(0 if 'trd' in SKIP else 2):
                        P.dma(STQ, dm(ckvT[k2 * 128:(k2 + 1) * 128, t0:t0 + 512], ckvT_st[:, k2, :]),
                              reads=[d_ckvT_st])
                    P.dma(STQ, dm(kiT2[:, t0:t0 + 512], kiT_st[:]), reads=[d_kiT_st])

                load_weights(wO, NTO + NFO)
                for grp in range(lim[1]):
                    setx(grp)
                    norm_transpose_group(x_own, grp)
                    fm_proj(NTO, NFO // 128, gdsT, grp * 512)
                    for r4 in range(4):
                        for kc in range(KC):
                            P.op("pe", mm(banks[6][:, 0:NTO], X["t"][:, kc, r4 * 128:(r4 + 1) * 128],
                                          Wb[:, kc, 0:NTO], start=(kc == 0), stop=(kc == KC - 1)),
                                 reads=[d_Wb, X["d"]], writes=[bdep[6]])
                        pt = banks[6]
                        P.op("act", act(junk[:, 0:384], pt[:, 0:384], AF.Square, accum=st[:, 3:4]),
                             reads=[bdep[6]], writes=[d_junk, d_st])
                        P.op("act", act(st[:, 3:4], st[:, 3:4], AF.Sqrt, bias=EPS, scale=1.0 / 384), reads=[d_st],
                             writes=[d_st])
                        P.op("dve", rcp(st[:, 5:6], st[:, 3:4]), reads=[d_st], writes=[d_st])
                        P.op("dve", stt(qln[:], pt[:, 0:384], st[:, 5:6], rc[:, 320:704], ALU.mult, ALU.mult),
                             reads=[bdep[6], d_st, d_rc], writes=[d_qln])
                        P.op("dve", ts(wi_st[:, r4, :], pt[:, 384:400], 1.0 / 32.0), reads=[bdep[6]],
                             writes=[d_wi_st])
                        for h3 in range(3):
                            P.op("pe", tr(pbf32[:, h3 * 128:(h3 + 1) * 128], qln[:, h3 * 128:(h3 + 1) * 128], idf[:]),
                                 reads=[d_qln, d_idf], writes=[d_pbf])
                        P.op("act", act(qlT_st[:, :, r4 * 128:(r4 + 1) * 128],
                                        pbf32[:, 0:384].rearrange("p (k t) -> p k t", k=3), AF.Copy),
                             reads=[d_pbf], writes=[d_qlT_st])
                    t0 = grp * 512
                    for k3 in range(3):
                        P.dma(STQ, dm(qlT[k3 * 128:(k3 + 1) * 128, t0:t0 + 512], qlT_st[:, k3, :]),
                              reads=[d_qlT_st])
                    P.dma(STQ, dm(widx[t0:t0 + 512, :].rearrange("(r p) c -> p r c", p=128), wi_st[:]),
                          reads=[d_wi_st])
                P.barrier()
                P.flush()

        if 2 in phases:
            with ExitStack() as es:
                sb = lambda name, shape, dt=F32: es.enter_context(nc.sbuf_tensor(name, list(shape), dt))
                NH = 4
                C = 64
                cst = {}
                for nm, src, shp in (("idf", ident_f, [128, 128]), ("pp", pp[0:128], [128, 24]),
                                     ("lw", lora_w[0:128], [128, 256]),
                                     ("w0", w0_row[0:1], [1, 256]), ("tri", tri, [128, 256]), ("ob", ones_blk, [128, 128]),
                                     ("mAT", maskAT4, [128, 512]), ("mNN", maskNN, [64, 512]), ("id4", id4, [64, 512]),
                                     ("sel4", sel4, [128, 4])):
                    t_ = sb("c_" + nm, shp)
                    dd = Dep()
                    P.dma("sp", dm(t_[:], src), writes=[dd])
                    cst[nm] = (t_, dd)
                idf, d_idf = cst["idf"]; ppt, d_pp = cst["pp"]; lwt, d_lw = cst["lw"]; w0t, d_w0 = cst["w0"]
                trit, d_tri = cst["tri"]; obt, d_ob = cst["ob"]; mAT, d_mAT = cst["mAT"]; mNN, d_mNN = cst["mNN"]
                i4, d_i4 = cst["id4"]; s4t, d_s4 = cst["sel4"]
                om = sb("om", [128, 24]); d_om = Dep()

                onesr = sb("onesr", [1, 128]); d_onesr = Dep()
                P.op("dve", mset(onesr[:], 1.0), writes=[d_onesr])
                zseg = sb("zseg", [128, 9, SEGT + 1]); d_zseg = Dep()
                zs = sb("zs", [128, 7, SEGT]); d_zs = [Dep() for _ in range(7)]
                t1 = sb("t1", [128, SEGT]); d_t1 = Dep()
                t2 = sb("t2", [128, SEGT]); d_t2 = Dep()
                t3 = sb("t3", [128, SEGT]); d_t3 = Dep()
                Wt2 = [[sb("W%d_%d" % (p_, k_), [128, SEGT]) for p_ in range(2)] for k_ in range(2)]
                d_W2 = [[Dep(), Dep()], [Dep(), Dep()]]
                Wi = [sb("Wi%d" % p_, [128, SEGT]) for p_ in range(2)]; d_Wi = [Dep(), Dep()]
                Wp = [sb("Wp%d" % p_, [128, SEGT]) for p_ in range(2)]; d_Wp = [Dep(), Dep()]
                a_sb = [sb("a%d" % p_, [128, SEGT]) for p_ in range(2)]; d_a = [Dep(), Dep()]
                AR2 = [[sb("AR%d_%d" % (p_, k_), [128, SEG, 2, C], BF16) for p_ in range(2)] for k_ in range(2)]
                d_AR2 = [[Dep(), Dep()], [Dep(), Dep()]]
                BK = [sb("BK%d" % p_, [128, SEG, 2, C], BF16) for p_ in range(2)]; d_BK = [Dep(), Dep()]
                BKh = [sb("BKh%d" % p_, [128, SEG, 2, C]) for p_ in range(2)]; d_BKh = [Dep(), Dep()]
                ARo2 = [[sb("ARo%d_%d" % (p_, k_), [64, SEG, 2, C], BF16) for p_ in range(2)] for k_ in range(2)]
                d_ARo2 = [[Dep(), Dep()], [Dep(), Dep()]]
                BKo = [sb("BKo%d" % p_, [64, SEG, 2, C], BF16) for p_ in range(2)]; d_BKo = [Dep(), Dep()]
                BKho = [sb("BKho%d" % p_, [64, SEG, 2, C]) for p_ in range(2)]; d_BKho = [Dep(), Dep()]
                WCo2 = [[sb("WCo%d_%d" % (p_, k_), [64, SEG]) for p_ in range(2)] for k_ in range(2)]
                d_WCo2 = [[Dep(), Dep()], [Dep(), Dep()]]
                bonT2 = [[sb("bon%d_%d" % (p_, k_), [128, SEGT]) for p_ in range(2)] for k_ in range(2)]
                d_bon2 = [[Dep(), Dep()], [Dep(), Dep()]]
                sgT2 = [[sb("sgT%d_%d" % (p_, k_), [128, SEGT]) for p_ in range(2)] for k_ in range(2)]
                d_sgT2 = [[Dep(), Dep()], [Dep(), Dep()]]
                WC4_2 = [sb("WC4_%d" % k_, [64, SEG, 4]) for k_ in range(2)]; d_WC4_2 = [Dep(), Dep()]
                tmpH = sb("tmpH", [64, 4, 64]); d_tmpH = Dep()
                bufs = [(Wt2[k_], d_W2[k_], WCo2[k_], d_WCo2[k_], AR2[k_], d_AR2[k_], ARo2[k_], d_ARo2[k_], bonT2[k_],
                         d_bon2[k_], sgT2[k_], d_sgT2[k_]) for k_ in range(2)]
                sig4 = sb("sig4", [128, SEGT // 128, 256]); d_sig = Dep()
                lnt4 = sb("lnt4", [64, 4, 512]); d_ln = Dep()
                P.dma("sp", dm(lnt4[:], lnrow.rearrange("(g p) c -> p g c", p=64)), writes=[d_ln])
                UV = sb("UV", [128, SEG, NH, C], BF16); d_UV = [Dep() for _ in range(SEG)]
                BKt = sb("BKt", [128, SEG, NH, C], BF16); d_BKt = [Dep() for _ in range(SEG)]
                ATs = sb("ATs", [128, SEG, NH, 128], BF16); d_ATs = [Dep() for _ in range(SEG)]
                TT = sb("TT", [64, SEG, NH, 128], BF16); d_TT = [Dep() for _ in range(SEG)]
                NNl = [sb("NN%d" % k_, [64, 2, NH, C], BF16) for k_ in range(2)]; d_NNl = [Dep(), Dep()]
                PQl = [sb("PQ%d" % k_, [64, 2, NH, C], BF16) for k_ in range(2)]; d_PQl = [Dep(), Dep()]
                Xs = sb("Xs", [64, NH, C], BF16); d_Xs = Dep()
                Hb2 = [sb("Hb%d" % k_, [64, NH, C], BF16) for k_ in range(2)]; d_Hb2 = [Dep(), Dep()]
                Hs = sb("Hs", [64, NH, C]); d_Hs = Dep()
                Ysb = sb("Ysb", [64, SEG, NH * C]); d_Y = [Dep() for _ in range(SEG)]
                ysq = sb("ysq", [64, NH * C]); d_ysq = Dep()
                stt_ = sb("stt_", [64, 8]); d_stt = Dep()
                oT = sb("oT", [128, SEGT]); d_oT = Dep()
                osel = sb("osel", [128, 128]); d_osel = Dep()
                P.op("dve", mset(TT[:], 0.0), writes=d_TT)

                def hAP(tiles, shifted, h):
                    p_, q_ = h // 2, h % 2
                    return (tiles[p_] if q_ == 0 else shifted[p_]), p_, q_

                def emit_load(PP, hg, seg):
                    tok0 = seg * SEGT
                    if seg == 0 and hg > 0:
                        PP.dma("sp", dm(ppt[:], pp[hg * 128:(hg + 1) * 128]), writes=[d_pp])
                        PP.dma("sp", dm(lwt[:], lora_w[hg * 128:(hg + 1) * 128]), writes=[d_lw])
                        PP.dma("sp", dm(w0t[:], w0_row[hg:hg + 1]), writes=[d_w0])
                    zv = zfa[hg * 1024:(hg + 1) * 1024].rearrange("(c p) t -> p c t", p=128)
                    zl = zfa[4096:4224]
                    if seg == 0:
                        PP.op("dve", mset(zseg[:, :, 0:1], 0.0), writes=[d_zseg])
                        PP.dma("sp", dm(zseg[:, 0:8, 1:SEGT + 1], zv[:, :, 0:SEGT]), writes=[d_zseg])
                        PP.dma("sp", dm(zseg[:, 8, 1:SEGT + 1], zl[:, 0:SEGT]), writes=[d_zseg])
                    else:
                        PP.dma("sp", dm(zseg[:, 0:8, :], zv[:, :, tok0 - 1:tok0 + SEGT]), writes=[d_zseg])
                        PP.dma("sp", dm(zseg[:, 8, :], zl[:, tok0 - 1:tok0 + SEGT]), writes=[d_zseg])

                def emit_prep(PP, hg, seg, par):
                    Wt, d_W, WCo, d_WCo, AR, d_AR, ARo, d_ARo, bonT, d_bon, sgT, d_sgT = bufs[par]
                    tok0 = seg * SEGT
                    if seg == 0:
                        PP.op("dve", ts(om[:], ppt[:], -1.0, 1.0, ALU.mult, ALU.add), reads=[d_pp], writes=[d_om])
                    for zi, ch in enumerate((0, 1, 2, 3, 4, 5, 8)):
                        tb, dtb = (t1, d_t1) if zi % 2 == 0 else (t2, d_t2)
                        PP.op("pool", ts(tb[:], zseg[:, ch, 0:SEGT], ppt[:, ch:ch + 1]),
                             reads=[d_zseg, d_pp], writes=[dtb])
                        PP.op("dve", stt(zs[:, zi, :], zseg[:, ch, 1:SEGT + 1], om[:, ch:ch + 1], tb[:], ALU.mult,
                                        ALU.add), reads=[d_zseg, d_om, dtb], writes=[d_zs[zi]])
                    PP.op("act", act(zs[0:64, 6, :], zs[0:64, 6, :], AF.Tanh), reads=[d_zs[6]], writes=[d_zs[6]])
                    for tl in range(SEGT // 128):
                        PP.op("pe", mm(banks[4][:, 0:256], zs[0:64, 6, tl * 128:(tl + 1) * 128], lwt[0:64, :],
                                       start=True, stop=False), reads=[d_zs[6], d_lw], writes=[bdep[4]])
                        PP.op("pe", mm(banks[4][:, 0:256], onesr[0:1, :], w0t[0:1, :], start=False, stop=True),
                              reads=[d_onesr, d_w0], writes=[bdep[4]])
                        PP.op("act", act(sig4[:, tl, :], banks[4][:, 0:256], AF.Sigmoid), reads=[bdep[4]],
                              writes=[d_sig])
                    for p_ in range(2):
                        for tl in range(SEGT // 128):
                            for ie in range(2):
                                PP.op("pe", mm(banks[5 + ie][:, tl * 128:(tl + 1) * 128],
                                               sig4[:, tl, p_ * 128:(p_ + 1) * 128], trit[:, ie * 128:(ie + 1) * 128]),
                                      reads=[d_sig, d_tri], writes=[bdep[5 + ie]])
                        PP.op("act", act(Wt[p_][:], banks[5][:], AF.Exp), reads=[bdep[5]], writes=[d_W[p_]])
                        PP.op("act", act(Wi[p_][:], banks[5][:], AF.Exp, scale=-1.0), reads=[bdep[5]],
                              writes=[d_Wi[p_]])
                        PP.op("act", act(Wp[p_][:], banks[6][:], AF.Exp), reads=[bdep[6]], writes=[d_Wp[p_]])
                    for p_ in range(2):
                        rI, kI, vI = p_, 2 + p_, 4 + p_
                        c3 = lambda ap: ap.rearrange("p (c t) -> p c t", t=C)
                        PP.op("pe", mm(pbf32[:], lwt[64:128, p_ * 128:(p_ + 1) * 128], zs[64:128, 6, :]),
                             reads=[d_lw, d_zs[6]], writes=[d_pbf])
                        PP.op("act", act(a_sb[p_][:], pbf32[:], AF.Sigmoid, bias=ppt[:, 11 + p_:12 + p_]),
                             reads=[d_pbf, d_pp], writes=[d_a[p_]])
                        PP.op("act", act(sgT[p_][:], zseg[:, 6 + p_, 1:SEGT + 1], AF.Silu), reads=[d_zseg],
                             writes=[d_sgT[p_]])
                        PP.op("dve", ts(t1[:], zs[:, kI, :], ppt[:, 13 + p_:14 + p_]), reads=[d_zs[kI], d_pp],
                             writes=[d_t1])
                        PP.op("dve", tt(t2[:], t1[:], t1[:], ALU.mult), reads=[d_t1], writes=[d_t2])
                        PP.op("pe", mm(pbf32[:], obt[:], t2[:]), reads=[d_ob, d_t2], writes=[d_pbf])
                        PP.op("act", act(t2[:], pbf32[:], AF.Sqrt), reads=[d_pbf], writes=[d_t2])
                        PP.op("dve", ts(t2[:], t2[:], 1e-12, None, ALU.max), reads=[d_t2], writes=[d_t2])
                        PP.op("dve", rcp(t2[:], t2[:]), reads=[d_t2], writes=[d_t2])
                        PP.op("dve", tt(t1[:], t1[:], t2[:], ALU.mult), reads=[d_t1, d_t2], writes=[d_t1])
                        PP.op("dve", ts(t3[:], a_sb[p_][:], ppt[:, 15 + p_:16 + p_], om[:, 15 + p_:16 + p_], ALU.mult,
                                       ALU.add), reads=[d_a[p_], d_pp, d_om], writes=[d_t3])
                        PP.op("dve", tt(t3[:], t3[:], zs[:, kI, :], ALU.mult), reads=[d_t3, d_zs[kI]], writes=[d_t3])
                        PP.op("dve", stt(AR[p_][:, :, 0, :], c3(t1[:]), -1.0, c3(Wp[p_][:]), ALU.mult, ALU.mult),
                             reads=[d_t1, d_Wp[p_]], writes=[d_AR[p_]])
                        PP.op("pool", tt(AR[p_][:, :, 1, :], c3(zs[:, rI, :]), c3(Wt[p_][:]), ALU.mult),
                             reads=[d_zs[rI], d_W[p_]], writes=[d_AR[p_]])
                        PP.op("pool", tt(BK[p_][:, :, 0, :], c3(t3[:]), c3(Wi[p_][:]), ALU.mult),
                             reads=[d_t3, d_Wi[p_]], writes=[d_BK[p_]])
                        PP.op("dve", tt(t2[:], t1[:], a_sb[p_][:], ALU.mult), reads=[d_t1, d_a[p_]], writes=[d_t2])
                        PP.op("dve", tt(BK[p_][:, :, 1, :], c3(t2[:]), c3(Wi[p_][:]), ALU.mult),
                             reads=[d_t2, d_Wi[p_]], writes=[d_BK[p_]])
                        for c in range(SEG):
                            eng = "dve" if c % 2 == 0 else "pool"
                            PP.op(eng, ts(BKh[p_][:, c, :, :], BK[p_][:, c, :, :],
                                         Wt[p_][:, c * C + C - 1:c * C + C]), reads=[d_BK[p_], d_W[p_]],
                                 writes=[d_BKh[p_]])
                        PP.op("dve", stt(t2[:], zs[:, rI, :], ppt[:, 17 + p_:18 + p_], t3[:], ALU.mult, ALU.mult),
                             reads=[d_zs[rI], d_pp, d_t3], writes=[d_t2])
                        PP.op("pe", mm(pbf32[:], obt[:], t2[:]), reads=[d_ob, d_t2], writes=[d_pbf])
                        PP.op("dve", tt(bonT[p_][:], pbf32[:], zs[:, vI, :], ALU.mult), reads=[d_pbf, d_zs[vI]],
                             writes=[d_bon[p_]])
                        PP.dma("sp", dm(ARo[p_][:], AR[p_][64:128]), reads=[d_AR[p_]], writes=[d_ARo[p_]])
                        PP.dma("sp", dm(BKo[p_][:], BK[p_][64:128]), reads=[d_BK[p_]], writes=[d_BKo[p_]])
                        PP.dma("sp", dm(BKho[p_][:], BKh[p_][64:128]), reads=[d_BKh[p_]], writes=[d_BKho[p_]])
                        PP.dma("sp", (lambda e, p_=p_: e.dma_start(
                            out=WCo[p_][:], in_=Wt[p_][64:128, :].rearrange("p (c t) -> p c t", t=C)[:, :, C - 1],
                            allow_slow_non_contiguous=True)),
                              reads=[d_W[p_]], writes=[d_WCo[p_]])
                    for p_ in range(2):
                        PP.op("pool", cp(WC4_2[par][:, :, 2 * p_],
                                         Wt[p_][0:64, :].rearrange("p (c t) -> p c t", t=C)[:, :, C - 1]),
                              reads=[d_W[p_]], writes=[d_WC4_2[par]])
                        PP.op("pool", cp(WC4_2[par][:, :, 2 * p_ + 1], WCo[p_][:]), reads=[d_WCo[p_]],
                              writes=[d_WC4_2[par]])


                def emit_out(PP, hg, seg, par):
                    tok0 = seg * SEGT
                    Wt, d_W, WCo, d_WCo, AR, d_AR, ARo, d_ARo, bonT, d_bon, sgT, d_sgT = bufs[par]
                    for c in range(SEG):
                        yv = Ysb[:, c, :].rearrange("p (h v) -> p h v", h=NH)
                        PP.op("dve", lambda e, yv=yv: e.tensor_reduce(out=stt_[:, 0:4], in_=yv, axis=AX.X, op=ALU.add),
                             reads=[d_Y[c]], writes=[d_stt])
                        PP.op("pool", tt(ysq[:], Ysb[:, c, :], Ysb[:, c, :], ALU.mult), reads=[d_Y[c]], writes=[d_ysq])
                        PP.op("dve", lambda e: e.tensor_reduce(out=stt_[:, 4:8],
                                                              in_=ysq[:].rearrange("p (h v) -> p h v", h=NH),
                                                              axis=AX.X, op=ALU.add), reads=[d_ysq], writes=[d_stt])
                        PP.op("dve", ts(stt_[:, 0:8], stt_[:, 0:8], 1.0 / C), reads=[d_stt], writes=[d_stt])
                        PP.op("dve", tt(ysq[:, 0:4], stt_[:, 0:4], stt_[:, 0:4], ALU.mult), reads=[d_stt],
                             writes=[d_ysq])
                        PP.op("dve", tt(stt_[:, 4:8], stt_[:, 4:8], ysq[:, 0:4], ALU.subtract), reads=[d_stt, d_ysq],
                             writes=[d_stt])
                        PP.op("act", act(stt_[:, 4:8], stt_[:, 4:8], AF.Sqrt, bias=GN_EPS), reads=[d_stt],
                             writes=[d_stt])
                        PP.op("dve", rcp(stt_[:, 4:8], stt_[:, 4:8]), reads=[d_stt], writes=[d_stt])
                        for h in range(NH):
                            PP.op("dve", ts(Ysb[:, c, h * C:(h + 1) * C], Ysb[:, c, h * C:(h + 1) * C],
                                           stt_[:, h:h + 1], stt_[:, 4 + h:5 + h], ALU.subtract, ALU.mult),
                                 reads=[d_Y[c], d_stt], writes=[d_Y[c]])
                        PP.op("pool", tt(Ysb[:, c, :], Ysb[:, c, :], lnt4[:, hg, 0:256], ALU.mult), reads=[d_Y[c], d_ln],
                             writes=[d_Y[c]])
                        PP.op("pool", tt(Ysb[:, c, :], Ysb[:, c, :], lnt4[:, hg, 256:512], ALU.add), reads=[d_Y[c], d_ln],
                             writes=[d_Y[c]])
                    for p_ in range(2):
                        b_ = 4 + p_
                        for c in range(SEG):
                            PP.op("pe", tr(banks[b_][:, c * C:(c + 1) * C], Ysb[:, c, p_ * 128:(p_ + 1) * 128],
                                          idf[0:64, 0:64]), reads=[d_Y[c], d_idf], writes=[bdep[b_]])
                        PP.op("dve", tt(oT[:], banks[b_][:], bonT[p_][:], ALU.add), reads=[bdep[b_], d_bon[p_]],
                             writes=[d_oT])
                        PP.op("dve", tt(oT[:], oT[:], sgT[p_][:], ALU.mult), reads=[d_oT, d_sgT[p_]], writes=[d_oT])
                        if dbg:
                            PP.dma("sp", dm(orw[hg * 256 + p_ * 128:hg * 256 + (p_ + 1) * 128, tok0:tok0 + SEGT], oT[:]),
                                  reads=[d_oT])
                        PP.op("dve", ts(osel[:], oT[:, 0:128], s4t[:, 0:1]), reads=[d_oT, d_s4], writes=[d_osel])
                        for jj in range(1, 4):
                            PP.op("dve", stt(osel[:], oT[:, jj * 128:(jj + 1) * 128], s4t[:, jj:jj + 1], osel[:], ALU.mult,
                                            ALU.add), reads=[d_oT, d_s4, d_osel], writes=[d_osel])
                        PP.dma("sp", dm(mixT[hg * 256 + p_ * 128:hg * 256 + (p_ + 1) * 128, seg * 128:(seg + 1) * 128],
                                       osel[:]), reads=[d_osel])

                eq = Defer(P)
                per_e = 0
                units = [(a_, b_) for a_ in range(4) for b_ in range(T // SEGT)]
                emit_load(P, units[0][0], units[0][1])
                emit_prep(P, units[0][0], units[0][1], 0)
                for n_, (hg, seg) in enumerate(units):
                    tok0 = seg * SEGT
                    par = n_ % 2
                    Wt, d_W, WCo, d_WCo, AR, d_AR, ARo, d_ARo, bonT, d_bon, sgT, d_sgT = bufs[par]
                    if n_ + 1 < len(units):
                        emit_load(P, units[n_ + 1][0], units[n_ + 1][1])
                    nn2 = lambda ap: ap.rearrange("p (a h v) -> p a h v", a=2, h=NH)
                    sqb = [(banks[4], bdep[4]), (banks[6], bdep[6])]
                    pqb = [(banks[5], bdep[5]), (pbf32, d_pbf)]

                    def pre0(c, NN, d_NN, PQ, d_PQ):
                        for p_ in range(2):
                            P.op("pe", tr(banks[0][0:64, p_ * 128:(p_ + 1) * 128], zs[:, 4 + p_, c * C:(c + 1) * C],
                                          idf[:]), reads=[d_zs[4 + p_], d_idf], writes=[bdep[0]])
                        P.op("act", act(UV[0:64, c, :, :], banks[0][0:64, 0:256].rearrange("p (h v) -> p h v", h=NH),
                                        AF.Copy), reads=[bdep[0]], writes=[d_UV[c]])
                        for h in range(NH):
                            tl_, p_, q_ = hAP(BKh, BKho, h)
                            P.op("pe", tr(banks[1][:, h * C:(h + 1) * C],
                                          tl_[0:64, c, :, :].rearrange("p a t -> p (a t)"), idf[0:64, 0:64]),
                                 reads=[d_BKh[p_] if q_ == 0 else d_BKho[p_], d_idf], writes=[bdep[1]])
                        P.op("act", act(BKt[:, c, :, :], banks[1][:, 0:256].rearrange("p (h v) -> p h v", h=NH),
                                        AF.Copy), reads=[bdep[1]], writes=[d_BKt[c]])
                        for h in range(NH):
                            bk_, p_, q_ = hAP(BK, BKo, h)
                            ar_, _, _ = hAP(AR, ARo, h)
                            rd = [d_BK[p_] if q_ == 0 else d_BKo[p_], d_AR[p_] if q_ == 0 else d_ARo[p_]]
                            P.op("pe", mm(banks[2][:, h * 128:(h + 1) * 128],
                                          bk_[0:64, c, :, :].rearrange("p a t -> p (a t)"),
                                          ar_[0:64, c, :, :].rearrange("p a t -> p (a t)")), reads=rd,
                                 writes=[bdep[2]])
                            P.op("pe", mm(banks[3][0:64, h * C:(h + 1) * C], ar_[0:64, c, 0, :], bk_[0:64, c, 1, :]),
                                 reads=rd, writes=[bdep[3]])
                            P.op("pe", mm(banks[3][0:64, 256 + h * C:256 + (h + 1) * C], bk_[0:64, c, 1, :],
                                          ar_[0:64, c, 0, :]), reads=rd, writes=[bdep[3]])
                        P.op("dve", tt(ATs[:, c, :, :], banks[2][:].rearrange("p (h v) -> p h v", h=NH),
                                       mAT[:].rearrange("p (h v) -> p h v", h=NH), ALU.mult),
                             reads=[bdep[2], d_mAT], writes=[d_ATs[c]])
                        P.op("dve", tt(NN[:], nn2(banks[3][0:64, :]), nn2(mNN[:]), ALU.mult),
                             reads=[bdep[3], d_mNN], writes=[d_NN])
                        P.op("pool", tt(PQ[:, 0, :, :], NN[:, 1, :, :], nn2(i4[:])[:, 0, :, :], ALU.add),
                             reads=[d_NN, d_i4], writes=[d_PQ])
                        P.op("pool", tt(PQ[:, 1, :, :], NN[:, 0, :, :], nn2(i4[:])[:, 1, :, :], ALU.add),
                             reads=[d_NN, d_i4], writes=[d_PQ])

                    for c2 in range(0, SEG, 2):
                        pair = [(c2 + k_, NNl[k_], d_NNl[k_], PQl[k_], d_PQl[k_], sqb[k_], pqb[k_]) for k_ in range(2)]
                        for (c, NN, d_NN, PQ, d_PQ, _, _) in pair:
                            pre0(c, NN, d_NN, PQ, d_PQ)
                        for lev in range(1, 6):
                            last = lev == 5
                            for (c, NN, d_NN, PQ, d_PQ, (sq, d_sq), _) in pair:
                                for h in range(NH):
                                    if not last:
                                        P.op("pe", mm(sq[0:64, h * C:(h + 1) * C], NN[:, 1, h, :], NN[:, 0, h, :]),
                                             reads=[d_NN], writes=[d_sq])
                                    P.op("pe", mm(sq[0:64, 256 + h * C:256 + (h + 1) * C], NN[:, 0, h, :],
                                                  NN[:, 1, h, :]), reads=[d_NN], writes=[d_sq])
                            for (c, NN, d_NN, PQ, d_PQ, (sq, d_sq), _) in pair:
                                if not last:
                                    P.op("act", act(NN[:], nn2(sq[0:64, :]), AF.Copy), reads=[d_sq], writes=[d_NN])
                                else:
                                    P.op("act", act(NN[:, 1, :, :], nn2(sq[0:64, :])[:, 1, :, :], AF.Copy),
                                         reads=[d_sq], writes=[d_NN])
                            for (c, NN, d_NN, PQ, d_PQ, _, (pq, d_pq)) in pair:
                                for h in range(NH):
                                    P.op("pe", mm(pq[0:64, h * C:(h + 1) * C], PQ[:, 1, h, :], NN[:, 1, h, :]),
                                         reads=[d_PQ, d_NN], writes=[d_pq])
                                    if not last:
                                        P.op("pe", mm(pq[0:64, 256 + h * C:256 + (h + 1) * C], PQ[:, 0, h, :],
                                                      NN[:, 0, h, :]), reads=[d_PQ, d_NN], writes=[d_pq])
                            for (c, NN, d_NN, PQ, d_PQ, _, (pq, d_pq)) in pair:
                                if not last:
                                    P.op("dve", tt(PQ[:], PQ[:], nn2(pq[0:64, :]), ALU.add), reads=[d_PQ, d_pq],
                                         writes=[d_PQ])
                                else:
                                    P.op("dve", tt(TT[:, c, :, C:2 * C], PQ[:, 0, :, :], nn2(pq[0:64, :])[:, 0, :, :],
                                                   ALU.add), reads=[d_PQ, d_pq], writes=[d_TT[c]])
                        eq.run(per_e)
                    eq.run()

                    dq = Defer(P)
                    if n_ + 1 < len(units):
                        emit_prep(dq, units[n_ + 1][0], units[n_ + 1][1], 1 - par)
                    per = (len(dq.q) + SEG - 1) // SEG
                    if seg == 0:
                        P.op("dve", mset(Hs[:], 0.0), writes=[d_Hs])
                        P.op("dve", mset(Hb2[0][:], 0.0), writes=[d_Hb2[0]])
                    for c in range(SEG):
                        Hb, d_Hb = Hb2[c % 2], d_Hb2[c % 2]
                        Hbn, d_Hbn = Hb2[(c + 1) % 2], d_Hb2[(c + 1) % 2]
                        for h in range(NH):
                            ar_, p_, q_ = hAP(AR, ARo, h)
                            rd = [d_AR[p_] if q_ == 0 else d_ARo[p_], d_Hb]
                            P.op("pe", mm(banks[0][0:64, h * C:(h + 1) * C], ar_[0:64, c, 0, :], Hb[:, h, :],
                                          start=True, stop=False), reads=rd, writes=[bdep[0]])
                            P.op("pe", mm(banks[0][0:64, h * C:(h + 1) * C], ATs[0:64, c, h, 0:C], UV[0:64, c, h, :],
                                          start=False, stop=True), reads=[d_ATs[c], d_UV[c]], writes=[bdep[0]])
                        P.op("act", act(Xs[:], banks[0][0:64, 0:256].rearrange("p (h v) -> p h v", h=NH), AF.Copy),
                             reads=[bdep[0]], writes=[d_Xs])
                        P.op("dve", tt(tmpH[:], Hs[:], WC4_2[par][:, c, :].unsqueeze(2).to_broadcast([64, NH, C]), ALU.mult),
                             reads=[d_Hs, d_WC4_2[par]], writes=[d_tmpH])
                        for h in range(NH):
                            P.op("pe", mm(banks[1][:, h * C:(h + 1) * C], TT[:, c, h, :], Xs[:, h, :]),
                                 reads=[d_TT[c], d_Xs], writes=[bdep[1]])
                        P.op("act", act(UV[64:128, c, :, :],
                                        banks[1][64:128, 0:256].rearrange("p (h v) -> p h v", h=NH), AF.Copy),
                             reads=[bdep[1]], writes=[d_UV[c]])
                        for h in range(NH):
                            P.op("pe", mm(banks[3][0:64, h * C:(h + 1) * C], BKt[:, c, h, :], UV[:, c, h, :]),
                                 reads=[d_BKt[c], d_UV[c]], writes=[bdep[3]])
                        for h in range(NH):
                            ar_, p_, q_ = hAP(AR, ARo, h)
                            rd = [d_AR[p_] if q_ == 0 else d_ARo[p_], d_Hb]
                            P.op("pe", mm(banks[2][0:64, h * C:(h + 1) * C], ar_[0:64, c, 1, :], Hb[:, h, :],
                                          start=True, stop=False), reads=rd, writes=[bdep[2]])
                            P.op("pe", mm(banks[2][0:64, h * C:(h + 1) * C], ATs[:, c, h, C:2 * C], UV[:, c, h, :],
                                          start=False, stop=True), reads=[d_ATs[c], d_UV[c]], writes=[bdep[2]])
                        P.op("act", act(Ysb[:, c, :], banks[2][0:64, 0:256], AF.Copy), reads=[bdep[2]],
                             writes=[d_Y[c]])
                        st3 = banks[3][0:64, 0:256].rearrange("p (h v) -> p h v", h=NH)
                        P.op("dve", tt(Hbn[:], tmpH[:], st3, ALU.add), reads=[d_tmpH, bdep[3]], writes=[d_Hbn])
                        P.op("dve", tt(Hs[:], tmpH[:], st3, ALU.add), reads=[d_tmpH, bdep[3]], writes=[d_Hs])
                        dq.run(per)

                    dq.run()
                    eq = Defer(P)
                    emit_out(eq, hg, seg, par)
                    per_e = (len(eq.q) + 3) // 4
                eq.run()
                P.barrier()
                P.flush()

        if 3 in phases:
            with ExitStack() as es:
                sb = lambda name, shape, dt=F32: es.enter_context(nc.sbuf_tensor(name, list(shape), dt))
                idf = sb("idf3", [128, 128]); d_idf = Dep()
                P.dma("sp", dm(idf[:], ident_f), writes=[d_idf])
                ckvT_sb = sb("ckvT_sb", [128, 2, T], BF16); d_ckvT = Dep()
                ckv_sb = sb("ckv_sb", [128, NB, 256], BF16); d_ckv = Dep()
                kiTb = sb("kiTb", [128, T], BF16); d_kiT = Dep()
                stage = sb("stage3", [128, T]); d_stage = Dep()
                wuq_b = sb("wuq_b", [128, 3, 1024], BF16); d_wuq = Dep()
                iwq_b = sb("iwq_b", [128, 3, 1024], BF16); d_iwq = Dep()
                ukT_b = sb("ukT_b", [128, 16, 256], BF16); d_uk = Dep()
                uvp_b = sb("uvp_b", [128, 2, 16, 128], BF16); d_uv = Dep()
                b15 = sb("b15", [128, 2048]); d_b15 = Dep()
                hc = sb("hc", [128, NIT]); d_hc = Dep()
                ones_bf = sb("ones_bf", [128, 128], BF16); d_ones = Dep()
                P.op("dve", mset(ones_bf[:], 1.0), writes=[d_ones])
                for k2 in range(2):
                    P.dma("sp", dm(ckvT_sb[:, k2, :], ckvT[k2 * 128:(k2 + 1) * 128, :]), writes=[d_ckvT])
                P.dma("sp", dm(ckv_sb[:], ckv_tok.rearrange("(kb p) r -> p kb r", p=128)), writes=[d_ckv])
                P.dma("sp", dm(b15[:], b15rep), writes=[d_b15])
                P.dma("sp", dm(hc[:], halfc), writes=[d_hc])
                P.dma("sp", dm(stage[:], kiT2), writes=[d_stage])
                P.op("dve", cp(kiTb[:], stage[:]), reads=[d_stage], writes=[d_kiT])
                for wsrc, wdst, dd in ((w_uq, wuq_b, d_wuq), (iw_q, iwq_b, d_iwq)):
                    P.dma("sp", dm(stage[:, 0:3072].rearrange("p (k n) -> p k n", k=3),
                                   wsrc.rearrange("(k p) n -> p k n", p=128)), writes=[d_stage])
                    P.op("dve", cp(wdst[:], stage[:, 0:3072].rearrange("p (k n) -> p k n", k=3)), reads=[d_stage],
                         writes=[dd])
                P.dma("sp", dm(stage[:, 0:4096], ukT), writes=[d_stage])
                P.op("dve", cp(ukT_b[:], stage[:, 0:4096].rearrange("p (a r) -> p a r", a=16)), reads=[d_stage],
                     writes=[d_uk])
                P.dma("sp", dm(stage[:, 0:4096], uvp), writes=[d_stage])
                P.op("dve", cp(uvp_b[:], stage[:, 0:4096].rearrange("p (c h m) -> p c h m", c=2, h=16)),
                     reads=[d_stage], writes=[d_uv])

                qlb = sb("qlb", [128, 3, 128], BF16); d_qlb = Dep()
                wi = sb("wi", [128, 48]); d_wi = Dep()
                qT_sb = sb("qT_sb", [128, 16, 128], BF16); d_qT = Dep()
                qiT_sb = sb("qiT_sb", [128, 16, 128], BF16); d_qiT = Dep()
                P.op("dve", mset(qT_sb[:], 0.0), writes=[d_qT])
                P.op("dve", mset(qiT_sb[:], 0.0), writes=[d_qiT])
                qaT = sb("qaT", [128, 2, 16, 128], BF16); d_qaT = Dep()
                acc = sb("acc", [128, T]); d_acc = Dep()
                rl = [sb("rl%d" % k, [128, 512]) for k in range(2)]; d_rl = [Dep(), Dep()]
                cj = sb("cj", [128, T], BF16); d_cj = Dep()
                msk = stage; d_msk = d_stage
                maskT = sb("maskT", [128, NB, 128], BF16); d_maskT = Dep()
                bs = sb("bs", [128, 8]); d_bs = Dep()
                hk = sb("hk", [128, NIT]); d_hk = Dep()
                mad = sb("mad", [128, 640]); d_mad = Dep()
                bv = [sb("bv%d" % k, [128, 512]) for k in range(2)]; d_bv = [Dep(), Dep()]
                lg = sb("lg", [128, 512]); d_lg = Dep()
                e_sb = [sb("e_sb%d" % k, [128, 512], BF16) for k in range(2)]; d_e = [Dep(), Dep()]
                pT = [sb("pT%d" % k, [128, 4, 128], BF16) for k in range(2)]; d_pT = [Dep(), Dep()]
                rs = sb("rs", [128, 512]); d_rs = Dep()
                on = sb("on", [128, 2, 4, 128], BF16); d_on = Dep()
                gd = sb("gd", [128, 128]); d_gd = Dep()
                mo = sb("mo", [128, 128]); d_mo = Dep()

                for i in range(OWNB):
                    L = 512 * (i + 1)
                    nkb = 4 * (i + 1)
                    wk0 = max(4 * i - 1, 0)
                    P.dma("sp", dm(qlb[:], qlT[:, i * 128:(i + 1) * 128].rearrange("(k p) t -> p k t", p=128)),
                          writes=[d_qlb])
                    P.dma("sp", dm(wi[:, 0:16], widx[i * 128:(i + 1) * 128, :]), writes=[d_wi])
                    P.op("act", act(wi[:, 32:48], wi[:, 0:16], AF.Sign), reads=[d_wi], writes=[d_wi])
                    P.op("dve", tt(wi[:, 16:32], wi[:, 0:16], wi[:, 32:48], ALU.mult), reads=[d_wi], writes=[d_wi])
                    for W_, dW, dst, ddst in ((wuq_b, d_wuq, qT_sb, d_qT), (iwq_b, d_iwq, qiT_sb, d_qiT)):
                        for hb in range(4):
                            b_ = hb % 2
                            for hl in range(4):
                                h = hb * 4 + hl
                                for kc in range(3):
                                    P.op("pe", mm(banks[b_][0:64, hl * 128:(hl + 1) * 128], W_[:, kc, h * 64:(h + 1) * 64],
                                                  qlb[:, kc, :], start=(kc == 0), stop=(kc == 2)),
                                         reads=[dW, d_qlb], writes=[bdep[b_]])
                            P.op("act", act(dst[0:64, hb * 4:(hb + 1) * 4, :],
                                            banks[b_][0:64, :].rearrange("p (k t) -> p k t", k=4), AF.Copy),
                                 reads=[bdep[b_]], writes=[ddst])
                    if K3STOP < 1:
                        continue
                    for rc in range(2):
                        for hb in range(4):
                            b_ = 5 + (hb % 2)
                            for hl in range(4):
                                h = hb * 4 + hl
                                P.op("pe", mm(banks[b_][:, hl * 128:(hl + 1) * 128],
                                              ukT_b[:, h, rc * 128:(rc + 1) * 128], qT_sb[:, h, :]),
                                     reads=[d_uk, d_qT], writes=[bdep[b_]])
                            P.op("dve", ts(qaT[:, rc, hb * 4:(hb + 1) * 4, :],
                                           banks[b_][:].rearrange("p (k t) -> p k t", k=4), 0.125),
                                 reads=[bdep[b_]], writes=[d_qaT])
                    if K3STOP < 2:
                        continue
                    for st_ in range(i + 1):
                        for h in range(16):
                            b_ = h % 2
                            P.op("pe", mm(banks[b_][:], qiT_sb[:, h, :], kiTb[:, st_ * 512:(st_ + 1) * 512]),
                                 reads=[d_qiT, d_kiT], writes=[bdep[b_]])
                            P.op("act", act(rl[b_][:], banks[b_][:], AF.Relu, scale=wi[:, 16 + h:17 + h]),
                                 reads=[bdep[b_], d_wi], writes=[d_rl[b_]])
                            a_ = acc[:, st_ * 512:(st_ + 1) * 512]
                            if h == 0:
                                P.op("dve", ts(a_, rl[b_][:], wi[:, 32:33]), reads=[d_rl[b_], d_wi], writes=[d_acc])
                            else:
                                P.op("dve", stt(a_, rl[b_][:], wi[:, 32 + h:33 + h], a_, ALU.mult, ALU.add),
                                     reads=[d_rl[b_], d_wi, d_acc], writes=[d_acc])
                    if K3STOP < 3:
                        continue
                    P.op("dve", lambda e, L=L: e.tensor_reduce(out=bs[:, 0:1], in_=acc[:, 0:L], axis=AX.X, op=ALU.max),
                         reads=[d_acc], writes=[d_bs])
                    P.op("dve", lambda e, L=L: e.tensor_reduce(out=bs[:, 1:2], in_=acc[:, 0:L], axis=AX.X, op=ALU.min),
                         reads=[d_acc], writes=[d_bs])
                    P.op("dve", ts(bs[:, 3:4], bs[:, 1:2], -1.0, None, ALU.add), reads=[d_bs], writes=[d_bs])
                    P.op("dve", tt(bs[:, 2:3], bs[:, 0:1], bs[:, 1:2], ALU.subtract), reads=[d_bs], writes=[d_bs])
                    P.op("dve", ts(bs[:, 2:3], bs[:, 2:3], 2.0, None, ALU.add), reads=[d_bs], writes=[d_bs])
                    P.op("dve", ts(hk[:], hc[:], bs[:, 2:3]), reads=[d_hc, d_bs], writes=[d_hk])
                    P.dma("sp", dm(mad[:], madd[i * 128:(i + 1) * 128, :]), writes=[d_mad])
                    wcol0 = 128 if i == 0 else 0
                    P.op("dve", tt(acc[:, wk0 * 128:L], acc[:, wk0 * 128:L], mad[:, wcol0:640], ALU.add),
                         reads=[d_acc, d_mad], writes=[d_acc])
                    P.op("dve", tt(bs[:, 4:5], bs[:, 3:4], hk[:, 0:1], ALU.add), reads=[d_bs, d_hk], writes=[d_bs])
                    for it in range(NIT):
                        P.op("dve", tsa(cj[:, 0:L], acc[:, 0:L], bs[:, 4:5], 0.0, ALU.is_ge, ALU.add, bs[:, 5:6]),
                             reads=[d_acc, d_bs], writes=[d_cj, d_bs])
                        P.op("dve", ts(bs[:, 6:7], bs[:, 5:6], KSEL - 0.5, hk[:, it:it + 1], ALU.is_ge, ALU.mult),
                             reads=[d_bs, d_hk], writes=[d_bs])
                        if it < NIT - 1:
                            P.op("dve", stt(bs[:, 4:5], bs[:, 6:7], hk[:, it + 1:it + 2], bs[:, 4:5], ALU.subtract, ALU.add),
                                 reads=[d_bs, d_hk], writes=[d_bs])
                        else:
                            P.op("dve", stt(bs[:, 3:4], bs[:, 6:7], hk[:, it:it + 1], bs[:, 4:5], ALU.subtract, ALU.add),
                                 reads=[d_bs, d_hk], writes=[d_bs])
                    P.op("dve", ts(msk[:, 0:L], acc[:, 0:L], bs[:, 3:4], None, ALU.is_ge), reads=[d_acc, d_bs],
                         writes=[d_msk])
                    if K3STOP < 4:
                        continue
                    for kg in range(nkb // 4):
                        for k4 in range(4):
                            kb = kg * 4 + k4
                            P.op("pe", tr(banks[4][:, k4 * 128:(k4 + 1) * 128], msk[:, kb * 128:(kb + 1) * 128], idf[:]),
                                 reads=[d_msk, d_idf], writes=[bdep[4]])
                        P.op("act", act(maskT[:, kg * 4:(kg + 1) * 4, :],
                                        banks[4][:].rearrange("p (k t) -> p k t", k=4), AF.Copy),
                             reads=[bdep[4]], writes=[d_maskT])
                    if K3STOP < 5:
                        continue
                    for hq in range(4):
                        for kb in range(nkb):
                            b_ = kb % 2
                            for rc in range(2):
                                P.op("pe", mm(banks[b_][:], ckvT_sb[:, rc, kb * 128:(kb + 1) * 128],
                                              qaT[:, rc, hq * 4:(hq + 1) * 4, :].rearrange("p h t -> p (h t)"),
                                              start=(rc == 0), stop=(rc == 1)), reads=[d_ckvT, d_qaT],
                                     writes=[bdep[b_]])
                            if kb >= wk0:
                                w_ = kb - (4 * i - 1)
                                r0 = (i * 5 + w_) * 128
                                P.dma("sp", dm(bv[b_][:], biasvar[r0:r0 + 128, hq * 512:(hq + 1) * 512]),
                                      writes=[d_bv[b_]])
                                P.op("pool", tt(bv[b_][:], bv[b_][:], b15[:, hq * 512:(hq + 1) * 512], ALU.subtract),
                                     reads=[d_bv[b_], d_b15], writes=[d_bv[b_]])
                                P.op("dve", tt(lg[:], banks[b_][:], bv[b_][:], ALU.add), reads=[bdep[b_], d_bv[b_]],
                                     writes=[d_lg])
                                P.op("act", act(e_sb[b_][:], lg[:], AF.Exp), reads=[d_lg], writes=[d_e[b_]])
                            else:
                                P.op("act", act(e_sb[b_][:], banks[b_][:], AF.Exp), reads=[bdep[b_]], writes=[d_e[b_]])
                            P.op("dve", tt(pT[b_][:], e_sb[b_][:].rearrange("p (h t) -> p h t", h=4),
                                           maskT[:, kb, :].unsqueeze(1).to_broadcast([128, 4, 128]), ALU.mult),
                                 reads=[d_e[b_], d_maskT], writes=[d_pT[b_]])
                            pflat = pT[b_][:].rearrange("p h t -> p (h t)")
                            for rc in range(2):
                                P.op("pe", mm(banks[2 + rc][:], ckv_sb[:, kb, rc * 128:(rc + 1) * 128], pflat,
                                              start=(kb == 0), stop=(kb == nkb - 1)), reads=[d_ckv, d_pT[b_]],
                                     writes=[bdep[2 + rc]])
                            P.op("pe", mm(banks[5][:], ones_bf[:], pflat, start=(kb == 0), stop=(kb == nkb - 1)),
                                 reads=[d_ones, d_pT[b_]], writes=[bdep[5]])
                        P.op("dve", rcp(rs[:], banks[5][:]), reads=[bdep[5]], writes=[d_rs])
                        for rc in range(2):
                            P.op("dve", tt(on[:, rc, :, :].rearrange("p h t -> p (h t)"), banks[2 + rc][:], rs[:],
                                           ALU.mult), reads=[bdep[2 + rc], d_rs], writes=[d_on])
                        for pr in range(2):
                            for hl2 in range(2):
                                hl = pr * 2 + hl2
                                h = hq * 4 + hl
                                for rc in range(2):
                                    P.op("pe", mm(banks[6][:, pr * 128:(pr + 1) * 128], uvp_b[:, rc, h, :],
                                                  on[:, rc, hl, :], start=(hl2 == 0 and rc == 0),
                                                  stop=(hl2 == 1 and rc == 1)), reads=[d_uv, d_on], writes=[bdep[6]])
                        for pr in range(2):
                            fch = hq * 2 + pr
                            P.dma("sp", dm(gd[:], gdsT[fch * 128:(fch + 1) * 128, i * 128:(i + 1) * 128]), writes=[d_gd])
                            P.op("act", act(gd[:], gd[:], AF.Silu), reads=[d_gd], writes=[d_gd])
                            P.op("dve", tt(mo[:], banks[6][:, pr * 128:(pr + 1) * 128], gd[:], ALU.mult),
                                 reads=[bdep[6], d_gd], writes=[d_mo])
                            P.dma("sp", dm(mixT[1024 + fch * 128:1024 + (fch + 1) * 128, i * 128:(i + 1) * 128], mo[:]),
                                  reads=[d_mo])
                P.barrier()
                P.flush()

        if (2 not in phases or 3 not in phases) and not mix_ext:
            with ExitStack() as es:
                zt = es.enter_context(nc.sbuf_tensor("zt", [128, TO], F32)); d_zt = Dep()
                P.op("dve", mset(zt[:], 0.0), writes=[d_zt])
                ks = ([] if 2 in phases else list(range(8))) + ([] if 3 in phases else list(range(8, KC)))
                for k in ks:
                    P.dma("sp", dm(mixT[k * 128:(k + 1) * 128, :], zt[:]), reads=[d_zt])
                P.barrier()
                P.flush()

        if 4 in phases:
            with ExitStack() as es:
                sb = lambda name, shape, dt=F32: es.enter_context(nc.sbuf_tensor(name, list(shape), dt))
                idf = sb("idf4", [128, 128]); d_idf = Dep()
                P.dma("sp", dm(idf[:], ident_f), writes=[d_idf])
                Wf = sb("Wf", [128, KC, D], BF16); d_Wf = Dep()
                Pw = sb("Pw", [128, 2, D], BF16); d_Pw = Dep()
                stg = [sb("stg4_%d" % i, [128, D]) for i in range(2)]
                d_stg = [Dep(), Dep()]
                h_all = sb("h_all", [128, OWNB, D]); d_h = [Dep() for _ in range(OWNB)]
                mx = sb("mx", [128, KC, 128]); d_mx = Dep()
                mxb = sb("mxb", [128, KC, 128], BF16); d_mxb = Dep()
                xt = sb("xt4", [128, D]); d_xt = Dep()
                fr = sb("fr", [128, D]); d_fr = Dep()
                sg = sb("sg", [128, 1024]); d_sg = Dep()
                pt_ = sb("pt4", [128, 256]); d_pt = Dep()
                pTb = sb("pTb", [128, 2, 128], BF16); d_pTb = Dep()
                st = sb("st4", [128, 4]); d_st = Dep()
                junk = sb("junk4", [128, D], BF16); d_junk = Dep()
                P.dma("sp", dm(fr[:], fin_row), writes=[d_fr])

                def load_w(wsrc, nk, dst, d_dst):
                    for kc in range(nk):
                        s_ = kc % 2
                        P.dma("sp", dm(stg[s_][:], wsrc[kc * 128:(kc + 1) * 128, :]), writes=[d_stg[s_]])
                        eng = "dve" if kc % 2 == 0 else "pool"
                        P.op(eng, cp(dst[:, kc, :], stg[s_][:]), reads=[d_stg[s_]], writes=[d_dst])

                load_w(w_out, KC, Wf, d_Wf)
                for i in range(OWNB):
                    P.dma("sp", dm(mx[:], mixT[:, i * 128:(i + 1) * 128].rearrange("(k p) t -> p k t", p=128)),
                          writes=[d_mx])
                    P.op("pool", cp(mxb[:], mx[:]), reads=[d_mx], writes=[d_mxb])
                    P.dma("sp", dm(xt[:], x_own[i * 128:(i + 1) * 128, :]), writes=[d_xt])
                    for n4 in range(4):
                        for kc in range(KC):
                            P.op("pe", mm(banks[n4][:], mxb[:, kc, :], Wf[:, kc, n4 * 512:(n4 + 1) * 512],
                                          start=(kc == 0), stop=(kc == KC - 1)), reads=[d_mxb, d_Wf],
                                 writes=[bdep[n4]])
                        P.op("dve", tt(h_all[:, i, n4 * 512:(n4 + 1) * 512], banks[n4][:],
                                       xt[:, n4 * 512:(n4 + 1) * 512], ALU.add), reads=[bdep[n4], d_xt],
                             writes=[d_h[i]])
                load_w(w_gate, KC, Wf, d_Wf)
                load_w(w_ple, 2, Pw, d_Pw)
                for i in range(OWNB):
                    for g in range(4):
                        for q in range(4):
                            kc = g * 4 + q
                            P.op("pe", tr(banks[4][:, q * 128:(q + 1) * 128], h_all[:, i, kc * 128:(kc + 1) * 128],
                                          idf[:]), reads=[d_h[i], d_idf], writes=[bdep[4]])
                        P.op("act", act(mxb[:, g * 4:(g + 1) * 4, :], banks[4][:].rearrange("p (k t) -> p k t", k=4),
                                        AF.Copy), reads=[bdep[4]], writes=[d_mxb])
                    P.dma("sp", dm(pt_[:], p_own[i * 128:(i + 1) * 128, :]), writes=[d_pt])
                    for k2 in range(2):
                        P.op("pe", tr(banks[4][:, k2 * 128:(k2 + 1) * 128], pt_[:, k2 * 128:(k2 + 1) * 128], idf[:]),
                             reads=[d_pt, d_idf], writes=[bdep[4]])
                    P.op("act", act(pTb[:], banks[4][:, 0:256].rearrange("p (k t) -> p k t", k=2), AF.Copy),
                         reads=[bdep[4]], writes=[d_pTb])
                    for half in range(2):
                        for n2 in range(2):
                            n0 = half * 1024 + n2 * 512
                            for kc in range(KC):
                                P.op("pe", mm(banks[n2][:], mxb[:, kc, :], Wf[:, kc, n0:n0 + 512],
                                              start=(kc == 0), stop=(kc == KC - 1)), reads=[d_mxb, d_Wf],
                                     writes=[bdep[n2]])
                            P.op("act", act(sg[:, n2 * 512:(n2 + 1) * 512], banks[n2][:], AF.Sigmoid),
                                 reads=[bdep[n2]], writes=[d_sg])
                            for k2 in range(2):
                                P.op("pe", mm(banks[2 + n2][:], pTb[:, k2, :], Pw[:, k2, n0:n0 + 512],
                                              start=(k2 == 0), stop=(k2 == 1)), reads=[d_pTb, d_Pw],
                                     writes=[bdep[2 + n2]])
                            P.op("dve", tt(sg[:, n2 * 512:(n2 + 1) * 512], banks[2 + n2][:],
                                           sg[:, n2 * 512:(n2 + 1) * 512], ALU.mult), reads=[bdep[2 + n2], d_sg],
                                 writes=[d_sg])
                            P.op("pool", tt(h_all[:, i, n0:n0 + 512], h_all[:, i, n0:n0 + 512],
                                            sg[:, n2 * 512:(n2 + 1) * 512], ALU.add), reads=[d_sg, d_h[i]],
                                 writes=[d_h[i]])
                    P.op("act", act(junk[:], h_all[:, i, :], AF.Square, accum=st[:, 0:1]), reads=[d_h[i]],
                         writes=[d_junk, d_st])
                    P.op("act", act(st[:, 1:2], st[:, 0:1], AF.Sqrt, bias=EPS, scale=1.0 / D), reads=[d_st],
                         writes=[d_st])
                    P.op("dve", rcp(st[:, 2:3], st[:, 1:2]), reads=[d_st], writes=[d_st])
                    P.op("dve", stt(h_all[:, i, :], h_all[:, i, :], st[:, 2:3], fr[:], ALU.mult, ALU.mult),
                         reads=[d_h[i], d_st, d_fr], writes=[d_h[i]])
                    P.dma("sp", dm(out[i * 128:(i + 1) * 128, :], h_all[:, i, :]), reads=[d_h[i]])
                P.barrier()
                P.flush()

        P.barrier()
        P.flush()
    return nc


def _bucket_table():
    rel = np.arange(-4095, 4096)
    n = np.abs(rel)
    nf = np.maximum(n, 1).astype(np.float32)
    large = 8 + (np.log(nf / np.float32(8.0)) / np.float32(math.log(16.0)) * np.float32(8.0)).astype(np.int32)
    large = np.minimum(large, 15)
    return np.where(rel > 0, 16, 0) + np.where(n < 8, n, large)


def host_prep(inp, c):
    b, j = c // 4, c % 4
    f32 = np.float32
    w_in = inp["w_in"][0]
    heads = [4 * j + i for i in range(4)]
    hc = lambda base: np.concatenate([np.arange(base + h * 64, base + (h + 1) * 64) for h in heads])
    gcols = lambda g: np.concatenate([np.arange(base + h * 64, base + (h + 1) * 64)
                                      for base in (0, 1024, 2048, 3200) for h in range(4 * g, 4 * g + 4)])
    ta_cols = np.arange(4608, 4928)
    to_cols = np.concatenate([np.arange(4224, 4608), np.arange(4928, 4944)])
    fo_cols = np.arange(4944, 5968)
    toks = np.concatenate([np.arange(bk * 128, (bk + 1) * 128) for bk in own_blocks(j)])
    m = {}
    m["x_all"] = np.ascontiguousarray(inp["x"][b])
    m["x_own"] = np.ascontiguousarray(inp["x"][b][toks])
    m["wAg"] = np.ascontiguousarray(np.concatenate([w_in[:, gcols(g)] for g in range(4)], 0))
    m["wAx"] = np.ascontiguousarray(w_in[:, np.concatenate([np.arange(3072, 3200), ta_cols])])
    m["wO"] = np.ascontiguousarray(w_in[:, np.concatenate([to_cols, fo_cols])])
    m["g_col"] = np.ascontiguousarray(inp["norm_g"][0].reshape(KC, 128).T)
    m["ident_f"] = np.eye(128, dtype=f32)
    row = np.concatenate([inp["ds_kv_norm_g"][0], inp["idx_k_norm_g"][0], inp["ds_q_norm_g"][0]])
    m["rowc"] = np.ascontiguousarray(np.broadcast_to(row[None, :], (128, 704))).astype(f32)
    mu = inp["rw_mu"][0]
    ppm = np.zeros((4, 128, 24), f32)
    lwm = np.zeros((4, 128, 256), f32)
    w0m = np.zeros((4, 256), f32)
    lnm = np.zeros((4, 64, 512), f32)
    for g in range(4):
        own = np.arange(g * 256, (g + 1) * 256)
        gc_ = gcols(g)
        for ch in range(6):
            ppm[g, :, ch] = mu[gc_[ch * 128:(ch + 1) * 128]]
        ppm[g, :, 8] = mu[3072:3200]
        for p_ in range(2):
            oc = own[p_ * 128:(p_ + 1) * 128]
            ppm[g, :, 11 + p_] = inp["rw_a0"][0][oc]
            ppm[g, :, 13 + p_] = inp["rw_k_k"][0][oc]
            ppm[g, :, 15 + p_] = inp["rw_k_a"][0][oc]
            ppm[g, :, 17 + p_] = inp["rw_r_k"][0].reshape(-1)[oc]
        lwm[g, 0:64] = inp["rw_w_up"][0][:, own]
        lwm[g, 64:128] = inp["rw_a_up"][0][:, own]
        w0m[g] = inp["rw_w0"][0][own]
        lnm[g] = np.concatenate([inp["rw_ln_g"][0][own], inp["rw_ln_b"][0][own]])[None, :]
    m["pp"] = ppm.reshape(4 * 128, 24)
    m["lora_w"] = lwm.reshape(4 * 128, 256)
    m["w0_row"] = w0m
    m["lnrow"] = lnm.reshape(4 * 64, 512)
    s4 = np.zeros((128, 4), f32); s4[:, j] = 1.0
    m["sel4"] = s4
    ii = np.arange(128)
    same = (ii[:, None] // 64) == (ii[None, :] // 64)
    cdec = -math.exp(-0.5)
    m["tri"] = np.concatenate([(same & (ii[:, None] <= ii[None, :])), (same & (ii[:, None] < ii[None, :]))],
                              1).astype(f32) * f32(cdec)
    m["ones_blk"] = same.astype(f32)
    i6 = np.arange(64)
    lt = (i6[:, None] < i6[None, :]).astype(f32)
    le = (i6[:, None] <= i6[None, :]).astype(f32)
    mat = np.block([[lt, le], [lt, le]])
    m["maskAT4"] = np.ascontiguousarray(np.tile(mat, (1, 4)))
    m["maskNN"] = np.ascontiguousarray(np.concatenate([np.tile(lt.T, (1, 4)), np.tile(lt, (1, 4))], 1))
    m["id4"] = np.ascontiguousarray(np.tile(np.eye(64, dtype=f32), (1, 8)))
    m["w_uq"] = np.ascontiguousarray(inp["ds_w_uq"][0])
    m["iw_q"] = np.ascontiguousarray(inp["idx_w_q"][0])
    wuk = inp["ds_w_uk"][0]
    ukm = np.zeros((128, 16, 256), f32)
    ukm[0:64] = wuk.transpose(2, 1, 0)
    m["ukT"] = ukm.reshape(128, 16 * 256)
    wuv = inp["ds_w_uv"][0]
    uvpm = np.zeros((128, 2, 16, 128), f32)
    for h in range(16):
        q_ = h % 2
        uvpm[:, :, h, q_ * 64:(q_ + 1) * 64] = wuv[:, h, :].reshape(2, 128, 64).transpose(1, 0, 2)
    m["uvp"] = uvpm.reshape(128, 4096)
    bt = _bucket_table()
    maddm = np.full((OWNB, 128, 640), -1e30, f32)
    biasm = np.zeros((OWNB, 5, 128, 16, 128), f32)
    rb = inp["rel_bias"]
    for i in range(OWNB):
        blk = 4 * i + j
        tpos = blk * 128 + np.arange(128)
        limit = (tpos // 64 + 1) * 64
        for w_ in range(5):
            kb = 4 * i - 1 + w_
            if kb < 0:
                continue
            spos = kb * 128 + np.arange(128)
            adm = spos[None, :] < limit[:, None]
            maddm[i, :, w_ * 128:(w_ + 1) * 128] = np.where(adm, 0.0, -1e30)
            rel = spos[:, None] - tpos[None, :]
            biasm[i, w_] = rb[bt[rel + 4095]].transpose(0, 2, 1)
    m["madd"] = maddm.reshape(OWNB * 128, 640)
    m["biasvar"] = biasm.reshape(OWNB * 5 * 128, 2048)
    m["b15rep"] = np.ascontiguousarray(np.broadcast_to(np.repeat(rb[15], 128)[None, :], (128, 2048))).astype(f32)
    m["halfc"] = np.ascontiguousarray(np.broadcast_to((0.5 ** np.arange(1, NIT + 1))[None, :], (128, NIT))).astype(f32)
    m["p_own"] = np.ascontiguousarray(inp["p"][0, b][toks])
    m["w_out"] = np.ascontiguousarray(inp["w_out"][0])
    m["w_gate"] = np.ascontiguousarray(inp["ple_gate_w"][0])
    m["w_ple"] = np.ascontiguousarray(inp["ple_w"][0])
    m["fin_row"] = np.ascontiguousarray(np.broadcast_to(inp["final_g"][None, :], (128, D))).astype(f32)
    return m


def kernel(**inputs):
    inp = {k: np.asarray(v) for k, v in inputs.items()}
    nc = build_nc()
    in_maps = [host_prep(inp, c) for c in range(NCORES)]
    res = run_bass_kernel_spmd(nc, in_maps, core_ids=list(range(NCORES)))
    out = np.zeros((2, T, D), np.float32)
    for c in range(NCORES):
        b, j = c // 4, c % 4
        o = res.results[c]["out"]
        for i, bk in enumerate(own_blocks(j)):
            out[b, bk * 128:(bk + 1) * 128] = o[i * 128:(i + 1) * 128]
    return out
```

```python
import math, os
SKIP = set(os.environ.get('KSKIP', '').split(','))
K3STOP = int(os.environ.get('K3STOP', '9'))
from contextlib import ExitStack
import numpy as np
import ml_dtypes
import concourse.bass as bass
import concourse.mybir as mybir
from concourse.bass_utils import run_bass_kernel_spmd

F32 = mybir.dt.float32
BF16 = mybir.dt.bfloat16
ALU = mybir.AluOpType
AF = mybir.ActivationFunctionType
AX = mybir.AxisListType

NCORES = 8
T = 4096
D = 2048
KC = 16
NB = 32
OWNB = 8
TO = 1024
EPS = 1e-6
GN_EPS = 64e-5
NFA = 1152
NZ = 4224
NTA = 320
NTO = 400
NFO = 1024
SEG = 8
SEGT = SEG * 64
NIT = 12
KSEL = 256


class Dep:
    __slots__ = ("w", "r")

    def __init__(self):
        self.w = None
        self.r = []


class Prog:
    ENGS = ("pe", "act", "dve", "pool", "sp")

    def __init__(self, nc, es, ndsem=24):
        self.nc = nc
        self.q = {e: [] for e in self.ENGS}
        self.sem = {e: es.enter_context(nc.semaphore("s_" + e)) for e in self.ENGS}
        self.cnt = {e: 0 for e in self.ENGS}
        self.real = {e: 0 for e in self.ENGS}
        self.known = {e: {} for e in self.ENGS}
        self.dsem = [es.enter_context(nc.semaphore("d%d" % i)) for i in range(ndsem)]
        self.dcnt = [0] * ndsem
        self.dnext = 0

    def _waits(self, eng, deps):
        need = {}
        kn = self.known[eng]
        for ev in deps:
            if ev is None:
                continue
            s, v = ev
            k = id(s)
            if kn.get(k, 0) >= v:
                continue
            if k not in need or need[k][1] < v:
                need[k] = (s, v)
        for k, (s, v) in need.items():
            kn[k] = v
        return list(need.values())

    def _deps(self, eng, reads, writes):
        deps = []
        for t in reads:
            deps.append(t.w)
        for t in writes:
            deps.append(t.w)
            deps.extend(t.r)
        if eng == "pe":
            ps = self.sem["pe"]
            deps = [d for d in deps if d is not None and d[0] is not ps]
        return deps

    def _post(self, ev, reads, writes):
        for t in reads:
            t.r.append(ev)
            if len(t.r) > 48:
                best = {}
                for (s, v) in t.r:
                    if id(s) not in best or best[id(s)][1] < v:
                        best[id(s)] = (s, v)
                t.r = list(best.values())
        for t in writes:
            t.w = ev
            t.r = []

    def op(self, eng, fn, reads=(), writes=()):
        waits = self._waits(eng, self._deps(eng, reads, writes))
        self.cnt[eng] += 1
        ev = (self.sem[eng], self.cnt[eng])
        self.q[eng].append((waits, fn, ev, 1))
        self._post(ev, reads, writes)
        return ev

    def dma(self, eng, fn, reads=(), writes=()):
        i = self.dnext
        self.dnext = (i + 1) % len(self.dsem)
        deps = self._deps(eng, reads, writes)
        if self.dcnt[i] > 0:
            deps.append((self.dsem[i], 16 * self.dcnt[i]))
        waits = self._waits(eng, deps)
        self.dcnt[i] += 1
        ev = (self.dsem[i], 16 * self.dcnt[i])
        self.q[eng].append((waits, fn, ev, 16))
        self._post(ev, reads, writes)
        return ev

    def barrier(self):
        evs = [(self.sem[e], self.cnt[e]) for e in self.ENGS if self.cnt[e] > 0]
        evs += [(self.dsem[i], 16 * self.dcnt[i]) for i in range(len(self.dsem)) if self.dcnt[i] > 0]
        for e in self.ENGS:
            waits = self._waits(e, evs)
            if waits:
                self.q[e].append((waits, None, None, 0))

    def flush(self):
        nc = self.nc
        sem2eng = {id(self.sem[e]): e for e in self.ENGS}
        needed = {e: set() for e in self.ENGS}
        for e in self.ENGS:
            for waits, fn, ev, inc in self.q[e]:
                for s_, v in waits:
                    if id(s_) in sem2eng:
                        needed[sem2eng[id(s_)]].add(v)
        newval = {e: {} for e in self.ENGS}
        for e in self.ENGS:
            c = self.real[e]
            for waits, fn, ev, inc in self.q[e]:
                if fn is None or inc != 1:
                    continue
                if ev[1] in needed[e]:
                    c += 1
                    newval[e][ev[1]] = c
            self.real[e] = c

        def mk(eng):
            items = self.q[eng]

            def body(e):
                for waits, fn, ev, inc in items:
                    for s_, v in waits:
                        if id(s_) in sem2eng:
                            v = newval[sem2eng[id(s_)]][v]
                        e.wait_ge(s_, v)
                    if fn is not None:
                        ins = fn(e)
                        if inc != 1:
                            ins.then_inc(ev[0], inc)
                        elif ev[1] in newval[eng]:
                            ins.then_inc(ev[0], 1)
            return body

        with nc.Block() as block:
            block.tensor(mk("pe"))
            block.scalar(mk("act"))
            block.vector(mk("dve"))
            block.gpsimd(mk("pool"))
            block.sync(mk("sp"))
        self.q = {e: [] for e in self.ENGS}


class Defer:
    def __init__(self, P):
        self.P = P
        self.q = []

    def op(self, *a, **k):
        self.q.append(("op", a, k))

    def dma(self, *a, **k):
        self.q.append(("dma", a, k))

    def run(self, n=None):
        n = len(self.q) if n is None else min(n, len(self.q))
        for _ in range(n):
            kind, a, k = self.q.pop(0)
            getattr(self.P, kind)(*a, **k)


def ts(out, in0, s1, s2=None, op0=ALU.mult, op1=None):
    if op1 is None:
        return lambda e: e.tensor_scalar(out=out, in0=in0, scalar1=s1, scalar2=None, op0=op0)
    return lambda e: e.tensor_scalar(out=out, in0=in0, scalar1=s1, scalar2=s2, op0=op0, op1=op1)


def tsa(out, in0, s1, s2, op0, op1, accum):
    return lambda e: e.tensor_scalar(out=out, in0=in0, scalar1=s1, scalar2=s2, op0=op0, op1=op1, accum_out=accum)


def tt(out, a, b, op):
    return lambda e: e.tensor_tensor(out=out, in0=a, in1=b, op=op)


def stt(out, in0, s, in1, op0, op1):
    return lambda e: e.scalar_tensor_tensor(out=out, in0=in0, scalar=s, in1=in1, op0=op0, op1=op1)


def act(out, in_, f, bias=None, scale=None, accum=None):
    kw = {}
    if bias is not None:
        kw["bias"] = bias
    if scale is not None:
        kw["scale"] = scale
    if accum is not None:
        kw["accum_out"] = accum
    return lambda e: e.activation(out=out, in_=in_, func=f, **kw)


def mm(out, lhsT, rhs, start=True, stop=True):
    return lambda e: e.matmul(out=out, lhsT=lhsT, rhs=rhs, start=start, stop=stop)


def tr(out, in_, ident):
    return lambda e: e.transpose(out=out, in_=in_, identity=ident)


def cp(out, in_):
    return lambda e: e.tensor_copy(out=out, in_=in_)


def dm(out, in_):
    return lambda e: e.dma_start(out=out, in_=in_)


def rcp(out, in_):
    return lambda e: e.reciprocal(out=out, in_=in_)


def mset(ap, v):
    return lambda e: e.memset(ap, v)


def own_blocks(j):
    return [4 * i + j for i in range(8)]


def build_nc(dbg=False, phases=(1, 2, 3, 4), lim=(8, 2), mix_ext=False, zfa_ext=False, p1_ext=False):
    nc = bass.Bass("TRN2", target_bir_lowering=False)
    ext = lambda name, shape, dt=F32: nc.dram_tensor(name, list(shape), dt, kind="ExternalInput").ap()
    scr_kind = "ExternalOutput" if dbg else "Internal"
    scr = lambda name, shape, dt=F32: nc.dram_tensor(name, list(shape), dt, kind=scr_kind).ap()

    x_own = ext("x_own", [TO, D])
    ident_f = ext("ident_f", [128, 128])
    if 1 in phases:
        x_all = ext("x_all", [T, D])
        wAg = ext("wAg", [4 * D, 1024])
        wAx = ext("wAx", [D, 448])
        wO = ext("wO", [D, NTO + NFO])
        g_col = ext("g_col", [128, KC])
        rowc = ext("rowc", [128, 256 + 64 + 384])

    zfa = (ext if zfa_ext else scr)("zfa", [NZ, T])
    xnc = nc.dram_tensor("xnc", [8 * 128, KC * 512], BF16, kind="Internal").ap()
    d_xnc = [Dep() for _ in range(8)]
    scr1 = ext if p1_ext else scr
    ckv_tok = scr1("ckv_tok", [T, 256], BF16)
    ckvT = scr1("ckvT", [256, T], BF16)
    kiT2 = scr1("kiT2", [128, T])
    qlT = scr1("qlT", [384, TO], BF16)
    widx = scr1("widx", [TO, 16])
    gdsT = scr1("gdsT", [NFO, TO])
    if 3 in phases:
        w_uq = ext("w_uq", [384, 1024])
        iw_q = ext("iw_q", [384, 1024])
        ukT = ext("ukT", [128, 16 * 256])
        uvp = ext("uvp", [128, 2 * 16 * 128])
        madd = ext("madd", [OWNB * 128, 640])
        biasvar = ext("biasvar", [OWNB * 5 * 128, 2048])
        b15rep = ext("b15rep", [128, 2048])
        halfc = ext("halfc", [128, NIT])
    orw = scr("orw", [1024, T])
    if 2 in phases:
        pp = ext("pp", [4 * 128, 24])
        lora_w = ext("lora_w", [4 * 128, 256])
        w0_row = ext("w0_row", [4, 256])
        sel4 = ext("sel4", [128, 4])
        tri = ext("tri", [128, 256])
        ones_blk = ext("ones_blk", [128, 128])
        maskAT4 = ext("maskAT4", [128, 512])
        maskNN = ext("maskNN", [64, 512])
        id4 = ext("id4", [64, 512])
        lnrow = ext("lnrow", [4 * 64, 512])
    mixT = (ext if mix_ext else scr)("mixT", [D, TO])
    if 4 in phases:
        p_own = ext("p_own", [TO, 256])
        w_out = ext("w_out", [D, D])
        w_gate = ext("w_gate", [D, D])
        w_ple = ext("w_ple", [256, D])
        fin_row = ext("fin_row", [128, D])
        out = nc.dram_tensor("out", [TO, D], F32, kind="ExternalOutput").ap()

    with ExitStack() as top:
        P = Prog(nc, top)
        banks = [top.enter_context(nc.psum_tensor("bank%d" % i, [128, 512], F32)) for i in range(7)]
        bdep = [Dep() for _ in range(7)]
        pbf = top.enter_context(nc.psum_tensor("pbf", [128, 1024], BF16))
        d_pbf = Dep()
        d_b6b = d_pbf
        pbf32 = pbf.bitcast(F32)

        if 1 in phases:
            STQ = "pool"
            with ExitStack() as es:
                sb = lambda name, shape, dt=F32: es.enter_context(nc.sbuf_tensor(name, list(shape), dt))
                idf = sb("idf", [128, 128]); d_idf = Dep()
                idb = sb("idb", [128, 128], BF16); d_idb = Dep()
                gc = sb("gc", [128, KC]); d_gc = Dep()
                rc = sb("rc", [128, 704]); d_rc = Dep()
                P.dma("sp", dm(idf[:], ident_f), writes=[d_idf])
                P.dma("sp", dm(gc[:], g_col), writes=[d_gc])
                P.dma("sp", dm(rc[:], rowc), writes=[d_rc])
                P.op("dve", cp(idb[:], idf[:]), reads=[d_idf], writes=[d_idb])
                Wb = sb("Wb", [128, KC, NFA + NTA], BF16); d_Wb = Dep()
                stg = [sb("stg%d" % i, [128, NFA + NTA]) for i in range(2)]
                d_stg = [Dep(), Dep()]
                xt = [sb("xt%d" % i, [128, D]) for i in range(2)]
                d_xt = [Dep(), Dep()]
                junk = sb("junk", [128, D], BF16); d_junk = Dep()
                st = sb("st", [128, 8]); d_st = Dep()
                xs = sb("xs", [128, D]); d_xs = Dep()
                xnT = sb("xnT", [128, KC, 512], BF16); d_xnT = Dep()
                xnT_b = sb("xnT_b", [128, KC, 512], BF16); d_xnT_b = Dep()
                xs_b = sb("xs_b", [128, D]); d_xs_b = Dep()
                xs_l = [(xs, d_xs), (xs_b, d_xs_b)]
                X = {"t": xnT, "d": d_xnT}

                def setx(k):
                    X["t"], X["d"] = (xnT, d_xnT) if k % 2 == 0 else (xnT_b, d_xnT_b)

                zst = [sb("zst%d" % i, [128, 512]) for i in range(2)]
                d_zst = [Dep(), Dep()]
                ckv_st = sb("ckv_st", [128, 4, 256], BF16); d_ckv_st = Dep()
                ckvT_st = sb("ckvT_st", [128, 2, 512], BF16); d_ckvT_st = Dep()
                ki2 = sb("ki2", [128, 128]); d_ki2 = Dep()
                kiT_st = sb("kiT_st", [128, 512]); d_kiT_st = Dep()
                qln = sb("qln", [128, 384]); d_qln = Dep()
                ckv_f = sb("ckv_f", [128, 256]); d_ckv_f = Dep()
                qlT_st = sb("qlT_st", [128, 3, 512], BF16); d_qlT_st = Dep()
                wi_st = sb("wi_st", [128, 4, 16]); d_wi_st = Dep()

                def load_weights(wsrc, ncols, col0=0):
                    for kc in range(KC):
                        s = kc % 2
                        P.dma("sp", dm(stg[s][:, 0:ncols], wsrc[kc * 128:(kc + 1) * 128, :]), writes=[d_stg[s]])
                        eng = "dve" if kc % 2 == 0 else "pool"
                        P.op(eng, ts(Wb[:, kc, col0:col0 + ncols], stg[s][:, 0:ncols], gc[:, kc:kc + 1]),
                             reads=[d_stg[s], d_gc], writes=[d_Wb])

                def norm_transpose_group(xsrc, grp):
                    for r4 in range(4):
                        rt = grp * 4 + r4
                        s = rt % 2
                        P.dma("sp", dm(xt[s][:], xsrc[rt * 128:(rt + 1) * 128, :]), writes=[d_xt[s]])
                        P.op("act", act(junk[:], xt[s][:], AF.Square, accum=st[:, 0:1]), reads=[d_xt[s]],
                             writes=[d_junk, d_st])
                        P.op("act", act(st[:, 1:2], st[:, 0:1], AF.Sqrt, bias=EPS, scale=1.0 / D), reads=[d_st],
                             writes=[d_st])
                        P.op("dve", rcp(st[:, 2:3], st[:, 1:2]), reads=[d_st], writes=[d_st])
                        xsc, d_xsc = xs_l[s]
                        P.op("dve", ts(xsc[:], xt[s][:], st[:, 2:3]), reads=[d_xt[s], d_st], writes=[d_xsc])
                        for g in range(4):
                            for q in range(4):
                                kc = g * 4 + q
                                P.op("pe", tr(banks[g][:, q * 128:(q + 1) * 128], xsc[:, kc * 128:(kc + 1) * 128],
                                              idf[:]), reads=[d_xsc, d_idf], writes=[bdep[g]])
                            eng = "act" if g % 2 == 0 else "dve"
                            o = X["t"][:, g * 4:(g + 1) * 4, r4 * 128:(r4 + 1) * 128]
                            i = banks[g][:].rearrange("p (k t) -> p k t", k=4)
                            if eng == "act":
                                P.op("act", act(o, i, AF.Copy), reads=[bdep[g]], writes=[X["d"]])
                            else:
                                P.op("dve", cp(o, i), reads=[bdep[g]], writes=[X["d"]])

                def fm_proj(col0, nchunk, dst, tok0, xb=None, dxb=None):
                    xb = X["t"] if xb is None else xb
                    dxb = X["d"] if dxb is None else dxb
                    for fcn in range(nchunk):
                        b = 4 + (fcn % 2)
                        for kc in range(KC):
                            P.op("pe", mm(banks[b][:], Wb[:, kc, col0 + fcn * 128:col0 + (fcn + 1) * 128],
                                          xb[:, kc, :], start=(kc == 0), stop=(kc == KC - 1)),
                                 reads=[d_Wb, dxb], writes=[bdep[b]])
                        s = fcn % 2
                        if s == 0:
                            P.op("act", act(zst[s][:], banks[b][:], AF.Copy), reads=[bdep[b]], writes=[d_zst[s]])
                        else:
                            P.op("dve", cp(zst[s][:], banks[b][:]), reads=[bdep[b]], writes=[d_zst[s]])
                        P.dma(STQ, dm(dst[fcn * 128:(fcn + 1) * 128, tok0:tok0 + 512], zst[s][:]),
                              reads=[d_zst[s]])

                for hgp in range(4):
                  load_weights(wAg[hgp * D:(hgp + 1) * D], 1024)
                  if hgp == 0:
                      load_weights(wAx, 448, 1024)
                  for grp in range(lim[0]):
                    if hgp == 0:
                        setx(grp)
                        norm_transpose_group(x_all, grp)
                        P.dma(STQ, dm(xnc[grp * 128:(grp + 1) * 128, :], X["t"][:].rearrange("p k t -> p (k t)")),
                              reads=[X["d"]], writes=[d_xnc[grp]])
                    else:
                        xb_, dxb_ = (xnT, d_xnT) if grp % 2 == 0 else (xnT_b, d_xnT_b)
                        P.dma("sp", dm(xb_[:].rearrange("p k t -> p (k t)"), xnc[grp * 128:(grp + 1) * 128, :]),
                              reads=[d_xnc[grp]], writes=[dxb_])
                        fm_proj(0, 8, zfa[hgp * 1024:(hgp + 1) * 1024], grp * 512, xb_, dxb_)
                        continue
                    fm_proj(0, 8, zfa[hgp * 1024:(hgp + 1) * 1024], grp * 512)
                    fm_proj(1024, 1, zfa[4096:4224], grp * 512)
                    for r4 in range(4):
                        for kc in range(KC):
                            P.op("pe", mm(banks[6][:, 0:NTA], X["t"][:, kc, r4 * 128:(r4 + 1) * 128],
                                          Wb[:, kc, NFA:NFA + NTA], start=(kc == 0), stop=(kc == KC - 1)),
                                 reads=[d_Wb, X["d"]], writes=[bdep[6]])
                        pt = banks[6]
                        P.op("act", act(junk[:, 0:256], pt[:, 0:256], AF.Square, accum=st[:, 3:4]),
                             reads=[bdep[6]], writes=[d_junk, d_st])
                        P.op("act", act(junk[:, 0:64], pt[:, 256:320], AF.Square, accum=st[:, 4:5]),
                             reads=[bdep[6]], writes=[d_junk, d_st])
                        P.op("act", act(st[:, 3:4], st[:, 3:4], AF.Sqrt, bias=EPS, scale=1.0 / 256), reads=[d_st],
                             writes=[d_st])
                        P.op("act", act(st[:, 4:5], st[:, 4:5], AF.Sqrt, bias=EPS, scale=1.0 / 64), reads=[d_st],
                             writes=[d_st])
                        P.op("dve", rcp(st[:, 5:7], st[:, 3:5]), reads=[d_st], writes=[d_st])
                        P.op("dve", stt(ckv_f[:], pt[:, 0:256], st[:, 5:6], rc[:, 0:256], ALU.mult, ALU.mult),
                             reads=[bdep[6], d_st, d_rc], writes=[d_ckv_f])
                        P.op("pool", cp(ckv_st[:, r4, :], ckv_f[:]), reads=[d_ckv_f], writes=[d_ckv_st])
                        P.op("dve", stt(ki2[:, 0:64], pt[:, 256:320], st[:, 6:7], rc[:, 256:320], ALU.mult, ALU.mult),
                             reads=[bdep[6], d_st, d_rc], writes=[d_ki2])
                        P.op("dve", stt(ki2[:, 64:128], pt[:, 256:320], st[:, 6:7], rc[:, 256:320], ALU.mult,
                                        ALU.mult), reads=[bdep[6], d_st, d_rc], writes=[d_ki2])
                        for h2 in range(0 if 'tr' in SKIP else 2):
                            P.op("pe", tr(pbf32[:, h2 * 128:(h2 + 1) * 128], ckv_f[:, h2 * 128:(h2 + 1) * 128],
                                          idf[:]), reads=[d_ckv_f, d_idf], writes=[d_pbf])
                        if 'tr' not in SKIP:
                            for k2 in range(2):
                                P.op("act", act(ckvT_st[:, k2, r4 * 128:(r4 + 1) * 128],
                                                pbf32[:, k2 * 128:(k2 + 1) * 128], AF.Copy),
                                     reads=[d_pbf], writes=[d_ckvT_st])
                        if 'ki' not in SKIP:
                            P.op("pe", tr(pbf32[:, 256:384], ki2[:], idf[:]), reads=[d_ki2, d_idf], writes=[d_b6b])
                            P.op("act", act(kiT_st[:, r4 * 128:(r4 + 1) * 128], pbf32[:, 256:384], AF.Copy),
                                 reads=[d_b6b], writes=[d_kiT_st])
                    t0 = grp * 512
                    P.dma(STQ, dm(ckv_tok[t0:t0 + 512, :].rearrange("(r p) c -> p r c", p=128), ckv_st[:]),
                          reads=[d_ckv_st])
                    for k2 in range(0 if 'trd' in SKIP else 2):
                        P.dma(STQ, dm(ckvT[k2 * 128:(k2 + 1) * 128, t0:t0 + 512], ckvT_st[:, k2, :]),
                              reads=[d_ckvT_st])
                    P.dma(STQ, dm(kiT2[:, t0:t0 + 512], kiT_st[:]), reads=[d_kiT_st])

                load_weights(wO, NTO + NFO)
                for grp in range(lim[1]):
                    setx(grp)
                    norm_transpose_group(x_own, grp)
                    fm_proj(NTO, NFO // 128, gdsT, grp * 512)
                    for r4 in range(4):
                        for kc in range(KC):
                            P.op("pe", mm(banks[6][:, 0:NTO], X["t"][:, kc, r4 * 128:(r4 + 1) * 128],
                                          Wb[:, kc, 0:NTO], start=(kc == 0), stop=(kc == KC - 1)),
                                 reads=[d_Wb, X["d"]], writes=[bdep[6]])
                        pt = banks[6]
                        P.op("act", act(junk[:, 0:384], pt[:, 0:384], AF.Square, accum=st[:, 3:4]),
                             reads=[bdep[6]], writes=[d_junk, d_st])
                        P.op("act", act(st[:, 3:4], st[:, 3:4], AF.Sqrt, bias=EPS, scale=1.0 / 384), reads=[d_st],
                             writes=[d_st])
                        P.op("dve", rcp(st[:, 5:6], st[:, 3:4]), reads=[d_st], writes=[d_st])
                        P.op("dve", stt(qln[:], pt[:, 0:384], st[:, 5:6], rc[:, 320:704], ALU.mult, ALU.mult),
                             reads=[bdep[6], d_st, d_rc], writes=[d_qln])
                        P.op("dve", ts(wi_st[:, r4, :], pt[:, 384:400], 1.0 / 32.0), reads=[bdep[6]],
                             writes=[d_wi_st])
                        for h3 in range(3):
                            P.op("pe", tr(pbf32[:, h3 * 128:(h3 + 1) * 128], qln[:, h3 * 128:(h3 + 1) * 128], idf[:]),
                                 reads=[d_qln, d_idf], writes=[d_pbf])
                        P.op("act", act(qlT_st[:, :, r4 * 128:(r4 + 1) * 128],
                                        pbf32[:, 0:384].rearrange("p (k t) -> p k t", k=3), AF.Copy),
                             reads=[d_pbf], writes=[d_qlT_st])
                    t0 = grp * 512
                    for k3 in range(3):
                        P.dma(STQ, dm(qlT[k3 * 128:(k3 + 1) * 128, t0:t0 + 512], qlT_st[:, k3, :]),
                              reads=[d_qlT_st])
                    P.dma(STQ, dm(widx[t0:t0 + 512, :].rearrange("(r p) c -> p r c", p=128), wi_st[:]),
                          reads=[d_wi_st])
                P.barrier()
                P.flush()

        if 2 in phases:
            with ExitStack() as es:
                sb = lambda name, shape, dt=F32: es.enter_context(nc.sbuf_tensor(name, list(shape), dt))
                NH = 4
                C = 64
                cst = {}
                for nm, src, shp in (("idf", ident_f, [128, 128]), ("pp", pp[0:128], [128, 24]),
                                     ("lw", lora_w[0:128], [128, 256]),
                                     ("w0", w0_row[0:1], [1, 256]), ("tri", tri, [128, 256]), ("ob", ones_blk, [128, 128]),
                                     ("mAT", maskAT4, [128, 512]), ("mNN", maskNN, [64, 512]), ("id4", id4, [64, 512]),
                                     ("sel4", sel4, [128, 4])):
                    t_ = sb("c_" + nm, shp)
                    dd = Dep()
                    P.dma("sp", dm(t_[:], src), writes=[dd])
                    cst[nm] = (t_, dd)
                idf, d_idf = cst["idf"]; ppt, d_pp = cst["pp"]; lwt, d_lw = cst["lw"]; w0t, d_w0 = cst["w0"]
                trit, d_tri = cst["tri"]; obt, d_ob = cst["ob"]; mAT, d_mAT = cst["mAT"]; mNN, d_mNN = cst["mNN"]
                i4, d_i4 = cst["id4"]; s4t, d_s4 = cst["sel4"]
                om = sb("om", [128, 24]); d_om = Dep()

                onesr = sb("onesr", [1, 128]); d_onesr = Dep()
                P.op("dve", mset(onesr[:], 1.0), writes=[d_onesr])
                zseg = sb("zseg", [128, 9, SEGT + 1]); d_zseg = Dep()
                zs = sb("zs", [128, 7, SEGT]); d_zs = [Dep() for _ in range(7)]
                t1 = sb("t1", [128, SEGT]); d_t1 = Dep()
                t2 = sb("t2", [128, SEGT]); d_t2 = Dep()
                t3 = sb("t3", [128, SEGT]); d_t3 = Dep()
                Wt2 = [[sb("W%d_%d" % (p_, k_), [128, SEGT]) for p_ in range(2)] for k_ in range(2)]
                d_W2 = [[Dep(), Dep()], [Dep(), Dep()]]
                Wi = [sb("Wi%d" % p_, [128, SEGT]) for p_ in range(2)]; d_Wi = [Dep(), Dep()]
                Wp = [sb("Wp%d" % p_, [128, SEGT]) for p_ in range(2)]; d_Wp = [Dep(), Dep()]
                a_sb = [sb("a%d" % p_, [128, SEGT]) for p_ in range(2)]; d_a = [Dep(), Dep()]
                AR2 = [[sb("AR%d_%d" % (p_, k_), [128, SEG, 2, C], BF16) for p_ in range(2)] for k_ in range(2)]
                d_AR2 = [[Dep(), Dep()], [Dep(), Dep()]]
                BK = [sb("BK%d" % p_, [128, SEG, 2, C], BF16) for p_ in range(2)]; d_BK = [Dep(), Dep()]
                BKh = [sb("BKh%d" % p_, [128, SEG, 2, C]) for p_ in range(2)]; d_BKh = [Dep(), Dep()]
                ARo2 = [[sb("ARo%d_%d" % (p_, k_), [64, SEG, 2, C], BF16) for p_ in range(2)] for k_ in range(2)]
                d_ARo2 = [[Dep(), Dep()], [Dep(), Dep()]]
                BKo = [sb("BKo%d" % p_, [64, SEG, 2, C], BF16) for p_ in range(2)]; d_BKo = [Dep(), Dep()]
                BKho = [sb("BKho%d" % p_, [64, SEG, 2, C]) for p_ in range(2)]; d_BKho = [Dep(), Dep()]
                WCo2 = [[sb("WCo%d_%d" % (p_, k_), [64, SEG]) for p_ in range(2)] for k_ in range(2)]
                d_WCo2 = [[Dep(), Dep()], [Dep(), Dep()]]
                bonT2 = [[sb("bon%d_%d" % (p_, k_), [128, SEGT]) for p_ in range(2)] for k_ in range(2)]
                d_bon2 = [[Dep(), Dep()], [Dep(), Dep()]]
                sgT2 = [[sb("sgT%d_%d" % (p_, k_), [128, SEGT]) for p_ in range(2)] for k_ in range(2)]
                d_sgT2 = [[Dep(), Dep()], [Dep(), Dep()]]
                WC4_2 = [sb("WC4_%d" % k_, [64, SEG, 4]) for k_ in range(2)]; d_WC4_2 = [Dep(), Dep()]
                tmpH = sb("tmpH", [64, 4, 64]); d_tmpH = Dep()
                bufs = [(Wt2[k_], d_W2[k_], WCo2[k_], d_WCo2[k_], AR2[k_], d_AR2[k_], ARo2[k_], d_ARo2[k_], bonT2[k_],
                         d_bon2[k_], sgT2[k_], d_sgT2[k_]) for k_ in range(2)]
                sig4 = sb("sig4", [128, SEGT // 128, 256]); d_sig = Dep()
                lnt4 = sb("lnt4", [64, 4, 512]); d_ln = Dep()
                P.dma("sp", dm(lnt4[:], lnrow.rearrange("(g p) c -> p g c", p=64)), writes=[d_ln])
                UV = sb("UV", [128, SEG, NH, C], BF16); d_UV = [Dep() for _ in range(SEG)]
                BKt = sb("BKt", [128, SEG, NH, C], BF16); d_BKt = [Dep() for _ in range(SEG)]
                ATs = sb("ATs", [128, SEG, NH, 128], BF16); d_ATs = [Dep() for _ in range(SEG)]
                TT = sb("TT", [64, SEG, NH, 128], BF16); d_TT = [Dep() for _ in range(SEG)]
                NNl = [sb("NN%d" % k_, [64, 2, NH, C], BF16) for k_ in range(2)]; d_NNl = [Dep(), Dep()]
                PQl = [sb("PQ%d" % k_, [64, 2, NH, C], BF16) for k_ in range(2)]; d_PQl = [Dep(), Dep()]
                Xs = sb("Xs", [64, NH, C], BF16); d_Xs = Dep()
                Hb2 = [sb("Hb%d" % k_, [64, NH, C], BF16) for k_ in range(2)]; d_Hb2 = [Dep(), Dep()]
                Hs = sb("Hs", [64, NH, C]); d_Hs = Dep()
                Ysb = sb("Ysb", [64, SEG, NH * C]); d_Y = [Dep() for _ in range(SEG)]
                ysq = sb("ysq", [64, NH * C]); d_ysq = Dep()
                stt_ = sb("stt_", [64, 8]); d_stt = Dep()
                oT = sb("oT", [128, SEGT]); d_oT = Dep()
                osel = sb("osel", [128, 128]); d_osel = Dep()
                P.op("dve", mset(TT[:], 0.0), writes=d_TT)

                def hAP(tiles, shifted, h):
                    p_, q_ = h // 2, h % 2
                    return (tiles[p_] if q_ == 0 else shifted[p_]), p_, q_

                def emit_load(PP, hg, seg):
                    tok0 = seg * SEGT
                    if seg == 0 and hg > 0:
                        PP.dma("sp", dm(ppt[:], pp[hg * 128:(hg + 1) * 128]), writes=[d_pp])
                        PP.dma("sp", dm(lwt[:], lora_w[hg * 128:(hg + 1) * 128]), writes=[d_lw])
                        PP.dma("sp", dm(w0t[:], w0_row[hg:hg + 1]), writes=[d_w0])
                    zv = zfa[hg * 1024:(hg + 1) * 1024].rearrange("(c p) t -> p c t", p=128)
                    zl = zfa[4096:4224]
                    if seg == 0:
                        PP.op("dve", mset(zseg[:, :, 0:1], 0.0), writes=[d_zseg])
                        PP.dma("sp", dm(zseg[:, 0:8, 1:SEGT + 1], zv[:, :, 0:SEGT]), writes=[d_zseg])
                        PP.dma("sp", dm(zseg[:, 8, 1:SEGT + 1], zl[:, 0:SEGT]), writes=[d_zseg])
                    else:
                        PP.dma("sp", dm(zseg[:, 0:8, :], zv[:, :, tok0 - 1:tok0 + SEGT]), writes=[d_zseg])
                        PP.dma("sp", dm(zseg[:, 8, :], zl[:, tok0 - 1:tok0 + SEGT]), writes=[d_zseg])

                def emit_prep(PP, hg, seg, par):
                    Wt, d_W, WCo, d_WCo, AR, d_AR, ARo, d_ARo, bonT, d_bon, sgT, d_sgT = bufs[par]
                    tok0 = seg * SEGT
                    if seg == 0:
                        PP.op("dve", ts(om[:], ppt[:], -1.0, 1.0, ALU.mult, ALU.add), reads=[d_pp], writes=[d_om])
                    for zi, ch in enumerate((0, 1, 2, 3, 4, 5, 8)):
                        tb, dtb = (t1, d_t1) if zi % 2 == 0 else (t2, d_t2)
                        PP.op("pool", ts(tb[:], zseg[:, ch, 0:SEGT], ppt[:, ch:ch + 1]),
                             reads=[d_zseg, d_pp], writes=[dtb])
                        PP.op("dve", stt(zs[:, zi, :], zseg[:, ch, 1:SEGT + 1], om[:, ch:ch + 1], tb[:], ALU.mult,
                                        ALU.add), reads=[d_zseg, d_om, dtb], writes=[d_zs[zi]])
                    PP.op("act", act(zs[0:64, 6, :], zs[0:64, 6, :], AF.Tanh), reads=[d_zs[6]], writes=[d_zs[6]])
                    for tl in range(SEGT // 128):
                        PP.op("pe", mm(banks[4][:, 0:256], zs[0:64, 6, tl * 128:(tl + 1) * 128], lwt[0:64, :],
                                       start=True, stop=False), reads=[d_zs[6], d_lw], writes=[bdep[4]])
                        PP.op("pe", mm(banks[4][:, 0:256], onesr[0:1, :], w0t[0:1, :], start=False, stop=True),
                              reads=[d_onesr, d_w0], writes=[bdep[4]])
                        PP.op("act", act(sig4[:, tl, :], banks[4][:, 0:256], AF.Sigmoid), reads=[bdep[4]],
                              writes=[d_sig])
                    for p_ in range(2):
                        for tl in range(SEGT // 128):
                            for ie in range(2):
                                PP.op("pe", mm(banks[5 + ie][:, tl * 128:(tl + 1) * 128],
                                               sig4[:, tl, p_ * 128:(p_ + 1) * 128], trit[:, ie * 128:(ie + 1) * 128]),
                                      reads=[d_sig, d_tri], writes=[bdep[5 + ie]])
                        PP.op("act", act(Wt[p_][:], banks[5][:], AF.Exp), reads=[bdep[5]], writes=[d_W[p_]])
                        PP.op("act", act(Wi[p_][:], banks[5][:], AF.Exp, scale=-1.0), reads=[bdep[5]],
                              writes=[d_Wi[p_]])
                        PP.op("act", act(Wp[p_][:], banks[6][:], AF.Exp), reads=[bdep[6]], writes=[d_Wp[p_]])
                    for p_ in range(2):
                        rI, kI, vI = p_, 2 + p_, 4 + p_
                        c3 = lambda ap: ap.rearrange("p (c t) -> p c t", t=C)
                        PP.op("pe", mm(pbf32[:], lwt[64:128, p_ * 128:(p_ + 1) * 128], zs[64:128, 6, :]),
                             reads=[d_lw, d_zs[6]], writes=[d_pbf])
                        PP.op("act", act(a_sb[p_][:], pbf32[:], AF.Sigmoid, bias=ppt[:, 11 + p_:12 + p_]),
                             reads=[d_pbf, d_pp], writes=[d_a[p_]])
                        PP.op("act", act(sgT[p_][:], zseg[:, 6 + p_, 1:SEGT + 1], AF.Silu), reads=[d_zseg],
                             writes=[d_sgT[p_]])
                        PP.op("dve", ts(t1[:], zs[:, kI, :], ppt[:, 13 + p_:14 + p_]), reads=[d_zs[kI], d_pp],
                             writes=[d_t1])
                        PP.op("dve", tt(t2[:], t1[:], t1[:], ALU.mult), reads=[d_t1], writes=[d_t2])
                        PP.op("pe", mm(pbf32[:], obt[:], t2[:]), reads=[d_ob, d_t2], writes=[d_pbf])
                        PP.op("act", act(t2[:], pbf32[:], AF.Sqrt), reads=[d_pbf], writes=[d_t2])
                        PP.op("dve", ts(t2[:], t2[:], 1e-12, None, ALU.max), reads=[d_t2], writes=[d_t2])
                        PP.op("dve", rcp(t2[:], t2[:]), reads=[d_t2], writes=[d_t2])
                        PP.op("dve", tt(t1[:], t1[:], t2[:], ALU.mult), reads=[d_t1, d_t2], writes=[d_t1])
                        PP.op("dve", ts(t3[:], a_sb[p_][:], ppt[:, 15 + p_:16 + p_], om[:, 15 + p_:16 + p_], ALU.mult,
                                       ALU.add), reads=[d_a[p_], d_pp, d_om], writes=[d_t3])
                        PP.op("dve", tt(t3[:], t3[:], zs[:, kI, :], ALU.mult), reads=[d_t3, d_zs[kI]], writes=[d_t3])
                        PP.op("dve", stt(AR[p_][:, :, 0, :], c3(t1[:]), -1.0, c3(Wp[p_][:]), ALU.mult, ALU.mult),
                             reads=[d_t1, d_Wp[p_]], writes=[d_AR[p_]])
                        PP.op("pool", tt(AR[p_][:, :, 1, :], c3(zs[:, rI, :]), c3(Wt[p_][:]), ALU.mult),
                             reads=[d_zs[rI], d_W[p_]], writes=[d_AR[p_]])
                        PP.op("pool", tt(BK[p_][:, :, 0, :], c3(t3[:]), c3(Wi[p_][:]), ALU.mult),
                             reads=[d_t3, d_Wi[p_]], writes=[d_BK[p_]])
                        PP.op("dve", tt(t2[:], t1[:], a_sb[p_][:], ALU.mult), reads=[d_t1, d_a[p_]], writes=[d_t2])
                        PP.op("dve", tt(BK[p_][:, :, 1, :], c3(t2[:]), c3(Wi[p_][:]), ALU.mult),
                             reads=[d_t2, d_Wi[p_]], writes=[d_BK[p_]])
                        for c in range(SEG):
                            eng = "dve" if c % 2 == 0 else "pool"
                            PP.op(eng, ts(BKh[p_][:, c, :, :], BK[p_][:, c, :, :],
                                         Wt[p_][:, c * C + C - 1:c * C + C]), reads=[d_BK[p_], d_W[p_]],
                                 writes=[d_BKh[p_]])
                        PP.op("dve", stt(t2[:], zs[:, rI, :], ppt[:, 17 + p_:18 + p_], t3[:], ALU.mult, ALU.mult),
                             reads=[d_zs[rI], d_pp, d_t3], writes=[d_t2])
                        PP.op("pe", mm(pbf32[:], obt[:], t2[:]), reads=[d_ob, d_t2], writes=[d_pbf])
                        PP.op("dve", tt(bonT[p_][:], pbf32[:], zs[:, vI, :], ALU.mult), reads=[d_pbf, d_zs[vI]],
                             writes=[d_bon[p_]])
                        PP.dma("sp", dm(ARo[p_][:], AR[p_][64:128]), reads=[d_AR[p_]], writes=[d_ARo[p_]])
                        PP.dma("sp", dm(BKo[p_][:], BK[p_][64:128]), reads=[d_BK[p_]], writes=[d_BKo[p_]])
                        PP.dma("sp", dm(BKho[p_][:], BKh[p_][64:128]), reads=[d_BKh[p_]], writes=[d_BKho[p_]])
                        PP.dma("sp", (lambda e, p_=p_: e.dma_start(
                            out=WCo[p_][:], in_=Wt[p_][64:128, :].rearrange("p (c t) -> p c t", t=C)[:, :, C - 1],
                            allow_slow_non_contiguous=True)),
                              reads=[d_W[p_]], writes=[d_WCo[p_]])
                    for p_ in range(2):
                        PP.op("pool", cp(WC4_2[par][:, :, 2 * p_],
                                         Wt[p_][0:64, :].rearrange("p (c t) -> p c t", t=C)[:, :, C - 1]),
                              reads=[d_W[p_]], writes=[d_WC4_2[par]])
                        PP.op("pool", cp(WC4_2[par][:, :, 2 * p_ + 1], WCo[p_][:]), reads=[d_WCo[p_]],
                              writes=[d_WC4_2[par]])


                def emit_out(PP, hg, seg, par):
                    tok0 = seg * SEGT
                    Wt, d_W, WCo, d_WCo, AR, d_AR, ARo, d_ARo, bonT, d_bon, sgT, d_sgT = bufs[par]
                    for c in range(SEG):
                        yv = Ysb[:, c, :].rearrange("p (h v) -> p h v", h=NH)
                        PP.op("dve", lambda e, yv=yv: e.tensor_reduce(out=stt_[:, 0:4], in_=yv, axis=AX.X, op=ALU.add),
                             reads=[d_Y[c]], writes=[d_stt])
                        PP.op("pool", tt(ysq[:], Ysb[:, c, :], Ysb[:, c, :], ALU.mult), reads=[d_Y[c]], writes=[d_ysq])
                        PP.op("dve", lambda e: e.tensor_reduce(out=stt_[:, 4:8],
                                                              in_=ysq[:].rearrange("p (h v) -> p h v", h=NH),
                                                              axis=AX.X, op=ALU.add), reads=[d_ysq], writes=[d_stt])
                        PP.op("dve", ts(stt_[:, 0:8], stt_[:, 0:8], 1.0 / C), reads=[d_stt], writes=[d_stt])
                        PP.op("dve", tt(ysq[:, 0:4], stt_[:, 0:4], stt_[:, 0:4], ALU.mult), reads=[d_stt],
                             writes=[d_ysq])
                        PP.op("dve", tt(stt_[:, 4:8], stt_[:, 4:8], ysq[:, 0:4], ALU.subtract), reads=[d_stt, d_ysq],
                             writes=[d_stt])
                        PP.op("act", act(stt_[:, 4:8], stt_[:, 4:8], AF.Sqrt, bias=GN_EPS), reads=[d_stt],
                             writes=[d_stt])
                        PP.op("dve", rcp(stt_[:, 4:8], stt_[:, 4:8]), reads=[d_stt], writes=[d_stt])
                        for h in range(NH):
                            PP.op("dve", ts(Ysb[:, c, h * C:(h + 1) * C], Ysb[:, c, h * C:(h + 1) * C],
                                           stt_[:, h:h + 1], stt_[:, 4 + h:5 + h], ALU.subtract, ALU.mult),
                                 reads=[d_Y[c], d_stt], writes=[d_Y[c]])
                        PP.op("pool", tt(Ysb[:, c, :], Ysb[:, c, :], lnt4[:, hg, 0:256], ALU.mult), reads=[d_Y[c], d_ln],
                             writes=[d_Y[c]])
                        PP.op("pool", tt(Ysb[:, c, :], Ysb[:, c, :], lnt4[:, hg, 256:512], ALU.add), reads=[d_Y[c], d_ln],
                             writes=[d_Y[c]])
                    for p_ in range(2):
                        b_ = 4 + p_
                        for c in range(SEG):
                            PP.op("pe", tr(banks[b_][:, c * C:(c + 1) * C], Ysb[:, c, p_ * 128:(p_ + 1) * 128],
                                          idf[0:64, 0:64]), reads=[d_Y[c], d_idf], writes=[bdep[b_]])
                        PP.op("dve", tt(oT[:], banks[b_][:], bonT[p_][:], ALU.add), reads=[bdep[b_], d_bon[p_]],
                             writes=[d_oT])
                        PP.op("dve", tt(oT[:], oT[:], sgT[p_][:], ALU.mult), reads=[d_oT, d_sgT[p_]], writes=[d_oT])
                        if dbg:
                            PP.dma("sp", dm(orw[hg * 256 + p_ * 128:hg * 256 + (p_ + 1) * 128, tok0:tok0 + SEGT], oT[:]),
                                  reads=[d_oT])
                        PP.op("dve", ts(osel[:], oT[:, 0:128], s4t[:, 0:1]), reads=[d_oT, d_s4], writes=[d_osel])
                        for jj in range(1, 4):
                            PP.op("dve", stt(osel[:], oT[:, jj * 128:(jj + 1) * 128], s4t[:, jj:jj + 1], osel[:], ALU.mult,
                                            ALU.add), reads=[d_oT, d_s4, d_osel], writes=[d_osel])
                        PP.dma("sp", dm(mixT[hg * 256 + p_ * 128:hg * 256 + (p_ + 1) * 128, seg * 128:(seg + 1) * 128],
                                       osel[:]), reads=[d_osel])

                eq = Defer(P)
                per_e = 0
                units = [(a_, b_) for a_ in range(4) for b_ in range(T // SEGT)]
                emit_load(P, units[0][0], units[0][1])
                emit_prep(P, units[0][0], units[0][1], 0)
                for n_, (hg, seg) in enumerate(units):
                    tok0 = seg * SEGT
                    par = n_ % 2
                    Wt, d_W, WCo, d_WCo, AR, d_AR, ARo, d_ARo, bonT, d_bon, sgT, d_sgT = bufs[par]
                    if n_ + 1 < len(units):
                        emit_load(P, units[n_ + 1][0], units[n_ + 1][1])
                    nn2 = lambda ap: ap.rearrange("p (a h v) -> p a h v", a=2, h=NH)
                    sqb = [(banks[4], bdep[4]), (banks[6], bdep[6])]
                    pqb = [(banks[5], bdep[5]), (pbf32, d_pbf)]

                    def pre0(c, NN, d_NN, PQ, d_PQ):
                        for p_ in range(2):
                            P.op("pe", tr(banks[0][0:64, p_ * 128:(p_ + 1) * 128], zs[:, 4 + p_, c * C:(c + 1) * C],
                                          idf[:]), reads=[d_zs[4 + p_], d_idf], writes=[bdep[0]])
                        P.op("act", act(UV[0:64, c, :, :], banks[0][0:64, 0:256].rearrange("p (h v) -> p h v", h=NH),
                                        AF.Copy), reads=[bdep[0]], writes=[d_UV[c]])
                        for h in range(NH):
                            tl_, p_, q_ = hAP(BKh, BKho, h)
                            P.op("pe", tr(banks[1][:, h * C:(h + 1) * C],
                                          tl_[0:64, c, :, :].rearrange("p a t -> p (a t)"), idf[0:64, 0:64]),
                                 reads=[d_BKh[p_] if q_ == 0 else d_BKho[p_], d_idf], writes=[bdep[1]])
                        P.op("act", act(BKt[:, c, :, :], banks[1][:, 0:256].rearrange("p (h v) -> p h v", h=NH),
                                        AF.Copy), reads=[bdep[1]], writes=[d_BKt[c]])
                        for h in range(NH):
                            bk_, p_, q_ = hAP(BK, BKo, h)
                            ar_, _, _ = hAP(AR, ARo, h)
                            rd = [d_BK[p_] if q_ == 0 else d_BKo[p_], d_AR[p_] if q_ == 0 else d_ARo[p_]]
                            P.op("pe", mm(banks[2][:, h * 128:(h + 1) * 128],
                                          bk_[0:64, c, :, :].rearrange("p a t -> p (a t)"),
                                          ar_[0:64, c, :, :].rearrange("p a t -> p (a t)")), reads=rd,
                                 writes=[bdep[2]])
                            P.op("pe", mm(banks[3][0:64, h * C:(h + 1) * C], ar_[0:64, c, 0, :], bk_[0:64, c, 1, :]),
                                 reads=rd, writes=[bdep[3]])
                            P.op("pe", mm(banks[3][0:64, 256 + h * C:256 + (h + 1) * C], bk_[0:64, c, 1, :],
                                          ar_[0:64, c, 0, :]), reads=rd, writes=[bdep[3]])
                        P.op("dve", tt(ATs[:, c, :, :], banks[2][:].rearrange("p (h v) -> p h v", h=NH),
                                       mAT[:].rearrange("p (h v) -> p h v", h=NH), ALU.mult),
                             reads=[bdep[2], d_mAT], writes=[d_ATs[c]])
                        P.op("dve", tt(NN[:], nn2(banks[3][0:64, :]), nn2(mNN[:]), ALU.mult),
                             reads=[bdep[3], d_mNN], writes=[d_NN])
                        P.op("pool", tt(PQ[:, 0, :, :], NN[:, 1, :, :], nn2(i4[:])[:, 0, :, :], ALU.add),
                             reads=[d_NN, d_i4], writes=[d_PQ])
                        P.op("pool", tt(PQ[:, 1, :, :], NN[:, 0, :, :], nn2(i4[:])[:, 1, :, :], ALU.add),
                             reads=[d_NN, d_i4], writes=[d_PQ])

                    for c2 in range(0, SEG, 2):
                        pair = [(c2 + k_, NNl[k_], d_NNl[k_], PQl[k_], d_PQl[k_], sqb[k_], pqb[k_]) for k_ in range(2)]
                        for (c, NN, d_NN, PQ, d_PQ, _, _) in pair:
                            pre0(c, NN, d_NN, PQ, d_PQ)
                        for lev in range(1, 6):
                            last = lev == 5
                            for (c, NN, d_NN, PQ, d_PQ, (sq, d_sq), _) in pair:
                                for h in range(NH):
                                    if not last:
                                        P.op("pe", mm(sq[0:64, h * C:(h + 1) * C], NN[:, 1, h, :], NN[:, 0, h, :]),
                                             reads=[d_NN], writes=[d_sq])
                                    P.op("pe", mm(sq[0:64, 256 + h * C:256 + (h + 1) * C], NN[:, 0, h, :],
                                                  NN[:, 1, h, :]), reads=[d_NN], writes=[d_sq])
                            for (c, NN, d_NN, PQ, d_PQ, (sq, d_sq), _) in pair:
                                if not last:
                                    P.op("act", act(NN[:], nn2(sq[0:64, :]), AF.Copy), reads=[d_sq], writes=[d_NN])
                                else:
                                    P.op("act", act(NN[:, 1, :, :], nn2(sq[0:64, :])[:, 1, :, :], AF.Copy),
                                         reads=[d_sq], writes=[d_NN])
                            for (c, NN, d_NN, PQ, d_PQ, _, (pq, d_pq)) in pair:
                                for h in range(NH):
                                    P.op("pe", mm(pq[0:64, h * C:(h + 1) * C], PQ[:, 1, h, :], NN[:, 1, h, :]),
                                         reads=[d_PQ, d_NN], writes=[d_pq])
                                    if not last:
                                        P.op("pe", mm(pq[0:64, 256 + h * C:256 + (h + 1) * C], PQ[:, 0, h, :],
                                                      NN[:, 0, h, :]), reads=[d_PQ, d_NN], writes=[d_pq])
                            for (c, NN, d_NN, PQ, d_PQ, _, (pq, d_pq)) in pair:
                                if not last:
                                    P.op("dve", tt(PQ[:], PQ[:], nn2(pq[0:64, :]), ALU.add), reads=[d_PQ, d_pq],
                                         writes=[d_PQ])
                                else:
                                    P.op("dve", tt(TT[:, c, :, C:2 * C], PQ[:, 0, :, :], nn2(pq[0:64, :])[:, 0, :, :],
                                                   ALU.add), reads=[d_PQ, d_pq], writes=[d_TT[c]])
                        eq.run(per_e)
                    eq.run()

                    dq = Defer(P)
                    if n_ + 1 < len(units):
                        emit_prep(dq, units[n_ + 1][0], units[n_ + 1][1], 1 - par)
                    per = (len(dq.q) + SEG - 1) // SEG
                    if seg == 0:
                        P.op("dve", mset(Hs[:], 0.0), writes=[d_Hs])
                        P.op("dve", mset(Hb2[0][:], 0.0), writes=[d_Hb2[0]])
                    for c in range(SEG):
                        Hb, d_Hb = Hb2[c % 2], d_Hb2[c % 2]
                        Hbn, d_Hbn = Hb2[(c + 1) % 2], d_Hb2[(c + 1) % 2]
                        for h in range(NH):
                            ar_, p_, q_ = hAP(AR, ARo, h)
                            rd = [d_AR[p_] if q_ == 0 else d_ARo[p_], d_Hb]
                            P.op("pe", mm(banks[0][0:64, h * C:(h + 1) * C], ar_[0:64, c, 0, :], Hb[:, h, :],
                                          start=True, stop=False), reads=rd, writes=[bdep[0]])
                            P.op("pe", mm(banks[0][0:64, h * C:(h + 1) * C], ATs[0:64, c, h, 0:C], UV[0:64, c, h, :],
                                          start=False, stop=True), reads=[d_ATs[c], d_UV[c]], writes=[bdep[0]])
                        P.op("act", act(Xs[:], banks[0][0:64, 0:256].rearrange("p (h v) -> p h v", h=NH), AF.Copy),
                             reads=[bdep[0]], writes=[d_Xs])
                        P.op("dve", tt(tmpH[:], Hs[:], WC4_2[par][:, c, :].unsqueeze(2).to_broadcast([64, NH, C]), ALU.mult),
                             reads=[d_Hs, d_WC4_2[par]], writes=[d_tmpH])
                        for h in range(NH):
                            P.op("pe", mm(banks[1][:, h * C:(h + 1) * C], TT[:, c, h, :], Xs[:, h, :]),
                                 reads=[d_TT[c], d_Xs], writes=[bdep[1]])
                        P.op("act", act(UV[64:128, c, :, :],
                                        banks[1][64:128, 0:256].rearrange("p (h v) -> p h v", h=NH), AF.Copy),
                             reads=[bdep[1]], writes=[d_UV[c]])
                        for h in range(NH):
                            P.op("pe", mm(banks[3][0:64, h * C:(h + 1) * C], BKt[:, c, h, :], UV[:, c, h, :]),
                                 reads=[d_BKt[c], d_UV[c]], writes=[bdep[3]])
                        for h in range(NH):
                            ar_, p_, q_ = hAP(AR, ARo, h)
                            rd = [d_AR[p_] if q_ == 0 else d_ARo[p_], d_Hb]
                            P.op("pe", mm(banks[2][0:64, h * C:(h + 1) * C], ar_[0:64, c, 1, :], Hb[:, h, :],
                                          start=True, stop=False), reads=rd, writes=[bdep[2]])
                            P.op("pe", mm(banks[2][0:64, h * C:(h + 1) * C], ATs[:, c, h, C:2 * C], UV[:, c, h, :],
                                          start=False, stop=True), reads=[d_ATs[c], d_UV[c]], writes=[bdep[2]])
                        P.op("act", act(Ysb[:, c, :], banks[2][0:64, 0:256], AF.Copy), reads=[bdep[2]],
                             writes=[d_Y[c]])
                        st3 = banks[3][0:64, 0:256].rearrange("p (h v) -> p h v", h=NH)
                        P.op("dve", tt(Hbn[:], tmpH[:], st3, ALU.add), reads=[d_tmpH, bdep[3]], writes=[d_Hbn])
                        P.op("dve", tt(Hs[:], tmpH[:], st3, ALU.add), reads=[d_tmpH, bdep[3]], writes=[d_Hs])
                        dq.run(per)

                    dq.run()
                    eq = Defer(P)
                    emit_out(eq, hg, seg, par)
                    per_e = (len(eq.q) + 3) // 4
                eq.run()
                P.barrier()
                P.flush()

        if 3 in phases:
            with ExitStack() as es:
                sb = lambda name, shape, dt=F32: es.enter_context(nc.sbuf_tensor(name, list(shape), dt))
                idf = sb("idf3", [128, 128]); d_idf = Dep()
                P.dma("sp", dm(idf[:], ident_f), writes=[d_idf])
                ckvT_sb = sb("ckvT_sb", [128, 2, T], BF16); d_ckvT = Dep()
                ckv_sb = sb("ckv_sb", [128, NB, 256], BF16); d_ckv = Dep()
                kiTb = sb("kiTb", [128, T], BF16); d_kiT = Dep()
                stage = sb("stage3", [128, T]); d_stage = Dep()
                wuq_b = sb("wuq_b", [128, 3, 1024], BF16); d_wuq = Dep()
                iwq_b = sb("iwq_b", [128, 3, 1024], BF16); d_iwq = Dep()
                ukT_b = sb("ukT_b", [128, 16, 256], BF16); d_uk = Dep()
                uvp_b = sb("uvp_b", [128, 2, 16, 128], BF16); d_uv = Dep()
                b15 = sb("b15", [128, 2048]); d_b15 = Dep()
                hc = sb("hc", [128, NIT]); d_hc = Dep()
                ones_bf = sb("ones_bf", [128, 128], BF16); d_ones = Dep()
                P.op("dve", mset(ones_bf[:], 1.0), writes=[d_ones])
                for k2 in range(2):
                    P.dma("sp", dm(ckvT_sb[:, k2, :], ckvT[k2 * 128:(k2 + 1) * 128, :]), writes=[d_ckvT])
                P.dma("sp", dm(ckv_sb[:], ckv_tok.rearrange("(kb p) r -> p kb r", p=128)), writes=[d_ckv])
                P.dma("sp", dm(b15[:], b15rep), writes=[d_b15])
                P.dma("sp", dm(hc[:], halfc), writes=[d_hc])
                P.dma("sp", dm(stage[:], kiT2), writes=[d_stage])
                P.op("dve", cp(kiTb[:], stage[:]), reads=[d_stage], writes=[d_kiT])
                for wsrc, wdst, dd in ((w_uq, wuq_b, d_wuq), (iw_q, iwq_b, d_iwq)):
                    P.dma("sp", dm(stage[:, 0:3072].rearrange("p (k n) -> p k n", k=3),
                                   wsrc.rearrange("(k p) n -> p k n", p=128)), writes=[d_stage])
                    P.op("dve", cp(wdst[:], stage[:, 0:3072].rearrange("p (k n) -> p k n", k=3)), reads=[d_stage],
                         writes=[dd])
                P.dma("sp", dm(stage[:, 0:4096], ukT), writes=[d_stage])
                P.op("dve", cp(ukT_b[:], stage[:, 0:4096].rearrange("p (a r) -> p a r", a=16)), reads=[d_stage],
                     writes=[d_uk])
                P.dma("sp", dm(stage[:, 0:4096], uvp), writes=[d_stage])
                P.op("dve", cp(uvp_b[:], stage[:, 0:4096].rearrange("p (c h m) -> p c h m", c=2, h=16)),
                     reads=[d_stage], writes=[d_uv])

                qlb = sb("qlb", [128, 3, 128], BF16); d_qlb = Dep()
                wi = sb("wi", [128, 48]); d_wi = Dep()
                qT_sb = sb("qT_sb", [128, 16, 128], BF16); d_qT = Dep()
                qiT_sb = sb("qiT_sb", [128, 16, 128], BF16); d_qiT = Dep()
                P.op("dve", mset(qT_sb[:], 0.0), writes=[d_qT])
                P.op("dve", mset(qiT_sb[:], 0.0), writes=[d_qiT])
                qaT = sb("qaT", [128, 2, 16, 128], BF16); d_qaT = Dep()
                acc = sb("acc", [128, T]); d_acc = Dep()
                rl = [sb("rl%d" % k, [128, 512]) for k in range(2)]; d_rl = [Dep(), Dep()]
                cj = sb("cj", [128, T], BF16); d_cj = Dep()
                msk = stage; d_msk = d_stage
                maskT = sb("maskT", [128, NB, 128], BF16); d_maskT = Dep()
                bs = sb("bs", [128, 8]); d_bs = Dep()
                hk = sb("hk", [128, NIT]); d_hk = Dep()
                mad = sb("mad", [128, 640]); d_mad = Dep()
                bv = [sb("bv%d" % k, [128, 512]) for k in range(2)]; d_bv = [Dep(), Dep()]
                lg = sb("lg", [128, 512]); d_lg = Dep()
                e_sb = [sb("e_sb%d" % k, [128, 512], BF16) for k in range(2)]; d_e = [Dep(), Dep()]
                pT = [sb("pT%d" % k, [128, 4, 128], BF16) for k in range(2)]; d_pT = [Dep(), Dep()]
                rs = sb("rs", [128, 512]); d_rs = Dep()
                on = sb("on", [128, 2, 4, 128], BF16); d_on = Dep()
                gd = sb("gd", [128, 128]); d_gd = Dep()
                mo = sb("mo", [128, 128]); d_mo = Dep()

                for i in range(OWNB):
                    L = 512 * (i + 1)
                    nkb = 4 * (i + 1)
                    wk0 = max(4 * i - 1, 0)
                    P.dma("sp", dm(qlb[:], qlT[:, i * 128:(i + 1) * 128].rearrange("(k p) t -> p k t", p=128)),
                          writes=[d_qlb])
                    P.dma("sp", dm(wi[:, 0:16], widx[i * 128:(i + 1) * 128, :]), writes=[d_wi])
                    P.op("act", act(wi[:, 32:48], wi[:, 0:16], AF.Sign), reads=[d_wi], writes=[d_wi])
                    P.op("dve", tt(wi[:, 16:32], wi[:, 0:16], wi[:, 32:48], ALU.mult), reads=[d_wi], writes=[d_wi])
                    for W_, dW, dst, ddst in ((wuq_b, d_wuq, qT_sb, d_qT), (iwq_b, d_iwq, qiT_sb, d_qiT)):
                        for hb in range(4):
                            b_ = hb % 2
                            for hl in range(4):
                                h = hb * 4 + hl
                                for kc in range(3):
                                    P.op("pe", mm(banks[b_][0:64, hl * 128:(hl + 1) * 128], W_[:, kc, h * 64:(h + 1) * 64],
                                                  qlb[:, kc, :], start=(kc == 0), stop=(kc == 2)),
                                         reads=[dW, d_qlb], writes=[bdep[b_]])
                            P.op("act", act(dst[0:64, hb * 4:(hb + 1) * 4, :],
                                            banks[b_][0:64, :].rearrange("p (k t) -> p k t", k=4), AF.Copy),
                                 reads=[bdep[b_]], writes=[ddst])
                    if K3STOP < 1:
                        continue
                    for rc in range(2):
                        for hb in range(4):
                            b_ = 5 + (hb % 2)
                            for hl in range(4):
                                h = hb * 4 + hl
                                P.op("pe", mm(banks[b_][:, hl * 128:(hl + 1) * 128],
                                              ukT_b[:, h, rc * 128:(rc + 1) * 128], qT_sb[:, h, :]),
                                     reads=[d_uk, d_qT], writes=[bdep[b_]])
                            P.op("dve", ts(qaT[:, rc, hb * 4:(hb + 1) * 4, :],
                                           banks[b_][:].rearrange("p (k t) -> p k t", k=4), 0.125),
                                 reads=[bdep[b_]], writes=[d_qaT])
                    if K3STOP < 2:
                        continue
                    for st_ in range(i + 1):
                        for h in range(16):
                            b_ = h % 2
                            P.op("pe", mm(banks[b_][:], qiT_sb[:, h, :], kiTb[:, st_ * 512:(st_ + 1) * 512]),
                                 reads=[d_qiT, d_kiT], writes=[bdep[b_]])
                            P.op("act", act(rl[b_][:], banks[b_][:], AF.Relu, scale=wi[:, 16 + h:17 + h]),
                                 reads=[bdep[b_], d_wi], writes=[d_rl[b_]])
                            a_ = acc[:, st_ * 512:(st_ + 1) * 512]
                            if h == 0:
                                P.op("dve", ts(a_, rl[b_][:], wi[:, 32:33]), reads=[d_rl[b_], d_wi], writes=[d_acc])
                            else:
                                P.op("dve", stt(a_, rl[b_][:], wi[:, 32 + h:33 + h], a_, ALU.mult, ALU.add),
                                     reads=[d_rl[b_], d_wi, d_acc], writes=[d_acc])
                    if K3STOP < 3:
                        continue
                    P.op("dve", lambda e, L=L: e.tensor_reduce(out=bs[:, 0:1], in_=acc[:, 0:L], axis=AX.X, op=ALU.max),
                         reads=[d_acc], writes=[d_bs])
                    P.op("dve", lambda e, L=L: e.tensor_reduce(out=bs[:, 1:2], in_=acc[:, 0:L], axis=AX.X, op=ALU.min),
                         reads=[d_acc], writes=[d_bs])
                    P.op("dve", ts(bs[:, 3:4], bs[:, 1:2], -1.0, None, ALU.add), reads=[d_bs], writes=[d_bs])
                    P.op("dve", tt(bs[:, 2:3], bs[:, 0:1], bs[:, 1:2], ALU.subtract), reads=[d_bs], writes=[d_bs])
                    P.op("dve", ts(bs[:, 2:3], bs[:, 2:3], 2.0, None, ALU.add), reads=[d_bs], writes=[d_bs])
                    P.op("dve", ts(hk[:], hc[:], bs[:, 2:3]), reads=[d_hc, d_bs], writes=[d_hk])
                    P.dma("sp", dm(mad[:], madd[i * 128:(i + 1) * 128, :]), writes=[d_mad])
                    wcol0 = 128 if i == 0 else 0
                    P.op("dve", tt(acc[:, wk0 * 128:L], acc[:, wk0 * 128:L], mad[:, wcol0:640], ALU.add),
                         reads=[d_acc, d_mad], writes=[d_acc])
                    P.op("dve", tt(bs[:, 4:5], bs[:, 3:4], hk[:, 0:1], ALU.add), reads=[d_bs, d_hk], writes=[d_bs])
                    for it in range(NIT):
                        P.op("dve", tsa(cj[:, 0:L], acc[:, 0:L], bs[:, 4:5], 0.0, ALU.is_ge, ALU.add, bs[:, 5:6]),
                             reads=[d_acc, d_bs], writes=[d_cj, d_bs])
                        P.op("dve", ts(bs[:, 6:7], bs[:, 5:6], KSEL - 0.5, hk[:, it:it + 1], ALU.is_ge, ALU.mult),
                             reads=[d_bs, d_hk], writes=[d_bs])
                        if it < NIT - 1:
                            P.op("dve", stt(bs[:, 4:5], bs[:, 6:7], hk[:, it + 1:it + 2], bs[:, 4:5], ALU.subtract, ALU.add),
                                 reads=[d_bs, d_hk], writes=[d_bs])
                        else:
                            P.op("dve", stt(bs[:, 3:4], bs[:, 6:7], hk[:, it:it + 1], bs[:, 4:5], ALU.subtract, ALU.add),
                                 reads=[d_bs, d_hk], writes=[d_bs])
                    P.op("dve", ts(msk[:, 0:L], acc[:, 0:L], bs[:, 3:4], None, ALU.is_ge), reads=[d_acc, d_bs],
                         writes=[d_msk])
                    if K3STOP < 4:
                        continue
                    for kg in range(nkb // 4):
                        for k4 in range(4):
                            kb = kg * 4 + k4
                            P.op("pe", tr(banks[4][:, k4 * 128:(k4 + 1) * 128], msk[:, kb * 128:(kb + 1) * 128], idf[:]),
                                 reads=[d_msk, d_idf], writes=[bdep[4]])
                        P.op("act", act(maskT[:, kg * 4:(kg + 1) * 4, :],
                                        banks[4][:].rearrange("p (k t) -> p k t", k=4), AF.Copy),
                             reads=[bdep[4]], writes=[d_maskT])
                    if K3STOP < 5:
                        continue
                    for hq in range(4):
                        for kb in range(nkb):
                            b_ = kb % 2
                            for rc in range(2):
                                P.op("pe", mm(banks[b_][:], ckvT_sb[:, rc, kb * 128:(kb + 1) * 128],
                                              qaT[:, rc, hq * 4:(hq + 1) * 4, :].rearrange("p h t -> p (h t)"),
                                              start=(rc == 0), stop=(rc == 1)), reads=[d_ckvT, d_qaT],
                                     writes=[bdep[b_]])
                            if kb >= wk0:
                                w_ = kb - (4 * i - 1)
                                r0 = (i * 5 + w_) * 128
                                P.dma("sp", dm(bv[b_][:], biasvar[r0:r0 + 128, hq * 512:(hq + 1) * 512]),
                                      writes=[d_bv[b_]])
                                P.op("pool", tt(bv[b_][:], bv[b_][:], b15[:, hq * 512:(hq + 1) * 512], ALU.subtract),
                                     reads=[d_bv[b_], d_b15], writes=[d_bv[b_]])
                                P.op("dve", tt(lg[:], banks[b_][:], bv[b_][:], ALU.add), reads=[bdep[b_], d_bv[b_]],
                                     writes=[d_lg])
                                P.op("act", act(e_sb[b_][:], lg[:], AF.Exp), reads=[d_lg], writes=[d_e[b_]])
                            else:
                                P.op("act", act(e_sb[b_][:], banks[b_][:], AF.Exp), reads=[bdep[b_]], writes=[d_e[b_]])
                            P.op("dve", tt(pT[b_][:], e_sb[b_][:].rearrange("p (h t) -> p h t", h=4),
                                           maskT[:, kb, :].unsqueeze(1).to_broadcast([128, 4, 128]), ALU.mult),
                                 reads=[d_e[b_], d_maskT], writes=[d_pT[b_]])
                            pflat = pT[b_][:].rearrange("p h t -> p (h t)")
                            for rc in range(2):
                                P.op("pe", mm(banks[2 + rc][:], ckv_sb[:, kb, rc * 128:(rc + 1) * 128], pflat,
                                              start=(kb == 0), stop=(kb == nkb - 1)), reads=[d_ckv, d_pT[b_]],
                                     writes=[bdep[2 + rc]])
                            P.op("pe", mm(banks[5][:], ones_bf[:], pflat, start=(kb == 0), stop=(kb == nkb - 1)),
                                 reads=[d_ones, d_pT[b_]], writes=[bdep[5]])
                        P.op("dve", rcp(rs[:], banks[5][:]), reads=[bdep[5]], writes=[d_rs])
                        for rc in range(2):
                            P.op("dve", tt(on[:, rc, :, :].rearrange("p h t -> p (h t)"), banks[2 + rc][:], rs[:],
                                           ALU.mult), reads=[bdep[2 + rc], d_rs], writes=[d_on])
                        for pr in range(2):
                            for hl2 in range(2):
                                hl = pr * 2 + hl2
                                h = hq * 4 + hl
                                for rc in range(2):
                                    P.op("pe", mm(banks[6][:, pr * 128:(pr + 1) * 128], uvp_b[:, rc, h, :],
                                                  on[:, rc, hl, :], start=(hl2 == 0 and rc == 0),
                                                  stop=(hl2 == 1 and rc == 1)), reads=[d_uv, d_on], writes=[bdep[6]])
                        for pr in range(2):
                            fch = hq * 2 + pr
                            P.dma("sp", dm(gd[:], gdsT[fch * 128:(fch + 1) * 128, i * 128:(i + 1) * 128]), writes=[d_gd])
                            P.op("act", act(gd[:], gd[:], AF.Silu), reads=[d_gd], writes=[d_gd])
                            P.op("dve", tt(mo[:], banks[6][:, pr * 128:(pr + 1) * 128], gd[:], ALU.mult),
                                 reads=[bdep[6], d_gd], writes=[d_mo])
                            P.dma("sp", dm(mixT[1024 + fch * 128:1024 + (fch + 1) * 128, i * 128:(i + 1) * 128], mo[:]),
                                  reads=[d_mo])
                P.barrier()
                P.flush()

        if (2 not in phases or 3 not in phases) and not mix_ext:
            with ExitStack() as es:
                zt = es.enter_context(nc.sbuf_tensor("zt", [128, TO], F32)); d_zt = Dep()
                P.op("dve", mset(zt[:], 0.0), writes=[d_zt])
                ks = ([] if 2 in phases else list(range(8))) + ([] if 3 in phases else list(range(8, KC)))
                for k in ks:
                    P.dma("sp", dm(mixT[k * 128:(k + 1) * 128, :], zt[:]), reads=[d_zt])
                P.barrier()
                P.flush()

        if 4 in phases:
            with ExitStack() as es:
                sb = lambda name, shape, dt=F32: es.enter_context(nc.sbuf_tensor(name, list(shape), dt))
                idf = sb("idf4", [128, 128]); d_idf = Dep()
                P.dma("sp", dm(idf[:], ident_f), writes=[d_idf])
                Wf = sb("Wf", [128, KC, D], BF16); d_Wf = Dep()
                Pw = sb("Pw", [128, 2, D], BF16); d_Pw = Dep()
                stg = [sb("stg4_%d" % i, [128, D]) for i in range(2)]
                d_stg = [Dep(), Dep()]
                h_all = sb("h_all", [128, OWNB, D]); d_h = [Dep() for _ in range(OWNB)]
                mx = sb("mx", [128, KC, 128]); d_mx = Dep()
                mxb = sb("mxb", [128, KC, 128], BF16); d_mxb = Dep()
                xt = sb("xt4", [128, D]); d_xt = Dep()
                fr = sb("fr", [128, D]); d_fr = Dep()
                sg = sb("sg", [128, 1024]); d_sg = Dep()
                pt_ = sb("pt4", [128, 256]); d_pt = Dep()
                pTb = sb("pTb", [128, 2, 128], BF16); d_pTb = Dep()
                st = sb("st4", [128, 4]); d_st = Dep()
                junk = sb("junk4", [128, D], BF16); d_junk = Dep()
                P.dma("sp", dm(fr[:], fin_row), writes=[d_fr])

                def load_w(wsrc, nk, dst, d_dst):
                    for kc in range(nk):
                        s_ = kc % 2
                        P.dma("sp", dm(stg[s_][:], wsrc[kc * 128:(kc + 1) * 128, :]), writes=[d_stg[s_]])
                        eng = "dve" if kc % 2 == 0 else "pool"
                        P.op(eng, cp(dst[:, kc, :], stg[s_][:]), reads=[d_stg[s_]], writes=[d_dst])

                load_w(w_out, KC, Wf, d_Wf)
                for i in range(OWNB):
                    P.dma("sp", dm(mx[:], mixT[:, i * 128:(i + 1) * 128].rearrange("(k p) t -> p k t", p=128)),
                          writes=[d_mx])
                    P.op("pool", cp(mxb[:], mx[:]), reads=[d_mx], writes=[d_mxb])
                    P.dma("sp", dm(xt[:], x_own[i * 128:(i + 1) * 128, :]), writes=[d_xt])
                    for n4 in range(4):
                        for kc in range(KC):
                            P.op("pe", mm(banks[n4][:], mxb[:, kc, :], Wf[:, kc, n4 * 512:(n4 + 1) * 512],
                                          start=(kc == 0), stop=(kc == KC - 1)), reads=[d_mxb, d_Wf],
                                 writes=[bdep[n4]])
                        P.op("dve", tt(h_all[:, i, n4 * 512:(n4 + 1) * 512], banks[n4][:],
                                       xt[:, n4 * 512:(n4 + 1) * 512], ALU.add), reads=[bdep[n4], d_xt],
                             writes=[d_h[i]])
                load_w(w_gate, KC, Wf, d_Wf)
                load_w(w_ple, 2, Pw, d_Pw)
                for i in range(OWNB):
                    for g in range(4):
                        for q in range(4):
                            kc = g * 4 + q
                            P.op("pe", tr(banks[4][:, q * 128:(q + 1) * 128], h_all[:, i, kc * 128:(kc + 1) * 128],
                                          idf[:]), reads=[d_h[i], d_idf], writes=[bdep[4]])
                        P.op("act", act(mxb[:, g * 4:(g + 1) * 4, :], banks[4][:].rearrange("p (k t) -> p k t", k=4),
                                        AF.Copy), reads=[bdep[4]], writes=[d_mxb])
                    P.dma("sp", dm(pt_[:], p_own[i * 128:(i + 1) * 128, :]), writes=[d_pt])
                    for k2 in range(2):
                        P.op("pe", tr(banks[4][:, k2 * 128:(k2 + 1) * 128], pt_[:, k2 * 128:(k2 + 1) * 128], idf[:]),
                             reads=[d_pt, d_idf], writes=[bdep[4]])
                    P.op("act", act(pTb[:], banks[4][:, 0:256].rearrange("p (k t) -> p k t", k=2), AF.Copy),
                         reads=[bdep[4]], writes=[d_pTb])
                    for half in range(2):
                        for n2 in range(2):
                            n0 = half * 1024 + n2 * 512
                            for kc in range(KC):
                                P.op("pe", mm(banks[n2][:], mxb[:, kc, :], Wf[:, kc, n0:n0 + 512],
                                              start=(kc == 0), stop=(kc == KC - 1)), reads=[d_mxb, d_Wf],
                                     writes=[bdep[n2]])
                            P.op("act", act(sg[:, n2 * 512:(n2 + 1) * 512], banks[n2][:], AF.Sigmoid),
                                 reads=[bdep[n2]], writes=[d_sg])
                            for k2 in range(2):
                                P.op("pe", mm(banks[2 + n2][:], pTb[:, k2, :], Pw[:, k2, n0:n0 + 512],
                                              start=(k2 == 0), stop=(k2 == 1)), reads=[d_pTb, d_Pw],
                                     writes=[bdep[2 + n2]])
                            P.op("dve", tt(sg[:, n2 * 512:(n2 + 1) * 512], banks[2 + n2][:],
                                           sg[:, n2 * 512:(n2 + 1) * 512], ALU.mult), reads=[bdep[2 + n2], d_sg],
                                 writes=[d_sg])
                            P.op("pool", tt(h_all[:, i, n0:n0 + 512], h_all[:, i, n0:n0 + 512],
                                            sg[:, n2 * 512:(n2 + 1) * 512], ALU.add), reads=[d_sg, d_h[i]],
                                 writes=[d_h[i]])
                    P.op("act", act(junk[:], h_all[:, i, :], AF.Square, accum=st[:, 0:1]), reads=[d_h[i]],
                         writes=[d_junk, d_st])
                    P.op("act", act(st[:, 1:2], st[:, 0:1], AF.Sqrt, bias=EPS, scale=1.0 / D), reads=[d_st],
                         writes=[d_st])
                    P.op("dve", rcp(st[:, 2:3], st[:, 1:2]), reads=[d_st], writes=[d_st])
                    P.op("dve", stt(h_all[:, i, :], h_all[:, i, :], st[:, 2:3], fr[:], ALU.mult, ALU.mult),
                         reads=[d_h[i], d_st, d_fr], writes=[d_h[i]])
                    P.dma("sp", dm(out[i * 128:(i + 1) * 128, :], h_all[:, i, :]), reads=[d_h[i]])
                P.barrier()
                P.flush()

        P.barrier()
        P.flush()
    return nc


def _bucket_table():
    rel = np.arange(-4095, 4096)
    n = np.abs(rel)
    nf = np.maximum(n, 1).astype(np.float32)
    large = 8 + (np.log(nf / np.float32(8.0)) / np.float32(math.log(16.0)) * np.float32(8.0)).astype(np.int32)
    large = np.minimum(large, 15)
    return np.where(rel > 0, 16, 0) + np.where(n < 8, n, large)


def host_prep(inp, c):
    b, j = c // 4, c % 4
    f32 = np.float32
    w_in = inp["w_in"][0]
    heads = [4 * j + i for i in range(4)]
    hc = lambda base: np.concatenate([np.arange(base + h * 64, base + (h + 1) * 64) for h in heads])
    gcols = lambda g: np.concatenate([np.arange(base + h * 64, base + (h + 1) * 64)
                                      for base in (0, 1024, 2048, 3200) for h in range(4 * g, 4 * g + 4)])
    ta_cols = np.arange(4608, 4928)
    to_cols = np.concatenate([np.arange(4224, 4608), np.arange(4928, 4944)])
    fo_cols = np.arange(4944, 5968)
    toks = np.concatenate([np.arange(bk * 128, (bk + 1) * 128) for bk in own_blocks(j)])
    m = {}
    m["x_all"] = np.ascontiguousarray(inp["x"][b])
    m["x_own"] = np.ascontiguousarray(inp["x"][b][toks])
    m["wAg"] = np.ascontiguousarray(np.concatenate([w_in[:, gcols(g)] for g in range(4)], 0))
    m["wAx"] = np.ascontiguousarray(w_in[:, np.concatenate([np.arange(3072, 3200), ta_cols])])
    m["wO"] = np.ascontiguousarray(w_in[:, np.concatenate([to_cols, fo_cols])])
    m["g_col"] = np.ascontiguousarray(inp["norm_g"][0].reshape(KC, 128).T)
    m["ident_f"] = np.eye(128, dtype=f32)
    row = np.concatenate([inp["ds_kv_norm_g"][0], inp["idx_k_norm_g"][0], inp["ds_q_norm_g"][0]])
    m["rowc"] = np.ascontiguousarray(np.broadcast_to(row[None, :], (128, 704))).astype(f32)
    mu = inp["rw_mu"][0]
    ppm = np.zeros((4, 128, 24), f32)
    lwm = np.zeros((4, 128, 256), f32)
    w0m = np.zeros((4, 256), f32)
    lnm = np.zeros((4, 64, 512), f32)
    for g in range(4):
        own = np.arange(g * 256, (g + 1) * 256)
        gc_ = gcols(g)
        for ch in range(6):
            ppm[g, :, ch] = mu[gc_[ch * 128:(ch + 1) * 128]]
        ppm[g, :, 8] = mu[3072:3200]
        for p_ in range(2):
            oc = own[p_ * 128:(p_ + 1) * 128]
            ppm[g, :, 11 + p_] = inp["rw_a0"][0][oc]
            ppm[g, :, 13 + p_] = inp["rw_k_k"][0][oc]
            ppm[g, :, 15 + p_] = inp["rw_k_a"][0][oc]
            ppm[g, :, 17 + p_] = inp["rw_r_k"][0].reshape(-1)[oc]
        lwm[g, 0:64] = inp["rw_w_up"][0][:, own]
        lwm[g, 64:128] = inp["rw_a_up"][0][:, own]
        w0m[g] = inp["rw_w0"][0][own]
        lnm[g] = np.concatenate([inp["rw_ln_g"][0][own], inp["rw_ln_b"][0][own]])[None, :]
    m["pp"] = ppm.reshape(4 * 128, 24)
    m["lora_w"] = lwm.reshape(4 * 128, 256)
    m["w0_row"] = w0m
    m["lnrow"] = lnm.reshape(4 * 64, 512)
    s4 = np.zeros((128, 4), f32); s4[:, j] = 1.0
    m["sel4"] = s4
    ii = np.arange(128)
    same = (ii[:, None] // 64) == (ii[None, :] // 64)
    cdec = -math.exp(-0.5)
    m["tri"] = np.concatenate([(same & (ii[:, None] <= ii[None, :])), (same & (ii[:, None] < ii[None, :]))],
                              1).astype(f32) * f32(cdec)
    m["ones_blk"] = same.astype(f32)
    i6 = np.arange(64)
    lt = (i6[:, None] < i6[None, :]).astype(f32)
    le = (i6[:, None] <= i6[None, :]).astype(f32)
    mat = np.block([[lt, le], [lt, le]])
    m["maskAT4"] = np.ascontiguousarray(np.tile(mat, (1, 4)))
    m["maskNN"] = np.ascontiguousarray(np.concatenate([np.tile(lt.T, (1, 4)), np.tile(lt, (1, 4))], 1))
    m["id4"] = np.ascontiguousarray(np.tile(np.eye(64, dtype=f32), (1, 8)))
    m["w_uq"] = np.ascontiguousarray(inp["ds_w_uq"][0])
    m["iw_q"] = np.ascontiguousarray(inp["idx_w_q"][0])
    wuk = inp["ds_w_uk"][0]
    ukm = np.zeros((128, 16, 256), f32)
    ukm[0:64] = wuk.transpose(2, 1, 0)
    m["ukT"] = ukm.reshape(128, 16 * 256)
    wuv = inp["ds_w_uv"][0]
    uvpm = np.zeros((128, 2, 16, 128), f32)
    for h in range(16):
        q_ = h % 2
        uvpm[:, :, h, q_ * 64:(q_ + 1) * 64] = wuv[:, h, :].reshape(2, 128, 64).transpose(1, 0, 2)
    m["uvp"] = uvpm.reshape(128, 4096)
    bt = _bucket_table()
    maddm = np.full((OWNB, 128, 640), -1e30, f32)
    biasm = np.zeros((OWNB, 5, 128, 16, 128), f32)
    rb = inp["rel_bias"]
    for i in range(OWNB):
        blk = 4 * i + j
        tpos = blk * 128 + np.arange(128)
        limit = (tpos // 64 + 1) * 64
        for w_ in range(5):
            kb = 4 * i - 1 + w_
            if kb < 0:
                continue
            spos = kb * 128 + np.arange(128)
            adm = spos[None, :] < limit[:, None]
            maddm[i, :, w_ * 128:(w_ + 1) * 128] = np.where(adm, 0.0, -1e30)
            rel = spos[:, None] - tpos[None, :]
            biasm[i, w_] = rb[bt[rel + 4095]].transpose(0, 2, 1)
    m["madd"] = maddm.reshape(OWNB * 128, 640)
    m["biasvar"] = biasm.reshape(OWNB * 5 * 128, 2048)
    m["b15rep"] = np.ascontiguousarray(np.broadcast_to(np.repeat(rb[15], 128)[None, :], (128, 2048))).astype(f32)
    m["halfc"] = np.ascontiguousarray(np.broadcast_to((0.5 ** np.arange(1, NIT + 1))[None, :], (128, NIT))).astype(f32)
    m["p_own"] = np.ascontiguousarray(inp["p"][0, b][toks])
    m["w_out"] = np.ascontiguousarray(inp["w_out"][0])
    m["w_gate"] = np.ascontiguousarray(inp["ple_gate_w"][0])
    m["w_ple"] = np.ascontiguousarray(inp["ple_w"][0])
    m["fin_row"] = np.ascontiguousarray(np.broadcast_to(inp["final_g"][None, :], (128, D))).astype(f32)
    return m


def kernel(**inputs):
    inp = {k: np.asarray(v) for k, v in inputs.items()}
    nc = build_nc()
    in_maps = [host_prep(inp, c) for c in range(NCORES)]
    res = run_bass_kernel_spmd(nc, in_maps, core_ids=list(range(NCORES)))
    out = np.zeros((2, T, D), np.float32)
    for c in range(NCORES):
        b, j = c // 4, c % 4
        o = res.results[c]["out"]
        for i, bk in enumerate(own_blocks(j)):
            out[b, bk * 128:(bk + 1) * 128] = o[i * 128:(i + 1) * 128]
    return out
```
